# Optimizing a Trainium2 kernel written in Bass

```python
import math
import jax
import jax.numpy as jnp
from jax import lax
import numpy as np

D_MODEL = 1024
BATCH = 4
SEQ = 8192
DEPTH = 2

D_MIX = D_MODEL
GROUP_W = D_MIX // 4
HEAD_DIM = 64
GLA_HEADS = GROUP_W // HEAD_DIM
GLA_GATE_RANK = 16
GLA_GATE_NORMALIZER = 16.0
GLA_CHUNK = 64
FOX_HEADS = GROUP_W // HEAD_DIM
FOX_BLOCK = 128
SSM_HEADS = GROUP_W // HEAD_DIM
SSM_HEAD_DIM = HEAD_DIM
SSM_GROUPS = 2
SSM_STATE = 128
SSM_CONV = 4
SSM_CHUNK = 128
SSM_XBC = GROUP_W + 2 * SSM_GROUPS * SSM_STATE
SC_GROUPS = 4
SC_CONV = 3
D_FF = 2816
EPS = 1e-6

GLA_COLS = 4 * GROUP_W + GLA_GATE_RANK
FOX_COLS = 3 * GROUP_W + FOX_HEADS
SSM_COLS = GROUP_W + SSM_XBC + SSM_HEADS
SC_COLS = 3 * GROUP_W
IN_COLS = GLA_COLS + FOX_COLS + SSM_COLS + SC_COLS

kernel_name = 'hybrid_parallel_heads_gla_fox_ssd_shortconv'


def rms_norm(x, g):
    xf = x.astype(jnp.float32)
    y = xf * lax.rsqrt(jnp.mean(xf * xf, axis=-1, keepdims=True) + EPS)
    return (y * g.astype(jnp.float32)).astype(x.dtype)


def headwise_rms(x, g, n):
    shp = x.shape
    y = rms_norm(x.reshape(shp[:-1] + (shp[-1] // n, n)), g.reshape(-1, n))
    return y.reshape(shp)


def swiglu(h, w_gate, w_up, w_down):
    return (jax.nn.silu(h @ w_gate) * (h @ w_up)) @ w_down


def causal_depthwise_conv(u, w):
    K, C = w.shape
    return lax.conv_general_dilated(
        u, w[:, None, :].astype(u.dtype), window_strides=(1,),
        padding=[(K - 1, 0)], dimension_numbers=('NWC', 'WIO', 'NWC'),
        feature_group_count=C)


def gla_mixer(q, k, v, g_out, g_lr, w_gate_up, b_gate, norm_g):
    dtype = q.dtype
    f32 = jnp.float32
    Bsz, L, _ = q.shape
    H, D, C = GLA_HEADS, HEAD_DIM, GLA_CHUNK
    NC = L // C
    log_a = jax.nn.log_sigmoid((g_lr @ w_gate_up + b_gate).astype(f32)) / GLA_GATE_NORMALIZER

    def chunked(t):
        return t.astype(f32).reshape(Bsz, NC, C, H, D)

    qc = chunked(q) * (D ** -0.5)
    kc, vc, gc = chunked(k), chunked(v), chunked(log_a)
    b = jnp.cumsum(gc, axis=2)
    b_last = b[:, :, -1]
    q_dec = qc * jnp.exp(b)
    k_dec = kc * jnp.exp(-b)
    causal = jnp.tril(jnp.ones((C, C), bool))
    att = jnp.einsum('bnihd,bnjhd->bnhij', q_dec, k_dec)
    att = jnp.where(causal, att, 0.0)
    o_intra = jnp.einsum('bnhij,bnjhv->bnihv', att, vc)
    k_to_end = kc * jnp.exp(b_last[:, :, None] - b)
    chunk_kv = jnp.einsum('bnjhd,bnjhv->nbhdv', k_to_end, vc)
    chunk_decay = jnp.exp(b_last).transpose(1, 0, 2, 3)

    def step(S, inp):
        kv, dec = inp
        return S * dec[..., None] + kv, S

    S0 = jnp.zeros((Bsz, H, D, D), f32)
    _, S_prev = lax.scan(step, S0, (chunk_kv, chunk_decay))
    o_inter = jnp.einsum('bnihd,nbhdv->bnihv', q_dec, S_prev)
    o = (o_intra + o_inter).reshape(Bsz, L, H * D)
    o = headwise_rms(o, norm_g, D) * jax.nn.silu(g_out.astype(f32))
    return o.astype(dtype)


def fox_mixer(q, k, v, f_logit, b_forget, q_norm, k_norm, out_norm):
    dtype = q.dtype
    f32 = jnp.float32
    Bsz, L, _ = q.shape
    H, D, T = FOX_HEADS, HEAD_DIM, FOX_BLOCK
    NB = L // T
    qh = rms_norm(q.astype(f32).reshape(Bsz, L, H, D), q_norm) * (D ** -0.5)
    kh = rms_norm(k.astype(f32).reshape(Bsz, L, H, D), k_norm)
    vh = v.astype(f32).reshape(Bsz, L, H, D)
    log_f = jax.nn.log_sigmoid(f_logit.astype(f32) + b_forget.astype(f32))
    F = jnp.cumsum(log_f, axis=1)
    q_blocks = qh.reshape(Bsz, NB, T, H, D).transpose(1, 0, 2, 3, 4)
    F_blocks = F.reshape(Bsz, NB, T, H).transpose(1, 0, 2, 3)
    F_keys = F.transpose(0, 2, 1)[:, :, None, :]
    k_pos = jnp.arange(L)

    def block(args):
        qb, Fb, i = args
        s = jnp.einsum('bqhd,bkhd->bhqk', qb, kh)
        s = s + (Fb.transpose(0, 2, 1)[..., None] - F_keys)
        q_pos = i * T + jnp.arange(T)
        s = jnp.where(k_pos[None, :] <= q_pos[:, None], s, -jnp.inf)
        p = jax.nn.softmax(s, axis=-1)
        return jnp.einsum('bhqk,bkhd->bqhd', p, vh)

    o = lax.map(block, (q_blocks, F_blocks, jnp.arange(NB)))
    o = o.transpose(1, 0, 2, 3, 4).reshape(Bsz, L, H * D)
    return headwise_rms(o, out_norm, D).astype(dtype)


def ssd_mixer(z, xbc, dt_raw, conv_w, conv_b, dt_bias, A_log, D_skip, norm_g):
    dtype = z.dtype
    f32 = jnp.float32
    Bsz, L, _ = z.shape
    H, P, G, N, Q = SSM_HEADS, SSM_HEAD_DIM, SSM_GROUPS, SSM_STATE, SSM_CHUNK
    NC = L // Q
    xbc = jax.nn.silu(causal_depthwise_conv(xbc, conv_w) + conv_b).astype(f32)
    xs = xbc[..., :GROUP_W].reshape(Bsz, L, H, P)
    Bm = jnp.repeat(xbc[..., GROUP_W:GROUP_W + G * N].reshape(Bsz, L, G, N), H // G, axis=2)
    Cm = jnp.repeat(xbc[..., GROUP_W + G * N:].reshape(Bsz, L, G, N), H // G, axis=2)
    dt = jax.nn.softplus(dt_raw.astype(f32) + dt_bias.astype(f32))
    A = -jnp.exp(A_log.astype(f32))
    a = (dt * A).reshape(Bsz, NC, Q, H).transpose(0, 3, 1, 2)
    Xd = (xs * dt[..., None]).reshape(Bsz, NC, Q, H, P)
    Bc = Bm.reshape(Bsz, NC, Q, H, N)
    Cc = Cm.reshape(Bsz, NC, Q, H, N)
    a_cs = jnp.cumsum(a, axis=-1)
    causal = jnp.tril(jnp.ones((Q, Q), bool))
    seg = a_cs[..., :, None] - a_cs[..., None, :]
    Lmat = jnp.exp(jnp.where(causal, seg, -jnp.inf))
    scores = jnp.einsum('bcihn,bcjhn->bhcij', Cc, Bc) * Lmat
    y_diag = jnp.einsum('bhcij,bcjhp->bcihp', scores, Xd)
    decay_to_end = jnp.exp(a_cs[..., -1:] - a_cs)
    chunk_states = jnp.einsum('bcjhn,bhcj,bcjhp->cbhpn', Bc, decay_to_end, Xd)
    chunk_decay = jnp.exp(a_cs[..., -1]).transpose(2, 0, 1)

    def step(S, inp):
        st, dec = inp
        return S * dec[..., None, None] + st, S

    S0 = jnp.zeros((Bsz, H, P, N), f32)
    _, S_prev = lax.scan(step, S0, (chunk_states, chunk_decay))
    y_off = jnp.einsum('bcihn,cbhpn,bhci->bcihp', Cc, S_prev, jnp.exp(a_cs))
    y = (y_diag + y_off).reshape(Bsz, L, H, P) + xs * D_skip.astype(f32)[:, None]
    y = y.reshape(Bsz, L, H * P) * jax.nn.silu(z.astype(f32))
    return headwise_rms(y, norm_g, GROUP_W // G).astype(dtype)


def short_conv_mixer(b_gate, c_gate, val, conv_w, out_norm):
    y = b_gate * causal_depthwise_conv(c_gate * val, conv_w)
    return headwise_rms(y, out_norm, GROUP_W // SC_GROUPS)


def hybrid_mixing(h, w_in, gla_w_gate_up, gla_b_gate, gla_norm,
                  fox_b_forget, fox_q_norm, fox_k_norm, fox_out_norm,
                  ssm_conv_w, ssm_conv_b, ssm_dt_bias, ssm_A_log, ssm_D, ssm_norm,
                  sc_conv_w, sc_out_norm, w_out):
    proj = h @ w_in
    sizes = ([GROUP_W] * 4 + [GLA_GATE_RANK] + [GROUP_W] * 3 + [FOX_HEADS]
             + [GROUP_W, SSM_XBC, SSM_HEADS] + [GROUP_W] * 3)
    cuts = [int(c) for c in np.cumsum(sizes)[:-1]]
    (a_q, a_k, a_v, a_g, a_lr, b_q, b_k, b_v, b_f,
     c_z, c_xbc, c_dt, d_b, d_c, d_v) = jnp.split(proj, cuts, axis=-1)
    y_a = gla_mixer(a_q, a_k, a_v, a_g, a_lr, gla_w_gate_up, gla_b_gate, gla_norm)
    y_b = fox_mixer(b_q, b_k, b_v, b_f, fox_b_forget, fox_q_norm, fox_k_norm, fox_out_norm)
    y_c = ssd_mixer(c_z, c_xbc, c_dt, ssm_conv_w, ssm_conv_b, ssm_dt_bias, ssm_A_log, ssm_D, ssm_norm)
    y_d = short_conv_mixer(d_b, d_c, d_v, sc_conv_w, sc_out_norm)
    y = jnp.concatenate([y_a, y_b, y_c, y_d], axis=-1).astype(h.dtype)
    return y @ w_out


def setup_inputs(seed: int = 0) -> dict:
    key = jax.random.key(seed)
    k = jax.random.split(key, 27)
    f32 = jnp.float32
    Ld = DEPTH

    def nrm(kk, shape, scale):
        return jax.random.normal(kk, shape, f32) * scale

    def gain(kk, shape):
        return 1.0 + 0.02 * jax.random.normal(kk, shape, f32)

    dt0 = jnp.exp(jax.random.uniform(k[16], (Ld, SSM_HEADS), f32, math.log(1e-3), math.log(1e-1)))
    return {
        'x': nrm(k[0], (BATCH, SEQ, D_MODEL), 1.0),
        'ffn1_norm': gain(k[1], (Ld, D_MODEL)),
        'ffn1_w_gate': nrm(k[2], (Ld, D_MODEL, D_FF), D_MODEL ** -0.5),
        'ffn1_w_up': nrm(k[3], (Ld, D_MODEL, D_FF), D_MODEL ** -0.5),
        'ffn1_w_down': nrm(k[4], (Ld, D_FF, D_MODEL), D_FF ** -0.5),
        'mix_norm': gain(k[5], (Ld, D_MODEL)),
        'w_in': nrm(k[6], (Ld, D_MODEL, IN_COLS), D_MODEL ** -0.5),
        'gla_w_gate_up': nrm(k[7], (Ld, GLA_GATE_RANK, GROUP_W), GLA_GATE_RANK ** -0.5),
        'gla_b_gate': nrm(k[8], (Ld, GROUP_W), 0.1),
        'gla_norm': gain(k[9], (Ld, GROUP_W)),
        'fox_b_forget': 2.0 + nrm(k[10], (Ld, FOX_HEADS), 0.1),
        'fox_q_norm': gain(k[11], (Ld, HEAD_DIM)),
        'fox_k_norm': gain(k[12], (Ld, HEAD_DIM)),
        'fox_out_norm': gain(k[13], (Ld, GROUP_W)),
        'ssm_conv_w': nrm(k[14], (Ld, SSM_CONV, SSM_XBC), SSM_CONV ** -0.5),
        'ssm_conv_b': nrm(k[15], (Ld, SSM_XBC), 0.02),
        'ssm_dt_bias': dt0 + jnp.log(-jnp.expm1(-dt0)),
        'ssm_A_log': jnp.log(jax.random.uniform(k[17], (Ld, SSM_HEADS), f32, 1.0, 16.0)),
        'ssm_D': 1.0 + nrm(k[18], (Ld, SSM_HEADS), 0.1),
        'ssm_norm': gain(k[19], (Ld, GROUP_W)),
        'sc_conv_w': nrm(k[20], (Ld, SC_CONV, GROUP_W), SC_CONV ** -0.5),
        'sc_out_norm': gain(k[21], (Ld, GROUP_W)),
        'w_out': nrm(k[22], (Ld, D_MIX, D_MODEL), D_MIX ** -0.5),
        'ffn2_norm': gain(k[23], (Ld, D_MODEL)),
        'ffn2_w_gate': nrm(k[24], (Ld, D_MODEL, D_FF), D_MODEL ** -0.5),
        'ffn2_w_up': nrm(k[25], (Ld, D_MODEL, D_FF), D_MODEL ** -0.5),
        'ffn2_w_down': nrm(k[26], (Ld, D_FF, D_MODEL), D_FF ** -0.5),
    }


def reference(x, ffn1_norm, ffn1_w_gate, ffn1_w_up, ffn1_w_down,
              mix_norm, w_in, gla_w_gate_up, gla_b_gate, gla_norm,
              fox_b_forget, fox_q_norm, fox_k_norm, fox_out_norm,
              ssm_conv_w, ssm_conv_b, ssm_dt_bias, ssm_A_log, ssm_D, ssm_norm,
              sc_conv_w, sc_out_norm, w_out,
              ffn2_norm, ffn2_w_gate, ffn2_w_up, ffn2_w_down):
    for l in range(DEPTH):
        x = x + 0.5 * swiglu(rms_norm(x, ffn1_norm[l]), ffn1_w_gate[l], ffn1_w_up[l], ffn1_w_down[l])
        h = rms_norm(x, mix_norm[l])
        x = x + hybrid_mixing(h, w_in[l], gla_w_gate_up[l], gla_b_gate[l], gla_norm[l],
                              fox_b_forget[l], fox_q_norm[l], fox_k_norm[l], fox_out_norm[l],
                              ssm_conv_w[l], ssm_conv_b[l], ssm_dt_bias[l], ssm_A_log[l],
                              ssm_D[l], ssm_norm[l], sc_conv_w[l], sc_out_norm[l], w_out[l])
        x = x + 0.5 * swiglu(rms_norm(x, ffn2_norm[l]), ffn2_w_gate[l], ffn2_w_up[l], ffn2_w_down[l])
    return x
```

```python
import numpy as np
from contextlib import ExitStack
import concourse.bass as bass
import concourse.mybir as mybir
from concourse.bass_utils import run_bass_kernel_spmd

F32 = mybir.dt.float32
BF16 = mybir.dt.bfloat16
ALU = mybir.AluOpType
AF = mybir.ActivationFunctionType
AX = mybir.AxisListType

SEM_CAP = 30000


class Buf:
    __slots__ = ("ap", "name", "lw", "rd")

    def __init__(self, ap, name):
        self.ap = ap
        self.name = name
        self.lw = None
        self.rd = {}

    def __getitem__(self, k):
        return V(self, self.ap[k])


class V:
    __slots__ = ("b", "ap")

    def __init__(self, b, ap):
        self.b = b
        self.ap = ap


def _bufs(*vs):
    return [v.b for v in vs if isinstance(v, V)]


def _ap(v):
    return v.ap if isinstance(v, V) else v


class Sched:
    ENGS = ("pe", "act", "dve", "pool", "sp")

    def __init__(self, nc, stack):
        self.nc = nc
        self.stack = stack
        self.streams = {e: [] for e in self.ENGS}
        self.dom = {e: [] for e in self.ENGS}
        self.waited = {e: {} for e in self.ENGS}
        self.nbuf = 0

    def sb(self, shape, dtype, name=None):
        self.nbuf += 1
        name = "%s_%d" % (name or "sb", self.nbuf)
        t = self.stack.enter_context(self.nc.sbuf_tensor(name, list(shape), dtype))
        return Buf(t, name)

    def ps(self, shape, dtype=F32, name=None):
        self.nbuf += 1
        name = "%s_%d" % (name or "ps", self.nbuf)
        t = self.stack.enter_context(self.nc.psum_tensor(name, list(shape), dtype))
        return Buf(t, name)

    def dram(self, ap, name):
        return Buf(ap, name)

    def _deps(self, eng, reads, writes):
        deps = {}

        def add(tok):
            if tok is None:
                return
            d, s = tok
            if d == "pe" and eng == "pe":
                return
            if deps.get(d, -1) < s:
                deps[d] = s

        for b in reads:
            add(b.lw)
        for b in writes:
            add(b.lw)
            for d, s in b.rd.items():
                add((d, s))
        w = self.waited[eng]
        out = []
        for d, s in deps.items():
            if w.get(d, -1) >= s:
                continue
            w[d] = s
            out.append((d, s))
        return out

    def op(self, eng, fn, reads=(), writes=(), dma=None):
        deps = self._deps(eng, reads, writes)
        dom = eng if dma is None else "dma:" + dma
        if dom not in self.dom:
            self.dom[dom] = []
        rec = {"fn": fn, "deps": deps, "sig": dma is not None, "dom": dom,
               "seq": len(self.dom[dom])}
        self.dom[dom].append(rec)
        self.streams[eng].append(rec)
        tok = (dom, rec["seq"])
        for b in reads:
            if b.rd.get(dom, -1) < rec["seq"]:
                b.rd[dom] = rec["seq"]
        for b in writes:
            b.lw = tok
            b.rd = {}
        return tok

    def barrier(self):
        last = {d: len(r) - 1 for d, r in self.dom.items() if r}
        for e in self.ENGS:
            deps = []
            for d, s in last.items():
                if self.waited[e].get(d, -1) >= s:
                    continue
                self.waited[e][d] = s
                deps.append((d, s))
            rec = {"fn": (lambda eng: eng.nop()), "deps": deps, "sig": False, "dom": e,
                   "seq": len(self.dom[e])}
            self.dom[e].append(rec)
            self.streams[e].append(rec)

    def emit(self):
        nc = self.nc
        for e in self.ENGS:
            for rec in self.streams[e]:
                for d, s in rec["deps"]:
                    self.dom[d][s]["sig"] = True
        semtab = {}
        for d, recs in self.dom.items():
            step = 16 if (d.startswith("dma:") and not d.startswith("dma:cc")) else 1
            cap = SEM_CAP // step
            c = 0
            for rec in recs:
                if rec["sig"]:
                    rec["sv"] = (c // cap, (c % cap + 1) * step)
                    c += 1
            nep = (c + cap - 1) // cap
            semtab[d] = [self.stack.enter_context(nc.semaphore("s_%s_%d" % (d.replace(":", "_"), i)))
                         for i in range(max(nep, 0))]
        self.nsem = sum(len(v) for v in semtab.values())
        block = self.stack.enter_context(nc.Block())

        def run(engname):
            def body(eng):
                for rec in self.streams[engname]:
                    for d, s in rec["deps"]:
                        ep, val = self.dom[d][s]["sv"]
                        eng.wait_ge(semtab[d][ep], val)
                    ins = rec["fn"](eng)
                    if rec["sig"]:
                        ep, val = rec["sv"]
                        step = 16 if (rec["dom"].startswith("dma:") and not rec["dom"].startswith("dma:cc")) else 1
                        ins.then_inc(semtab[rec["dom"]][ep], step)
            return body

        block.tensor(run("pe"))
        block.scalar(run("act"))
        block.vector(run("dve"))
        block.gpsimd(run("pool"))
        block.sync(run("sp"))


class Ops:
    def __init__(self, S):
        self.S = S

    def dma(self, out, in_, chan, eng="sp"):
        return self.S.op(eng, lambda e: e.dma_start(out=out.ap, in_=in_.ap), reads=[in_.b], writes=[out.b], dma=chan)

    def act(self, out, in_, func, bias=0.0, scale=1.0, eng="act"):
        return self.S.op(eng, lambda e: e.activation(out=out.ap, in_=in_.ap, func=func, bias=_ap(bias), scale=_ap(scale)),
                         reads=_bufs(in_, bias, scale), writes=[out.b])

    def tt(self, out, in0, in1, op, eng="dve"):
        return self.S.op(eng, lambda e: e.tensor_tensor(out=out.ap, in0=in0.ap, in1=in1.ap, op=op),
                         reads=_bufs(in0, in1), writes=[out.b])

    def ts(self, out, in0, s1, s2, op0, op1=None, eng="dve"):
        if op1 is None:
            return self.S.op(eng, lambda e: e.tensor_scalar(out=out.ap, in0=in0.ap, scalar1=_ap(s1), scalar2=None, op0=op0),
                             reads=_bufs(in0, s1), writes=[out.b])
        return self.S.op(eng, lambda e: e.tensor_scalar(out=out.ap, in0=in0.ap, scalar1=_ap(s1), scalar2=_ap(s2), op0=op0, op1=op1),
                         reads=_bufs(in0, s1, s2), writes=[out.b])

    def stt(self, out, in0, scalar, in1, op0, op1, eng="dve"):
        eng = "dve"
        return self.S.op(eng, lambda e: e.scalar_tensor_tensor(out=out.ap, in0=in0.ap, scalar=_ap(scalar), in1=in1.ap, op0=op0, op1=op1),
                         reads=_bufs(in0, scalar, in1), writes=[out.b])

    def copy(self, out, in_, eng="dve"):
        if eng == "act":
            return self.act(out, in_, AF.Copy)
        return self.S.op(eng, lambda e: e.tensor_copy(out=out.ap, in_=in_.ap), reads=[in_.b], writes=[out.b])

    def recip(self, out, in_):
        return self.S.op("dve", lambda e: e.reciprocal(out=out.ap, in_=in_.ap), reads=[in_.b], writes=[out.b])

    def memset(self, out, val, eng="pool"):
        return self.S.op(eng, lambda e: e.memset(out.ap, val), writes=[out.b])

    def aselect(self, out, in_, pattern, cmp, fill, base, cm):
        return self.S.op("pool", lambda e: e.affine_select(out=out.ap, in_=in_.ap, pattern=pattern, compare_op=cmp,
                                                           fill=fill, base=base, channel_multiplier=cm),
                         reads=[in_.b], writes=[out.b])

    def mm(self, out, pairs, extra_reads=()):
        n = len(pairs)

        def fn(e):
            ins = None
            for i, (l, r) in enumerate(pairs):
                ins = e.matmul(out.ap, lhsT=l.ap, rhs=r.ap, start=(i == 0), stop=(i == n - 1))
            return ins
        rd = []
        for l, r in pairs:
            rd.append(l.b)
            rd.append(r.b)
        rd.extend(extra_reads)
        return self.S.op("pe", fn, reads=rd, writes=[out.b])

    def mm1(self, out, l, r, start, stop):
        return self.S.op("pe", lambda e: e.matmul(out.ap, lhsT=l.ap, rhs=r.ap, start=start, stop=stop),
                         reads=[l.b, r.b], writes=[out.b])

    def transpose(self, out, in_, ident):
        return self.S.op("pe", lambda e: e.transpose(out.ap, in_.ap, ident.ap), reads=[in_.b, ident.b], writes=[out.b])


D_MODEL = 1024
D_FF = 2816
NKC = D_MODEL // 128
NFC = D_FF // 128
IN_COLS = 3608
TT = 512
EPS = 1e-6
FOX_SHIFT = 10.0
NEG = -30000.0
SEQ = 8192
HALF = SEQ // 2
CH = 1024
GROUPS = [[0, 1], [2, 3], [4, 5], [6, 7]]

C_AQ, C_AK, C_AV, C_AG, C_ALR = 0, 256, 512, 768, 1024
C_BQ, C_BK, C_BV, C_BF = 1040, 1296, 1552, 1808
C_CZ, C_CX, C_CDT = 1812, 2068, 2836
C_DB, C_DC, C_DV = 2840, 3096, 3352
P1_GQ, P1_GK, P1_GV, P1_GG, P1_GLR, P1_DB, P1_DC, P1_DV, P1_N = 0, 128, 256, 384, 512, 528, 656, 784, 912
P2_BQ, P2_BK, P2_BV, P2_CZ, P2_CX, P2_CB, P2_CC, P2_CDT, P2_N = 0, 128, 256, 386, 514, 642, 770, 898, 900

K_FFN1, K_MIX, K_FFN2 = 0, 8, 16
K_GLAN = 24
K_FQN, K_FKN = 25, 26
K_FON = 27
K_CB = 29
K_CW = 32
K_D = 44
K_SN = 45
K_SCW = 46
K_SCN = 49
K_SEL = 50
NCOLT = 52
M_ALOG, M_FB, M_DTB, M_BG = 0, 2, 4, 6
M_ALOG4, M_DTB4, M_BG4 = 134, 142, 150
NTM = 150 + 512


def host_tables(inp, l, r):
    f32 = np.float32
    cols = np.zeros((128, NCOLT), f32)
    hs = slice(r * 128, (r + 1) * 128)

    def put(k, vec):
        v = np.asarray(vec, f32).reshape(-1, 128)
        for c in range(v.shape[0]):
            cols[:, k + c] = v[c]
    put(K_FFN1, inp["ffn1_norm"][l]); put(K_MIX, inp["mix_norm"][l]); put(K_FFN2, inp["ffn2_norm"][l])
    put(K_GLAN, inp["gla_norm"][l][hs])
    put(K_FQN, np.tile(inp["fox_q_norm"][l], 2)); put(K_FKN, np.tile(inp["fox_k_norm"][l], 2))
    fo = np.asarray(inp["fox_out_norm"][l], f32).reshape(4, 64)
    for h in range(2):
        cols[0:64, K_FON + h] = fo[2 * r + h]
    cb = np.asarray(inp["ssm_conv_b"][l], f32)
    cw = np.asarray(inp["ssm_conv_w"][l], f32)
    chs = [slice(r * 128, (r + 1) * 128), slice(256 + r * 128, 256 + (r + 1) * 128), slice(512 + r * 128, 512 + (r + 1) * 128)]
    for c in range(3):
        cols[:, K_CB + c] = cb[chs[c]]
        for k in range(4):
            cols[:, K_CW + 3 * k + c] = cw[k][chs[c]]
    put(K_D, np.repeat(np.asarray(inp["ssm_D"][l], f32)[2 * r:2 * r + 2], 64))
    put(K_SN, inp["ssm_norm"][l][hs])
    for k in range(3):
        put(K_SCW + k, inp["sc_conv_w"][l][k][hs])
    put(K_SCN, inp["sc_out_norm"][l][hs])
    cols[:, K_SEL + r] = 1.0
    tm = np.zeros((128, NTM), f32)
    tm[:, M_ALOG:M_ALOG + 2] = np.asarray(inp["ssm_A_log"][l], f32)[None, 2 * r:2 * r + 2]
    tm[:, M_FB:M_FB + 2] = np.asarray(inp["fox_b_forget"][l], f32)[None, 2 * r:2 * r + 2]
    tm[:, M_DTB:M_DTB + 2] = np.asarray(inp["ssm_dt_bias"][l], f32)[None, 2 * r:2 * r + 2]
    tm[:, M_BG:M_BG + 128] = np.asarray(inp["gla_b_gate"][l], f32)[None, hs]
    tm[:, M_ALOG4:M_ALOG4 + 8] = np.tile(tm[:, M_ALOG:M_ALOG + 2], (1, 4))
    tm[:, M_DTB4:M_DTB4 + 8] = np.tile(tm[:, M_DTB:M_DTB + 2], (1, 4))
    tm[:, M_BG4:M_BG4 + 512] = np.tile(tm[:, M_BG:M_BG + 128], (1, 4))
    return cols, tm


def host_win(inp, l, r):
    w = np.asarray(inp["w_in"][l], np.float32)
    h = lambda c0: w[:, c0 + r * 128: c0 + (r + 1) * 128]
    p1 = np.concatenate([h(C_AQ), h(C_AK), h(C_AV), h(C_AG), w[:, C_ALR:C_ALR + 16], h(C_DB), h(C_DC), h(C_DV)], axis=1)
    p2 = np.concatenate([h(C_BQ), h(C_BK), h(C_BV), w[:, C_BF + 2 * r:C_BF + 2 * r + 2], h(C_CZ),
                         h(C_CX), h(C_CX + 256), h(C_CX + 512), w[:, C_CDT + 2 * r:C_CDT + 2 * r + 2]], axis=1)
    assert p1.shape[1] == P1_N and p2.shape[1] == P2_N
    return np.ascontiguousarray(p1), np.ascontiguousarray(p2)


def host_wout(inp, l):
    w = np.asarray(inp["w_out"][l], np.float32)
    rows = []
    for rr in range(2):
        for blk in range(4):
            rows.append(w[blk * 256 + rr * 128: blk * 256 + (rr + 1) * 128])
    return np.ascontiguousarray(np.concatenate(rows, axis=0))


import os
DBG = set(os.environ.get("MIXDBG", "sc,gla,ssd,fox,fox2,fox3,b").split(","))
ARENA_UNITS = 212000 // 2


class Builder:
    def __init__(self, nc, stack, S_tok):
        self.nc = nc
        self.S_tok = S_tok
        self.S = Sched(nc, stack)
        self.o = Ops(self.S)
        S = self.S
        self.arena = stack.enter_context(nc.sbuf_tensor("arena", [128, ARENA_UNITS], BF16))
        self.poff = 0
        self.nb = 0
        self.psum = [S.ps([128, TT], F32, "bank%d" % i) for i in range(8)]
        self.pi = 0
        self.ring = list(range(8))
        self.consts()
        self.pbase = self.poff

    def alloc(self, shape, dtype, name):
        n = 1
        for d in shape[1:]:
            n *= d
        units = n * (2 if dtype == F32 else 1)
        units = (units + 15) // 16 * 16
        assert self.poff + units <= ARENA_UNITS, ("SBUF arena overflow", name, self.poff, units)
        ap = self.arena[:, self.poff:self.poff + units]
        self.poff += units
        if dtype == F32:
            ap = ap.bitcast(F32)
            ap = ap[:, 0:n]
        else:
            ap = ap[:, 0:n]
        if len(shape) == 3:
            ap = ap.rearrange("p (a b) -> p a b", a=shape[1])
        elif len(shape) == 4:
            ap = ap.rearrange("p (a b c) -> p a b c", a=shape[1], b=shape[2])
        if shape[0] < 128:
            ap = ap[0:shape[0]]
        self.nb += 1
        return Buf(ap, "%s_%d" % (name, self.nb))

    def phase_begin(self):
        self.S.barrier()
        self.poff = self.pbase

    def bank(self):
        b = self.psum[self.ring[self.pi % len(self.ring)]]
        self.pi += 1
        return b

    def consts(self):
        o = self.o
        A = self.alloc
        self.ones_mean = {}
        for blk in (1024, 128, 64):
            t = A([128, 128], BF16, "ones%d" % blk)
            o.memset(t[:], 1.0 / blk)
            if blk == 64:
                o.memset(t[0:64, 64:128], 0.0)
                o.memset(t[64:128, 0:64], 0.0)
            self.ones_mean[blk] = t
        self.eps_col = A([128, 1], F32, "epscol")
        o.memset(self.eps_col[:], EPS)
        self.ident_b = A([128, 128], BF16, "identb")
        o.memset(self.ident_b[:], 1.0)
        o.aselect(self.ident_b[:], self.ident_b[:], [[-1, 128]], ALU.is_equal, 0.0, 0, 1)
        self.ident_f = A([128, 128], F32, "identf")
        o.memset(self.ident_f[:], 1.0)
        o.aselect(self.ident_f[:], self.ident_f[:], [[-1, 128]], ALU.is_equal, 0.0, 0, 1)
        self.triU = A([128, 128], F32, "triU")
        o.memset(self.triU[:], 1.0)
        o.aselect(self.triU[:], self.triU[:], [[1, 128]], ALU.is_ge, 0.0, 0, -1)
        self.triU_b = A([128, 128], BF16, "triUb")
        o.copy(self.triU_b[:], self.triU[:], eng="pool")
        self.triG = A([128, 128], F32, "triG")
        o.memset(self.triG[:], -1.0 / 16.0)
        o.aselect(self.triG[:], self.triG[:], [[1, 128]], ALU.is_ge, 0.0, 0, -1)
        self.triS = A([128, 128], F32, "triS")
        o.memset(self.triS[:], 1.0)
        o.aselect(self.triS[:], self.triS[:], [[-1, 128]], ALU.is_gt, 0.0, 0, 1)
        self.maskb = A([128, 128], F32, "maskb")
        o.memset(self.maskb[:], 0.0)
        o.aselect(self.maskb[:], self.maskb[:], [[1, 128]], ALU.is_ge, NEG, 0, -1)
        self.ones_f = A([128, 128], F32, "onesf")
        o.memset(self.ones_f[:], 1.0)
        self.triU4 = A([128, 4, 128], F32, "triU4")
        self.triS4 = A([128, 4, 128], F32, "triS4")
        self.ones4 = A([128, 4, 128], F32, "ones4")
        for s4 in range(4):
            o.copy(self.triU4[:, s4, :], self.triU[:], eng="pool")
            o.copy(self.triS4[:, s4, :], self.triS[:], eng="pool")
        o.memset(self.ones4[:], 1.0)
        self.sel = A([96, 4, 128], BF16, "sel")
        o.memset(self.sel[:], 0.0)
        for h in range(4):
            v = self.sel[0:96, h, :]
            o.ts(v, self.ones_f[0:96, :], self.ident_f[0:96, h:h + 1], None, ALU.mult, eng="pool")
            o.stt(v, self.ones_f[0:96, :], self.ident_f[0:96, 32 + h:33 + h], v, ALU.mult, ALU.add, eng="pool")
            o.stt(v, self.ones_f[0:96, :], self.ident_f[0:96, 64 + h:65 + h], v, ALU.mult, ALU.add, eng="pool")
        self.wden = A([65, 64], BF16, "wden")
        o.memset(self.wden[0:65, :], EPS)
        o.memset(self.wden[0:64, :], 1.0 / 64)

    def load_weight(self, dst, w_ap, nk, ncols, chan, rows0=0):
        wd = self.S.dram(w_ap, "wdram")
        for k in range(nk):
            src = w_ap[rows0 + k * 128: rows0 + (k + 1) * 128, :]
            self.o.dma(dst[:, k, :], V(wd, src), chan, eng="pool")

    def load_tables(self, cols_ap, tm_ap):
        self.cols = self.alloc([128, NCOLT], F32, "cols")
        self.tm = self.alloc([128, NTM], F32, "tm")
        self.o.dma(self.cols[:], V(self.S.dram(cols_ap, "colsd"), cols_ap), "cols")
        self.o.dma(self.tm[:], V(self.S.dram(tm_ap, "tmd"), tm_ap), "tm")

    def alloc_xnorm(self):
        self.xt = self.alloc([128, NKC, TT], F32, "xt")
        self.ht = self.alloc([128, NKC, TT], BF16, "ht")
        self.sq = self.alloc([128, 2, TT], BF16, "sq")
        self.rstd = self.alloc([128, TT], F32, "rstd")

    def rmsnorm_tile(self, kcol):
        o = self.o
        ps = self.bank()
        for c in range(NKC):
            o.act(self.sq[:, c % 2, :], self.xt[:, c, :], AF.Square)
            o.mm1(ps[:], self.ones_mean[1024][:], self.sq[:, c % 2, :], c == 0, c == NKC - 1)
        o.act(self.rstd[:], ps[:], AF.Sqrt, bias=self.eps_col[:])
        o.recip(self.rstd[:], self.rstd[:])
        for c in range(NKC):
            o.stt(self.ht[:, c, :], self.xt[:, c, :], self.cols[:, kcol + c:kcol + c + 1], self.rstd[:], ALU.mult, ALU.mult,
                  eng="dve" if c % 2 == 0 else "pool")

    def group_rms(self, y, blk, gain, out, tmp, post_mul=None, post_scale=None):
        o = self.o
        sqb = self.gsq
        o.act(sqb[:], y, AF.Square)
        ps = self.bank()
        o.mm(ps[:], [(self.ones_mean[blk][:], sqb[:])])
        o.act(self.grs[:], ps[:], AF.Sqrt, bias=self.eps_col[:])
        o.recip(self.grs[:], self.grs[:])
        if post_mul is None and post_scale is None:
            o.stt(out, y, gain, self.grs[:], ALU.mult, ALU.mult)
        else:
            o.stt(tmp, y, gain, self.grs[:], ALU.mult, ALU.mult)
            if post_mul is not None:
                o.tt(out, tmp, post_mul, ALU.mult)
            else:
                o.ts(out, tmp, post_scale, None, ALU.mult)

    def ffn_phase(self, x_in, x_out, cols_ap, tm_ap, kcol, wg_ap, wu_ap, wd_ap, h_out=None, h_gather=None):
        o = self.o
        self.phase_begin()
        self.ring = list(range(8))
        wg = self.alloc([128, NKC, D_FF], BF16, "wg")
        wu = self.alloc([128, NKC, D_FF], BF16, "wu")
        wdn = self.alloc([128, NFC, D_MODEL], BF16, "wd")
        self.load_tables(cols_ap, tm_ap)
        self.load_weight(wg, wg_ap, NKC, D_FF, "wA")
        self.load_weight(wu, wu_ap, NKC, D_FF, "wB")
        self.load_weight(wdn, wd_ap, NFC, D_MODEL, "wC")
        self.alloc_xnorm()
        act_t = self.alloc([128, NFC, TT], BF16, "actT")
        sg = [self.alloc([128, TT], F32, "sg%d" % i) for i in range(2)]
        xin_r = x_in.ap.rearrange("(c p) s -> p c s", p=128)
        xout_r = x_out.ap.rearrange("(c p) s -> p c s", p=128)
        for t in range(self.S_tok // TT):
            ts = slice(t * TT, (t + 1) * TT)
            o.dma(self.xt[:], V(x_in, xin_r[:, :, ts]), "xt")
            self.rmsnorm_tile(kcol)
            for j in range(NFC):
                pg = self.bank()
                pu = self.bank()
                o.mm(pg[:], [(wg[:, k, j * 128:(j + 1) * 128], self.ht[:, k, :]) for k in range(NKC)])
                o.mm(pu[:], [(wu[:, k, j * 128:(j + 1) * 128], self.ht[:, k, :]) for k in range(NKC)])
                o.act(sg[j % 2][:], pg[:], AF.Silu)
                o.tt(act_t[:, j, :], sg[j % 2][:], pu[:], ALU.mult)
            for c in range(NKC):
                py = self.bank()
                o.mm(py[:], [(wdn[:, j, c * 128:(c + 1) * 128], act_t[:, j, :]) for j in range(NFC)])
                o.stt(self.xt[:, c, :], py[:], 0.5, self.xt[:, c, :], ALU.mult, ALU.add)
            o.dma(V(x_out, xout_r[:, :, ts]), self.xt[:], "xo")
            if h_out is not None:
                self.rmsnorm_tile(K_MIX)
                hb, hv = h_out[(t * TT) // CH]
                c0 = (t * TT) % CH
                o.dma(V(hb, hv.rearrange("(c p) s -> p c s", p=128)[:, :, c0:c0 + TT]), self.ht[:], "ho")
                if c0 + TT == CH and h_gather is not None:
                    k_ = (t * TT) // CH
                    self.all_gather(h_out[k_][0], h_gather[k_][0])

    def all_gather(self, src, dst):
        self.S.op("pool", lambda e: e.collective_compute("AllGather", ALU.bypass, replica_groups=GROUPS,
                                                         ins=[src.ap.opt()], outs=[dst.ap.opt()]),
                  reads=[src], writes=[dst], dma="ccag")

    def mix_a(self, part, hfull, cols_ap, tm_ap, w_ap, wup_ap, scr):
        o = self.o
        A = self.alloc
        NT = SEQ // TT
        self.phase_begin()
        self.ring = [1, 2, 3, 4, 5, 6, 7]
        acc = self.psum[0]
        VA_, EB_ = (A([128, SEQ // 128, 2, 66], BF16, "VA"), A([128, SEQ // 128, 2], F32, "EB"))
        if part == 2:
            self.VA, self.EB = VA_, EB_
        self.mixb_base = self.poff
        NW = P1_N if part == 1 else P2_N
        win = A([128, NKC, NW], BF16, "win")
        self.load_weight(win, w_ap, NKC, NW, "wA")
        self.load_tables(cols_ap, tm_ap)
        cols, tm = self.cols, self.tm
        self.ht = A([128, NKC, TT], BF16, "ht")
        self.gsq = A([128, TT], BF16, "gsq")
        self.grs = A([128, TT], F32, "grs")
        T = [A([128, TT], F32, "T%d" % i) for i in range(6)]
        Yt = A([128, 4, TT], BF16, "Yt")
        sm = [A([128, 2], F32, "sm%d" % i) for i in range(6)]
        Q = [A([128, 128], F32, "Q%d" % i) for i in range(4)]
        Qb = [A([128, 128], BF16, "Qb%d" % i) for i in range(4)]
        if part == 1:
            scu = A([128, TT + 2], F32, "scu")
            o.memset(scu[:, 0:2], 0.0)
            wup = A([16, 128], F32, "wup")
            o.dma(wup[:], V(self.S.dram(wup_ap, "wupd"), wup_ap), "wup")
            GS32 = A([128, 64], F32, "GS32")
            GSb = A([128, 64], BF16, "GSb")
            o.memset(GS32[:], 0.0)
            o.memset(GSb[:], 0.0)
            la = A([128, 4, 128], F32, "la")
            vtm = A([128, 4, 128], BF16, "vtm")
            qdb = A([128, TT], BF16, "qdb")
            kdb = A([128, TT], BF16, "kdb")
            kdt = A([128, 4, 128], BF16, "kdt")
            attm = [A([128, 4, 128], BF16, "attm%d" % i) for i in range(2)]
            glr = A([32, TT], F32, "glr")
        else:
            o.memset(self.VA[:], 1.0)
            aneg4 = A([128, 8], F32, "aneg4")
            o.act(aneg4[:], tm[:, M_ALOG4:M_ALOG4 + 8], AF.Exp)
            acs4 = A([128, 8], F32, "acs4")
            dte4 = A([128, 8], F32, "dte4")
            eal4 = A([128, 8], F32, "eal4")
            Xd4 = A([128, 4, 128], BF16, "Xd4")
            Xdd4 = A([128, 4, 128], BF16, "Xdd4")
            Btm4 = A([128, 4, 128], BF16, "Btm4")
            lseg4 = [A([128, 4, 128], F32, "lseg4_%d" % h) for h in range(2)]
            abc4 = [A([128, 4, 128], F32, "abc4_%d" % h) for h in range(2)]
            LT4 = [A([128, TT], F32, "LT4_%d" % h) for h in range(2)]
            Ecs4 = [A([128, TT], F32, "Ecs4_%d" % h) for h in range(2)]
            WT4 = [A([128, TT], BF16, "WT4_%d" % h) for h in range(2)]
            CsT4 = [A([128, TT], BF16, "CsT4_%d" % h) for h in range(2)]
            raw = [A([128, TT + 3], F32, "raw%d" % c) for c in range(3)]
            for c in range(3):
                o.memset(raw[c][:, 0:3], 0.0)
            xs32 = A([128, TT], F32, "xs32")
            xsb = A([128, TT], BF16, "xsb")
            BT = A([128, TT], BF16, "BT")
            CT = A([128, TT], BF16, "CT")
            zs = A([128, TT], F32, "zs")
            dt_tm = A([128, 4, 2], F32, "dt_tm")
            a_tm = A([128, 4, 2], F32, "a_tm")
            Xd = A([128, 128], BF16, "Xd")
            Xdd = A([128, 128], BF16, "Xdd")
            Btm = A([128, 128], BF16, "Btm")
            ST32 = [A([128, 64], F32, "ST32_%d" % h) for h in range(2)]
            STb = [A([128, 64], BF16, "STb%d" % h) for h in range(2)]
            for h in range(2):
                o.memset(ST32[h][:], 0.0)
                o.memset(STb[h][:], 0.0)
            gcar_bc = A([128, 2], F32, "gcar_bc")
            gcar_T = A([96, 1], F32, "gcar_T")
            o.memset(gcar_bc[:], 0.0)
            o.memset(gcar_T[:], 0.0)
            nl = A([128, 4, 2], F32, "nl")
            nl3 = A([128, 4, 96], F32, "nl3")
            o.memset(nl3[:], 0.0)
            Fp = A([96, TT], BF16, "Fp")
            F1 = [A([96, TT], F32, "F1_%d" % i) for i in range(3)]
            F1b = [A([96, TT], BF16, "F1b_%d" % i) for i in range(3)]
            qb = A([128, TT], BF16, "qb")
            kb = A([128, TT], BF16, "kb")
            o.memset(Fp[:], 0.0)

        hrs = [(b_, v_.rearrange("(r c p) s -> p r c s", r=2, p=128)) for b_, v_ in hfull]
        Yrs = [(b_, v_.rearrange("(hf c p) s -> p hf c s", hf=2, p=128)) for b_, v_ in scr["Y"]]

        def proj_fm(col0, n=128):
            ps = self.bank()
            o.mm(ps[0:n, :], [(win[:, k, col0:col0 + n], self.ht[:, k, :]) for k in range(NKC)])
            return ps

        def proj_tm(sub, col0, n):
            ps = self.bank()
            o.mm(ps[:, 0:n], [(self.ht[:, k, sub * 128:(sub + 1) * 128], win[:, k, col0:col0 + n]) for k in range(NKC)])
            return ps

        for t in range(NT):
            ts_ = slice(t * TT, (t + 1) * TT)
            hf, tl = t // (NT // 2), t % (NT // 2)
            ck = (tl * TT) // CH
            tsl = slice((tl * TT) % CH, (tl * TT) % CH + TT)
            hfb, hr = hrs[ck]
            Yb_, Yr = Yrs[ck]
            o.dma(self.ht[:], V(hfb, hr[:, hf, :, tsl]), "ht")
            if part == 1:
                pc = proj_fm(P1_DC)
                o.copy(T[0][:], pc[:], eng="act")
                pv = proj_fm(P1_DV)
                o.tt(scu[:, 2:TT + 2], T[0][:], pv[:], ALU.mult)
                o.ts(T[1][:], scu[:, 0:TT], cols[:, K_SCW:K_SCW + 1], None, ALU.mult, eng="pool")
                o.stt(T[1][:], scu[:, 1:TT + 1], cols[:, K_SCW + 1:K_SCW + 2], T[1][:], ALU.mult, ALU.add)
                o.stt(T[1][:], scu[:, 2:TT + 2], cols[:, K_SCW + 2:K_SCW + 3], T[1][:], ALU.mult, ALU.add)
                o.copy(scu[:, 0:2], scu[:, TT:TT + 2], eng="pool")
                pb = proj_fm(P1_DB)
                o.tt(T[2][:], T[1][:], pb[:], ALU.mult)
                self.group_rms(T[2][:], 64, cols[:, K_SCN:K_SCN + 1], Yt[:, 3, :], None)
                pl = proj_fm(P1_GLR, 16)
                o.copy(glr[0:16, :], pl[0:16, :], eng="act")
                PZ = self.bank()
                for s_ in range(4):
                    o.mm(PZ[:, s_ * 128:(s_ + 1) * 128], [(glr[0:16, s_ * 128:(s_ + 1) * 128], wup[:])])
                laf = la.ap.rearrange("p a b -> p (a b)")
                laf = V(la, laf)
                o.tt(laf, PZ[:], tm[:, M_BG4:M_BG4 + 512], ALU.add)
                o.act(laf, laf, AF.Exp, scale=-1.0)
                o.act(laf, laf, AF.Ln, bias=1.0)
                PV_ = self.bank()
                for s_ in range(4):
                    o.mm(PV_[:, s_ * 128:(s_ + 1) * 128],
                         [(self.ht[:, k, s_ * 128:(s_ + 1) * 128], win[:, k, P1_GV:P1_GV + 128]) for k in range(NKC)])
                o.copy(V(vtm, vtm.ap.rearrange("p a b -> p (a b)")), PV_[:], eng="act")
                pq = proj_fm(P1_GQ)
                o.copy(T[0][:], pq[:], eng="act")
                pk = proj_fm(P1_GK)
                o.copy(T[1][:], pk[:], eng="act")
                pgo = proj_fm(P1_GG)
                o.act(T[2][:], pgo[:], AF.Silu)
                PB = self.bank()
                for s_ in range(4):
                    o.mm(PB[:, s_ * 128:(s_ + 1) * 128], [(la[:, s_, :], self.triG[:])])
                eb, enb = T[5], T[4]
                o.act(eb[:], PB[:], AF.Exp)
                o.act(enb[:], PB[:], AF.Exp, scale=-1.0)
                o.stt(qdb[:], T[0][:], 0.125, eb[:], ALU.mult, ALU.mult)
                o.tt(kdb[:], T[1][:], enb[:], ALU.mult)
                PTr = self.bank()
                PTb = V(PTr, PTr.ap[:, 0:256].bitcast(BF16))
                for s_ in range(4):
                    o.transpose(V(PTr, PTb.ap[:, s_ * 128:(s_ + 1) * 128]), kdb[:, s_ * 128:(s_ + 1) * 128], self.ident_b[:])
                o.copy(V(kdt, kdt.ap.rearrange("p a b -> p (a b)")), PTb)
                for hh in range(2):
                    R = slice(hh * 64, (hh + 1) * 64)
                    PA = self.bank()
                    for s_ in range(4):
                        cs = slice(s_ * 128, (s_ + 1) * 128)
                        o.mm(PA[:, cs], [(kdb[R, cs], qdb[R, cs])])
                    o.tt(V(attm[hh], attm[hh].ap.rearrange("p a b -> p (a b)")), PA[:],
                         V(self.triU4, self.triU4.ap.rearrange("p a b -> p (a b)")), ALU.mult)
                PM = self.bank()
                for s_ in range(4):
                    for hh in range(2):
                        R = slice(hh * 64, (hh + 1) * 64)
                        o.mm(PM[R, s_ * 64:(s_ + 1) * 64], [(kdt[:, s_, R], vtm[:, s_, hh * 64:(hh + 1) * 64])])
                for s_ in range(4):
                    cs = slice(s_ * 128, (s_ + 1) * 128)
                    for hh in range(2):
                        R = slice(hh * 64, (hh + 1) * 64)
                        o.mm(acc[R, cs], [(vtm[:, s_, hh * 64:(hh + 1) * 64], attm[hh][:, s_, :]), (GSb[R, :], qdb[R, cs])])
                    eL = eb[:, s_ * 128 + 127:s_ * 128 + 128]
                    o.ts(GS32[:], GS32[:], eL, None, ALU.mult)
                    o.stt(GS32[:], PM[:, s_ * 64:(s_ + 1) * 64], eL, GS32[:], ALU.mult, ALU.add)
                    o.copy(GSb[:], GS32[:], eng="pool")
                o.copy(T[3][:], acc[:], eng="act")
                self.group_rms(T[3][:], 64, cols[:, K_GLAN:K_GLAN + 1], Yt[:, 0, :], T[4][:], post_mul=T[2][:])
                o.dma(V(Yb_, Yr[:, hf, 0, tsl]), Yt[:, 0, :], "yo")
                o.dma(V(Yb_, Yr[:, hf, 3, tsl]), Yt[:, 3, :], "yo2")
            else:
                for c in range(3):
                    pr = proj_fm(P2_CX + c * 128)
                    o.copy(raw[c][:, 3:TT + 3], pr[:], eng="act")
                    tc_ = T[0] if c % 2 == 0 else T[1]
                    o.ts(tc_[:], raw[c][:, 0:TT], cols[:, K_CW + c:K_CW + c + 1], None, ALU.mult, eng="pool")
                    for k in range(1, 4):
                        o.stt(tc_[:], raw[c][:, k:TT + k], cols[:, K_CW + 3 * k + c:K_CW + 3 * k + c + 1], tc_[:], ALU.mult, ALU.add)
                    o.copy(raw[c][:, 0:3], raw[c][:, TT:TT + 3], eng="pool")
                    bcol = cols[:, K_CB + c:K_CB + c + 1]
                    if c == 0:
                        o.act(xs32[:], tc_[:], AF.Silu, bias=bcol)
                        o.copy(xsb[:], xs32[:], eng="pool")
                    elif c == 1:
                        o.act(BT[:], tc_[:], AF.Silu, bias=bcol)
                    else:
                        o.act(CT[:], tc_[:], AF.Silu, bias=bcol)
                pz = proj_fm(P2_CZ)
                o.act(zs[:], pz[:], AF.Silu)
                PD = self.bank()
                for s_ in range(4):
                    o.mm(PD[:, s_ * 2:(s_ + 1) * 2],
                         [(self.ht[:, k, s_ * 128:(s_ + 1) * 128], win[:, k, P2_CDT:P2_CDT + 2]) for k in range(NKC)])
                dt4 = V(dt_tm, dt_tm.ap.rearrange("p a b -> p (a b)"))
                a4 = V(a_tm, a_tm.ap.rearrange("p a b -> p (a b)"))
                o.tt(dt4, PD[:, 0:8], tm[:, M_DTB4:M_DTB4 + 8], ALU.add)
                o.act(dt4, dt4, AF.Exp)
                o.act(dt4, dt4, AF.Ln, bias=1.0)
                o.stt(a4, dt4, -1.0, aneg4[:], ALU.mult, ALU.mult)
                PCS = self.bank()
                o.mm(PCS[:, 0:8], [(self.triU[:], a4)])
                o.mm(PCS[:, 8:16], [(self.ones_f[:], a4)])
                o.copy(acs4[:], PCS[:, 0:8])
                o.tt(dte4[:], PCS[:, 8:16], acs4[:], ALU.subtract)
                o.act(dte4[:], dte4[:], AF.Exp)
                o.act(eal4[:], PCS[:, 8:16], AF.Exp)
                PX = self.bank()
                PXb = V(PX, PX.ap[:, 0:256].bitcast(BF16))
                for s_ in range(4):
                    o.transpose(V(PX, PXb.ap[:, s_ * 128:(s_ + 1) * 128]), xsb[:, s_ * 128:(s_ + 1) * 128], self.ident_b[:])
                Xd8 = V(Xd4, Xd4.ap.rearrange("p a (h e) -> p (a h) e", h=2))
                Xdd8 = V(Xdd4, Xdd4.ap.rearrange("p a (h e) -> p (a h) e", h=2))
                o.tt(Xd8, V(PX, PXb.ap.rearrange("p (g e) -> p g e", e=64)),
                     V(dt_tm, dt4.ap.unsqueeze(2).broadcast_to([128, 8, 64])), ALU.mult)
                o.tt(Xdd8, Xd8, V(dte4, dte4.ap.unsqueeze(2).broadcast_to([128, 8, 64])), ALU.mult, eng="pool")
                PBt = self.bank()
                PBb = V(PBt, PBt.ap[:, 0:256].bitcast(BF16))
                for s_ in range(4):
                    o.transpose(V(PBt, PBb.ap[:, s_ * 128:(s_ + 1) * 128]), BT[:, s_ * 128:(s_ + 1) * 128], self.ident_b[:])
                o.copy(V(Btm4, Btm4.ap.rearrange("p a b -> p (a b)")), PBb)
                PSC = self.bank()
                for s_ in range(4):
                    cs = slice(s_ * 128, (s_ + 1) * 128)
                    o.mm(PSC[:, cs], [(BT[:, cs], CT[:, cs])])
                for h in range(2):
                    ah = V(a_tm, a_tm.ap[:, :, h:h + 1].broadcast_to([128, 4, 128]))
                    o.tt(lseg4[h][:], self.triS4[:], ah, ALU.mult, eng="pool")
                    o.tt(abc4[h][:], self.ones4[:], ah, ALU.mult, eng="pool")
                    PSEG = self.bank()
                    PAB = self.bank()
                    for s_ in range(4):
                        cs = slice(s_ * 128, (s_ + 1) * 128)
                        o.mm(PSEG[:, cs], [(lseg4[h][:, s_, :], self.triU[:]), (self.ident_f[:], self.maskb[:])])
                        o.mm(PAB[:, cs], [(abc4[h][:, s_, :], self.triU[:])])
                    o.act(LT4[h][:], PSEG[:], AF.Exp)
                    o.tt(WT4[h][:], PSC[:], LT4[h][:], ALU.mult)
                    o.act(Ecs4[h][:], PAB[:], AF.Exp)
                    o.tt(CsT4[h][:], CT[:], Ecs4[h][:], ALU.mult)
                for s_ in range(4):
                    cs = slice(s_ * 128, (s_ + 1) * 128)
                    for h in range(2):
                        hc = slice(h * 64, (h + 1) * 64)
                        o.mm(acc[hc, cs], [(Xd4[:, s_, hc], WT4[h][:, cs]), (STb[h][:], CsT4[h][:, cs])])
                        pst = self.bank()
                        o.mm(pst[:, 0:64], [(Btm4[:, s_, :], Xdd4[:, s_, hc])])
                        o.stt(ST32[h][:], ST32[h][:], eal4[:, s_ * 2 + h:s_ * 2 + h + 1], pst[:, 0:64], ALU.mult, ALU.add)
                        o.copy(STb[h][:], ST32[h][:], eng="pool")
                o.stt(T[2][:], xs32[:], cols[:, K_D:K_D + 1], acc[:], ALU.mult, ALU.add)
                o.tt(T[3][:], T[2][:], zs[:], ALU.mult)
                self.group_rms(T[3][:], 128, cols[:, K_SN:K_SN + 1], Yt[:, 2, :], None)
                o.dma(V(Yb_, Yr[:, hf, 2, tsl]), Yt[:, 2, :], "yo")
                pq = proj_fm(P2_BQ)
                o.copy(T[0][:], pq[:], eng="act")
                self.group_rms(T[0][:], 64, cols[:, K_FQN:K_FQN + 1], qb[:], T[4][:], post_scale=0.125)
                o.dma(V(scr["Q"], scr["Q"].ap[:, ts_]), qb[:], "qo")
                pk = proj_fm(P2_BK)
                o.copy(T[1][:], pk[:], eng="act")
                self.group_rms(T[1][:], 64, cols[:, K_FKN:K_FKN + 1], kb[:], None)
                o.dma(V(scr["K"], scr["K"].ap[:, ts_]), kb[:], "ko")
                for s_ in range(4):
                    kt = t * 4 + s_
                    pv = proj_tm(s_, P2_BV, 130)
                    for h in range(2):
                        o.copy(self.VA[:, kt, h, 0:64], pv[:, h * 64:(h + 1) * 64], eng="act" if h % 2 == 0 else "dve")
                    o.tt(nl[:, s_, :], pv[:, 128:130], tm[:, M_FB:M_FB + 2], ALU.add)
                    o.act(nl[:, s_, :], nl[:, s_, :], AF.Exp, scale=-1.0)
                    o.act(nl[:, s_, :], nl[:, s_, :], AF.Ln, bias=1.0)
                    for g in range(3):
                        o.copy(nl3[:, s_, g * 32:g * 32 + 2], nl[:, s_, :], eng="pool")
                    pg_ = self.bank()
                    o.mm(pg_[:, 0:2], [(self.ones_f[:], nl[:, s2, :]) for s2 in range(s_)] + [(self.triU[:], nl[:, s_, :])])
                    o.tt(sm[3][:], pg_[:, 0:2], gcar_bc[:], ALU.add)
                    o.ts(self.EB[:, kt, :], sm[3][:], -FOX_SHIFT, None, ALU.add)
                ptot = self.bank()
                o.mm(ptot[:, 0:2], [(self.ones_f[:], nl[:, s2, :]) for s2 in range(4)])
                o.tt(gcar_bc[:], gcar_bc[:], ptot[:, 0:2], ALU.add)
                pgt = self.bank()
                for s_ in range(4):
                    cs = slice(s_ * 128, (s_ + 1) * 128)
                    o.mm(pgt[0:96, cs], [(nl3[:, s2, :], self.ones_f[:]) for s2 in range(s_)] + [(nl3[:, s_, :], self.triU[:])])
                GT = F1[0]
                o.ts(GT[:], pgt[0:96, :], gcar_T[:, 0:1], None, ALU.add)
                o.copy(gcar_T[:], GT[:, TT - 1:TT])
                hi, mid, lo = F1b[0], F1b[1], F1b[2]
                r1, r2 = F1[1], F1[2]
                o.ts(hi[:], GT[:], -1.0, None, ALU.mult)
                o.stt(r1[:], GT[:], -1.0, hi[:], ALU.mult, ALU.subtract)
                o.copy(mid[:], r1[:])
                o.tt(r2[:], r1[:], mid[:], ALU.subtract)
                o.copy(lo[:], r2[:])
                o.copy(Fp[0:4, :], hi[0:4, :])
                o.copy(Fp[32:36, :], mid[32:36, :])
                o.copy(Fp[64:68, :], lo[64:68, :])
                o.dma(V(scr["FP"], scr["FP"].ap[:, ts_]), Fp[:], "fpo")

    def mix_b(self, cols_ap, scr, y_gather=None):
        o = self.o
        A = self.alloc
        NT = SEQ // TT
        self.S.barrier()
        self.poff = self.mixb_base
        self.ring = [2, 3, 4, 5, 6, 7]
        oacc = [self.psum[0], self.psum[1]]
        cols = A([128, NCOLT], F32, "cols_b")
        o.dma(cols[:], V(self.S.dram(cols_ap, "colsd2"), cols_ap), "cols")
        KTh = [A([67, SEQ], BF16, "KTh%d" % h) for h in range(2)]
        for h in range(2):
            o.memset(KTh[h][64:67, :], 1.0)
            for q4 in range(4):
                qs = slice(q4 * (SEQ // 4), (q4 + 1) * (SEQ // 4))
                o.dma(KTh[h][0:64, qs], V(scr["K"], scr["K"].ap[h * 64:(h + 1) * 64, qs]), "kth%d" % h)
        Qh = [[A([67, TT], BF16, "Qh%d_%d" % (i, h)) for h in range(2)] for i in range(2)]
        FPr = scr["FP"].ap.rearrange("(g x) s -> x g s", x=32)
        PT = [A([128, TT], BF16, "PT%d" % i) for i in range(4)]
        Oa = A([65, TT], F32, "Oa")
        Osq = A([65, TT], BF16, "Osq")
        rs = A([64, TT], F32, "rs_b")
        Yb = [A([64, TT], BF16, "Yb%d" % h) for h in range(4)]
        Yrows = [(b_, v_.rearrange("(hf c g p) s -> p hf c g s", hf=2, c=4, g=2)) for b_, v_ in scr["Y"]]
        LAG = 2
        npt = 0
        nyb = 0

        def load_qf(qt):
            ts2 = slice(qt * TT, (qt + 1) * TT)
            for h in range(2):
                o.dma(Qh[qt % 2][h][0:64, :], V(scr["Q"], scr["Q"].ap[h * 64:(h + 1) * 64, ts2]), "qi%d_%d" % (qt % 2, h))
                o.dma(Qh[qt % 2][h][64:67, :], V(scr["FP"], FPr[h, :, ts2]), "qi%d_%d" % (qt % 2, h))
        load_qf(0)
        for qt in range(NT):
            hf, tl = qt // (NT // 2), qt % (NT // 2)
            tsl = slice((tl * TT) % CH, (tl * TT) % CH + TT)
            Ybuf, Yrow = Yrows[(tl * TT) // CH]
            if qt + 1 < NT:
                load_qf(qt + 1)
            Qc = Qh[qt % 2]
            nk = (qt + 1) * 4
            blocks = [(h, kt) for h in range(2) for kt in range(nk)]
            pend = []

            def stage2(item):
                nonlocal nyb
                h, kt, col0, pt = item
                oa = oacc[h % 2]
                o.mm1(oa[0:65, col0:TT], self.VA[:, kt, h, 0:65], pt[:, col0:TT], kt == 0, kt == nk - 1)
                if kt == nk - 1:
                    o.copy(Oa[:], oa[0:65, :], eng="act")
                    o.act(Osq[:], Oa[:], AF.Square)
                    pd = self.bank()
                    o.mm(pd[0:64, :], [(self.wden[:], Osq[:])])
                    o.act(rs[:], pd[0:64, :], AF.Sqrt)
                    o.recip(rs[:], rs[:])
                    yb = Yb[nyb % 4]
                    nyb += 1
                    o.stt(yb[:], Oa[0:64, :], cols[0:64, K_FON + h:K_FON + h + 1], rs[:], ALU.mult, ALU.mult)
                    o.dma(V(Ybuf, Yrow[:, hf, 1, h, tsl]), yb[:], "ybo%d" % (nyb % 4))

            for (h, kt) in blocks:
                R = slice(h * 64, (h + 1) * 64)
                c = kt - qt * 4
                col0 = max(c, 0) * 128
                sT = self.bank()
                o.mm(sT[:, col0:TT], [(KTh[h][:, kt * 128:(kt + 1) * 128], Qc[h][:, col0:TT])])
                if c >= 0:
                    o.tt(sT[:, col0:col0 + 128], sT[:, col0:col0 + 128], self.maskb[:], ALU.add)
                pt = PT[npt % 4]
                npt += 1
                o.act(pt[:, col0:TT], sT[:, col0:TT], AF.Exp, bias=self.EB[:, kt, h:h + 1])
                pend.append((h, kt, col0, pt))
                if len(pend) > LAG:
                    stage2(pend.pop(0))
            while pend:
                stage2(pend.pop(0))
            if y_gather is not None and hf == 1 and (tl * TT) % CH + TT == CH:
                k_ = (tl * TT) // CH
                self.all_gather(scr["Y"][k_][0], y_gather[k_][0])

    def outproj_phase(self, x_in, x_out, yfull, cols_ap, tm_ap, w_out_ap):
        o = self.o
        A = self.alloc
        self.phase_begin()
        self.ring = list(range(8))
        wo = A([128, NKC, D_MODEL], BF16, "wo")
        self.load_weight(wo, w_out_ap, NKC, D_MODEL, "wA")
        self.load_tables(cols_ap, tm_ap)
        cols = self.cols
        xt = A([128, NKC, TT], F32, "xt_o")
        Y0 = A([128, NKC, TT], BF16, "Y0")
        Y1 = A([128, NKC, TT], BF16, "Y1")
        Ys = A([128, NKC, TT], BF16, "Ys")
        xin_r = x_in.ap.rearrange("(c p) s -> p c s", p=128)
        xout_r = x_out.ap.rearrange("(c p) s -> p c s", p=128)
        yrs = [(b_, v_.rearrange("(r hf c p) s -> p hf r c s", r=2, hf=2, p=128)) for b_, v_ in yfull]
        for t in range(HALF // TT):
            ts_ = slice(t * TT, (t + 1) * TT)
            o.dma(xt[:], V(x_in, xin_r[:, :, ts_]), "xt")
            yfb, yr = yrs[(t * TT) // CH]
            tsc = slice((t * TT) % CH, (t * TT) % CH + TT)
            for rr in range(2):
                o.dma(Y0[:, rr * 4:(rr + 1) * 4, :], V(yfb, yr[:, 0, rr, :, tsc]), "y0")
                o.dma(Y1[:, rr * 4:(rr + 1) * 4, :], V(yfb, yr[:, 1, rr, :, tsc]), "y1")
            o.ts(Ys[:], Y0[:], cols[:, K_SEL:K_SEL + 1], None, ALU.mult)
            o.stt(Ys[:], Y1[:], cols[:, K_SEL + 1:K_SEL + 2], Ys[:], ALU.mult, ALU.add)
            for oc in range(NKC):
                py = self.bank()
                o.mm(py[:], [(wo[:, k, oc * 128:(oc + 1) * 128], Ys[:, k, :]) for k in range(NKC)])
                o.tt(xt[:, oc, :], xt[:, oc, :], py[:], ALU.add)
            o.dma(V(x_out, xout_r[:, :, ts_]), xt[:], "xo")


def build_program(depth, debug=None):
    nc = bass.Bass("TRN2", target_bir_lowering=False)
    stack = ExitStack()

    def din(name, shape, dt=F32):
        return nc.dram_tensor(name, list(shape), dt, kind="ExternalInput").ap()
    xT = din("xT", [D_MODEL, HALF])
    w = {}
    for l in range(depth):
        for f in (1, 2):
            w["g%d_%d" % (f, l)] = din("ffn%d_w_gate_%d" % (f, l), [D_MODEL, D_FF])
            w["u%d_%d" % (f, l)] = din("ffn%d_w_up_%d" % (f, l), [D_MODEL, D_FF])
            w["d%d_%d" % (f, l)] = din("ffn%d_w_down_%d" % (f, l), [D_FF, D_MODEL])
        w["win1_%d" % l] = din("w_in1_%d" % l, [D_MODEL, P1_N])
        w["win2_%d" % l] = din("w_in2_%d" % l, [D_MODEL, P2_N])
        w["wout_%d" % l] = din("w_out_%d" % l, [D_MODEL, D_MODEL])
        w["wup_%d" % l] = din("gla_wup_%d" % l, [16, 128])
        w["cols_%d" % l] = din("cols_%d" % l, [128, NCOLT])
        w["tm_%d" % l] = din("tm_%d" % l, [128, NTM])
    outT = nc.dram_tensor("outT", [D_MODEL, HALF], F32, kind="ExternalOutput").ap()
    with stack:
        B = Builder(nc, stack, HALF)
        S = B.S

        def scratch(name, shape, dt):
            return S.dram(nc.dram_tensor(name, list(shape), dt).ap(), name)
        xa = scratch("xa", [D_MODEL, HALF], F32)
        xb = scratch("xb", [D_MODEL, HALF], F32)
        NCH = HALF // CH

        def cc_pair(name):
            srcs, dsts = [], []
            for k in range(NCH):
                sb_ = scratch("%s_s%d" % (name, k), [128, 8 * CH], BF16)
                db_ = scratch("%s_d%d" % (name, k), [256, 8 * CH], BF16)
                srcs.append((sb_, sb_.ap.rearrange("p (a s) -> (p a) s", s=CH)))
                dsts.append((db_, db_.ap.rearrange("p (a s) -> (p a) s", s=CH)))
            return srcs, dsts
        hsrc, hfull = cc_pair("h")
        ysrc, yfull = cc_pair("y")
        scr = {"Y": ysrc,
               "Q": scratch("scrQ", [128, SEQ], BF16),
               "K": scratch("scrK", [128, SEQ], BF16),
               "FP": scratch("scrFP", [96, SEQ], BF16)}
        cur = S.dram(xT, "xT")
        xout = S.dram(outT, "outT")
        for l in range(depth):
            last = l == depth - 1
            ct = (w["cols_%d" % l], w["tm_%d" % l])
            B.ffn_phase(cur, xa, ct[0], ct[1], K_FFN1, w["g1_%d" % l], w["u1_%d" % l], w["d1_%d" % l], h_out=hsrc, h_gather=hfull)
            B.mix_a(1, hfull, ct[0], ct[1], w["win1_%d" % l], w["wup_%d" % l], scr)
            B.mix_a(2, hfull, ct[0], ct[1], w["win2_%d" % l], w["wup_%d" % l], scr)
            B.mix_b(ct[0], scr, y_gather=yfull)
            B.outproj_phase(xa, xb, yfull, ct[0], ct[1], w["wout_%d" % l])
            dst = xout if last else xa
            B.ffn_phase(xb, dst, ct[0], ct[1], K_FFN2, w["g2_%d" % l], w["u2_%d" % l], w["d2_%d" % l])
            cur = dst
        S.barrier()
        S.emit()
    return nc


def make_inputs(inp, depth, r):
    d = {}
    for l in range(depth):
        ffn = {1: (inp["ffn1_w_gate"], inp["ffn1_w_up"], inp["ffn1_w_down"]),
               2: (inp["ffn2_w_gate"], inp["ffn2_w_up"], inp["ffn2_w_down"])}
        for f in (1, 2):
            d["ffn%d_w_gate_%d" % (f, l)] = np.ascontiguousarray(ffn[f][0][l], dtype=np.float32)
            d["ffn%d_w_up_%d" % (f, l)] = np.ascontiguousarray(ffn[f][1][l], dtype=np.float32)
            d["ffn%d_w_down_%d" % (f, l)] = np.ascontiguousarray(ffn[f][2][l], dtype=np.float32)
        p1, p2 = host_win(inp, l, r)
        d["w_in1_%d" % l] = p1
        d["w_in2_%d" % l] = p2
        d["w_out_%d" % l] = host_wout(inp, l)
        d["gla_wup_%d" % l] = np.ascontiguousarray(np.asarray(inp["gla_w_gate_up"][l], np.float32)[:, r * 128:(r + 1) * 128])
        cols, tm = host_tables(inp, l, r)
        d["cols_%d" % l] = cols
        d["tm_%d" % l] = tm
    return d


_PROG_CACHE = {}


def kernel(**inputs):
    x = np.asarray(inputs["x"], np.float32)
    Bsz, L, D = x.shape
    depth = inputs["w_in"].shape[0]
    assert (Bsz, L, D) == (4, SEQ, D_MODEL)
    if depth not in _PROG_CACHE:
        _PROG_CACHE[depth] = build_program(depth)
    nc = _PROG_CACHE[depth]
    shared = [make_inputs(inputs, depth, r) for r in range(2)]
    in_maps = []
    for c in range(8):
        b, r = c // 2, c % 2
        m = dict(shared[r])
        m["xT"] = np.ascontiguousarray(x[b, r * HALF:(r + 1) * HALF].T)
        in_maps.append(m)
    res = run_bass_kernel_spmd(nc, in_maps, core_ids=list(range(8)))
    out = np.empty((Bsz, L, D), np.float32)
    for c in range(8):
        b, r = c // 2, c % 2
        out[b, r * HALF:(r + 1) * HALF] = res.results[c]["outT"].T
    return out
```

```python
import numpy as np
from contextlib import ExitStack
import concourse.bass as bass
import concourse.mybir as mybir
from concourse.bass_utils import run_bass_kernel_spmd

F32 = mybir.dt.float32
BF16 = mybir.dt.bfloat16
ALU = mybir.AluOpType
AF = mybir.ActivationFunctionType
AX = mybir.AxisListType

SEM_CAP = 30000


class Buf:
    __slots__ = ("ap", "name", "lw", "rd")

    def __init__(self, ap, name):
        self.ap = ap
        self.name = name
        self.lw = None
        self.rd = {}

    def __getitem__(self, k):
        return V(self, self.ap[k])


class V:
    __slots__ = ("b", "ap")

    def __init__(self, b, ap):
        self.b = b
        self.ap = ap


def _bufs(*vs):
    return [v.b for v in vs if isinstance(v, V)]


def _ap(v):
    return v.ap if isinstance(v, V) else v


class Sched:
    ENGS = ("pe", "act", "dve", "pool", "sp")

    def __init__(self, nc, stack):
        self.nc = nc
        self.stack = stack
        self.streams = {e: [] for e in self.ENGS}
        self.dom = {e: [] for e in self.ENGS}
        self.waited = {e: {} for e in self.ENGS}
        self.nbuf = 0

    def sb(self, shape, dtype, name=None):
        self.nbuf += 1
        name = "%s_%d" % (name or "sb", self.nbuf)
        t = self.stack.enter_context(self.nc.sbuf_tensor(name, list(shape), dtype))
        return Buf(t, name)

    def ps(self, shape, dtype=F32, name=None):
        self.nbuf += 1
        name = "%s_%d" % (name or "ps", self.nbuf)
        t = self.stack.enter_context(self.nc.psum_tensor(name, list(shape), dtype))
        return Buf(t, name)

    def dram(self, ap, name):
        return Buf(ap, name)

    def _deps(self, eng, reads, writes):
        deps = {}

        def add(tok):
            if tok is None:
                return
            d, s = tok
            if d == "pe" and eng == "pe":
                return
            if deps.get(d, -1) < s:
                deps[d] = s

        for b in reads:
            add(b.lw)
        for b in writes:
            add(b.lw)
            for d, s in b.rd.items():
                add((d, s))
        w = self.waited[eng]
        out = []
        for d, s in deps.items():
            if w.get(d, -1) >= s:
                continue
            w[d] = s
            out.append((d, s))
        return out

    def op(self, eng, fn, reads=(), writes=(), dma=None):
        deps = self._deps(eng, reads, writes)
        dom = eng if dma is None else "dma:" + dma
        if dom not in self.dom:
            self.dom[dom] = []
        rec = {"fn": fn, "deps": deps, "sig": dma is not None, "dom": dom,
               "seq": len(self.dom[dom])}
        self.dom[dom].append(rec)
        self.streams[eng].append(rec)
        tok = (dom, rec["seq"])
        for b in reads:
            if b.rd.get(dom, -1) < rec["seq"]:
                b.rd[dom] = rec["seq"]
        for b in writes:
            b.lw = tok
            b.rd = {}
        return tok

    def barrier(self):
        last = {d: len(r) - 1 for d, r in self.dom.items() if r}
        for e in self.ENGS:
            deps = []
            for d, s in last.items():
                if self.waited[e].get(d, -1) >= s:
                    continue
                self.waited[e][d] = s
                deps.append((d, s))
            rec = {"fn": (lambda eng: eng.nop()), "deps": deps, "sig": False, "dom": e,
                   "seq": len(self.dom[e])}
            self.dom[e].append(rec)
            self.streams[e].append(rec)

    def emit(self):
        nc = self.nc
        for e in self.ENGS:
            for rec in self.streams[e]:
                for d, s in rec["deps"]:
                    self.dom[d][s]["sig"] = True
        semtab = {}
        for d, recs in self.dom.items():
            step = 16 if (d.startswith("dma:") and not d.startswith("dma:cc")) else 1
            cap = SEM_CAP // step
            c = 0
            for rec in recs:
                if rec["sig"]:
                    rec["sv"] = (c // cap, (c % cap + 1) * step)
                    c += 1
            nep = (c + cap - 1) // cap
            semtab[d] = [self.stack.enter_context(nc.semaphore("s_%s_%d" % (d.replace(":", "_"), i)))
                         for i in range(max(nep, 0))]
        self.nsem = sum(len(v) for v in semtab.values())
        block = self.stack.enter_context(nc.Block())

        def run(engname):
            def body(eng):
                for rec in self.streams[engname]:
                    for d, s in rec["deps"]:
                        ep, val = self.dom[d][s]["sv"]
                        eng.wait_ge(semtab[d][ep], val)
                    ins = rec["fn"](eng)
                    if rec["sig"]:
                        ep, val = rec["sv"]
                        step = 16 if (rec["dom"].startswith("dma:") and not rec["dom"].startswith("dma:cc")) else 1
                        ins.then_inc(semtab[rec["dom"]][ep], step)
            return body

        block.tensor(run("pe"))
        block.scalar(run("act"))
        block.vector(run("dve"))
        block.gpsimd(run("pool"))
        block.sync(run("sp"))


class Ops:
    def __init__(self, S):
        self.S = S

    def dma(self, out, in_, chan, eng="sp"):
        return self.S.op(eng, lambda e: e.dma_start(out=out.ap, in_=in_.ap), reads=[in_.b], writes=[out.b], dma=chan)

    def act(self, out, in_, func, bias=0.0, scale=1.0, eng="act"):
        return self.S.op(eng, lambda e: e.activation(out=out.ap, in_=in_.ap, func=func, bias=_ap(bias), scale=_ap(scale)),
                         reads=_bufs(in_, bias, scale), writes=[out.b])

    def tt(self, out, in0, in1, op, eng="dve"):
        return self.S.op(eng, lambda e: e.tensor_tensor(out=out.ap, in0=in0.ap, in1=in1.ap, op=op),
                         reads=_bufs(in0, in1), writes=[out.b])

    def ts(self, out, in0, s1, s2, op0, op1=None, eng="dve"):
        if op1 is None:
            return self.S.op(eng, lambda e: e.tensor_scalar(out=out.ap, in0=in0.ap, scalar1=_ap(s1), scalar2=None, op0=op0),
                             reads=_bufs(in0, s1), writes=[out.b])
        return self.S.op(eng, lambda e: e.tensor_scalar(out=out.ap, in0=in0.ap, scalar1=_ap(s1), scalar2=_ap(s2), op0=op0, op1=op1),
                         reads=_bufs(in0, s1, s2), writes=[out.b])

    def stt(self, out, in0, scalar, in1, op0, op1, eng="dve"):
        eng = "dve"
        return self.S.op(eng, lambda e: e.scalar_tensor_tensor(out=out.ap, in0=in0.ap, scalar=_ap(scalar), in1=in1.ap, op0=op0, op1=op1),
                         reads=_bufs(in0, scalar, in1), writes=[out.b])

    def copy(self, out, in_, eng="dve"):
        if eng == "act":
            return self.act(out, in_, AF.Copy)
        return self.S.op(eng, lambda e: e.tensor_copy(out=out.ap, in_=in_.ap), reads=[in_.b], writes=[out.b])

    def recip(self, out, in_):
        return self.S.op("dve", lambda e: e.reciprocal(out=out.ap, in_=in_.ap), reads=[in_.b], writes=[out.b])

    def memset(self, out, val, eng="pool"):
        return self.S.op(eng, lambda e: e.memset(out.ap, val), writes=[out.b])

    def aselect(self, out, in_, pattern, cmp, fill, base, cm):
        return self.S.op("pool", lambda e: e.affine_select(out=out.ap, in_=in_.ap, pattern=pattern, compare_op=cmp,
                                                           fill=fill, base=base, channel_multiplier=cm),
                         reads=[in_.b], writes=[out.b])

    def mm(self, out, pairs, extra_reads=()):
        n = len(pairs)

        def fn(e):
            ins = None
            for i, (l, r) in enumerate(pairs):
                ins = e.matmul(out.ap, lhsT=l.ap, rhs=r.ap, start=(i == 0), stop=(i == n - 1))
            return ins
        rd = []
        for l, r in pairs:
            rd.append(l.b)
            rd.append(r.b)
        rd.extend(extra_reads)
        return self.S.op("pe", fn, reads=rd, writes=[out.b])

    def mm1(self, out, l, r, start, stop):
        return self.S.op("pe", lambda e: e.matmul(out.ap, lhsT=l.ap, rhs=r.ap, start=start, stop=stop),
                         reads=[l.b, r.b], writes=[out.b])

    def transpose(self, out, in_, ident):
        return self.S.op("pe", lambda e: e.transpose(out.ap, in_.ap, ident.ap), reads=[in_.b, ident.b], writes=[out.b])


D_MODEL = 1024
D_FF = 2816
NKC = D_MODEL // 128
NFC = D_FF // 128
IN_COLS = 3608
TT = 512
EPS = 1e-6
FOX_SHIFT = 10.0
NEG = -30000.0
SEQ = 8192
HALF = SEQ // 2
CH = 1024
GROUPS = [[0, 1], [2, 3], [4, 5], [6, 7]]

C_AQ, C_AK, C_AV, C_AG, C_ALR = 0, 256, 512, 768, 1024
C_BQ, C_BK, C_BV, C_BF = 1040, 1296, 1552, 1808
C_CZ, C_CX, C_CDT = 1812, 2068, 2836
C_DB, C_DC, C_DV = 2840, 3096, 3352
P1_GQ, P1_GK, P1_GV, P1_GG, P1_GLR, P1_DB, P1_DC, P1_DV, P1_N = 0, 128, 256, 384, 512, 528, 656, 784, 912
P2_BQ, P2_BK, P2_BV, P2_CZ, P2_CX, P2_CB, P2_CC, P2_CDT, P2_N = 0, 128, 256, 386, 514, 642, 770, 898, 900

K_FFN1, K_MIX, K_FFN2 = 0, 8, 16
K_GLAN = 24
K_FQN, K_FKN = 25, 26
K_FON = 27
K_CB = 29
K_CW = 32
K_D = 44
K_SN = 45
K_SCW = 46
K_SCN = 49
K_SEL = 50
NCOLT = 52
M_ALOG, M_FB, M_DTB, M_BG = 0, 2, 4, 6
M_ALOG4, M_DTB4, M_BG4 = 134, 142, 150
NTM = 150 + 512


def host_tables(inp, l, r):
    f32 = np.float32
    cols = np.zeros((128, NCOLT), f32)
    hs = slice(r * 128, (r + 1) * 128)

    def put(k, vec):
        v = np.asarray(vec, f32).reshape(-1, 128)
        for c in range(v.shape[0]):
            cols[:, k + c] = v[c]
    put(K_FFN1, inp["ffn1_norm"][l]); put(K_MIX, inp["mix_norm"][l]); put(K_FFN2, inp["ffn2_norm"][l])
    put(K_GLAN, inp["gla_norm"][l][hs])
    put(K_FQN, np.tile(inp["fox_q_norm"][l], 2)); put(K_FKN, np.tile(inp["fox_k_norm"][l], 2))
    fo = np.asarray(inp["fox_out_norm"][l], f32).reshape(4, 64)
    for h in range(2):
        cols[0:64, K_FON + h] = fo[2 * r + h]
    cb = np.asarray(inp["ssm_conv_b"][l], f32)
    cw = np.asarray(inp["ssm_conv_w"][l], f32)
    chs = [slice(r * 128, (r + 1) * 128), slice(256 + r * 128, 256 + (r + 1) * 128), slice(512 + r * 128, 512 + (r + 1) * 128)]
    for c in range(3):
        cols[:, K_CB + c] = cb[chs[c]]
        for k in range(4):
            cols[:, K_CW + 3 * k + c] = cw[k][chs[c]]
    put(K_D, np.repeat(np.asarray(inp["ssm_D"][l], f32)[2 * r:2 * r + 2], 64))
    put(K_SN, inp["ssm_norm"][l][hs])
    for k in range(3):
        put(K_SCW + k, inp["sc_conv_w"][l][k][hs])
    put(K_SCN, inp["sc_out_norm"][l][hs])
    cols[:, K_SEL + r] = 1.0
    tm = np.zeros((128, NTM), f32)
    tm[:, M_ALOG:M_ALOG + 2] = np.asarray(inp["ssm_A_log"][l], f32)[None, 2 * r:2 * r + 2]
    tm[:, M_FB:M_FB + 2] = np.asarray(inp["fox_b_forget"][l], f32)[None, 2 * r:2 * r + 2]
    tm[:, M_DTB:M_DTB + 2] = np.asarray(inp["ssm_dt_bias"][l], f32)[None, 2 * r:2 * r + 2]
    tm[:, M_BG:M_BG + 128] = np.asarray(inp["gla_b_gate"][l], f32)[None, hs]
    tm[:, M_ALOG4:M_ALOG4 + 8] = np.tile(tm[:, M_ALOG:M_ALOG + 2], (1, 4))
    tm[:, M_DTB4:M_DTB4 + 8] = np.tile(tm[:, M_DTB:M_DTB + 2], (1, 4))
    tm[:, M_BG4:M_BG4 + 512] = np.tile(tm[:, M_BG:M_BG + 128], (1, 4))
    return cols, tm


def host_win(inp, l, r):
    w = np.asarray(inp["w_in"][l], np.float32)
    h = lambda c0: w[:, c0 + r * 128: c0 + (r + 1) * 128]
    p1 = np.concatenate([h(C_AQ), h(C_AK), h(C_AV), h(C_AG), w[:, C_ALR:C_ALR + 16], h(C_DB), h(C_DC), h(C_DV)], axis=1)
    p2 = np.concatenate([h(C_BQ), h(C_BK), h(C_BV), w[:, C_BF + 2 * r:C_BF + 2 * r + 2], h(C_CZ),
                         h(C_CX), h(C_CX + 256), h(C_CX + 512), w[:, C_CDT + 2 * r:C_CDT + 2 * r + 2]], axis=1)
    assert p1.shape[1] == P1_N and p2.shape[1] == P2_N
    return np.ascontiguousarray(p1), np.ascontiguousarray(p2)


def host_wout(inp, l):
    w = np.asarray(inp["w_out"][l], np.float32)
    rows = []
    for rr in range(2):
        for blk in range(4):
            rows.append(w[blk * 256 + rr * 128: blk * 256 + (rr + 1) * 128])
    return np.ascontiguousarray(np.concatenate(rows, axis=0))


import os
DBG = set(os.environ.get("MIXDBG", "sc,gla,ssd,fox,fox2,fox3,b").split(","))
ARENA_UNITS = 212000 // 2


class Builder:
    def __init__(self, nc, stack, S_tok):
        self.nc = nc
        self.S_tok = S_tok
        self.S = Sched(nc, stack)
        self.o = Ops(self.S)
        S = self.S
        self.arena = stack.enter_context(nc.sbuf_tensor("arena", [128, ARENA_UNITS], BF16))
        self.poff = 0
        self.nb = 0
        self.psum = [S.ps([128, TT], F32, "bank%d" % i) for i in range(8)]
        self.pi = 0
        self.ring = list(range(8))
        self.consts()
        self.pbase = self.poff

    def alloc(self, shape, dtype, name):
        n = 1
        for d in shape[1:]:
            n *= d
        units = n * (2 if dtype == F32 else 1)
        units = (units + 15) // 16 * 16
        assert self.poff + units <= ARENA_UNITS, ("SBUF arena overflow", name, self.poff, units)
        ap = self.arena[:, self.poff:self.poff + units]
        self.poff += units
        if dtype == F32:
            ap = ap.bitcast(F32)
            ap = ap[:, 0:n]
        else:
            ap = ap[:, 0:n]
        if len(shape) == 3:
            ap = ap.rearrange("p (a b) -> p a b", a=shape[1])
        elif len(shape) == 4:
            ap = ap.rearrange("p (a b c) -> p a b c", a=shape[1], b=shape[2])
        if shape[0] < 128:
            ap = ap[0:shape[0]]
        self.nb += 1
        return Buf(ap, "%s_%d" % (name, self.nb))

    def phase_begin(self):
        self.S.barrier()
        self.poff = self.pbase

    def bank(self):
        b = self.psum[self.ring[self.pi % len(self.ring)]]
        self.pi += 1
        return b

    def consts(self):
        o = self.o
        A = self.alloc
        self.ones_mean = {}
        for blk in (1024, 128, 64):
            t = A([128, 128], BF16, "ones%d" % blk)
            o.memset(t[:], 1.0 / blk)
            if blk == 64:
                o.memset(t[0:64, 64:128], 0.0)
                o.memset(t[64:128, 0:64], 0.0)
            self.ones_mean[blk] = t
        self.eps_col = A([128, 1], F32, "epscol")
        o.memset(self.eps_col[:], EPS)
        self.ident_b = A([128, 128], BF16, "identb")
        o.memset(self.ident_b[:], 1.0)
        o.aselect(self.ident_b[:], self.ident_b[:], [[-1, 128]], ALU.is_equal, 0.0, 0, 1)
        self.ident_f = A([128, 128], F32, "identf")
        o.memset(self.ident_f[:], 1.0)
        o.aselect(self.ident_f[:], self.ident_f[:], [[-1, 128]], ALU.is_equal, 0.0, 0, 1)
        self.triU = A([128, 128], F32, "triU")
        o.memset(self.triU[:], 1.0)
        o.aselect(self.triU[:], self.triU[:], [[1, 128]], ALU.is_ge, 0.0, 0, -1)
        self.triU_b = A([128, 128], BF16, "triUb")
        o.copy(self.triU_b[:], self.triU[:], eng="pool")
        self.triG = A([128, 128], F32, "triG")
        o.memset(self.triG[:], -1.0 / 16.0)
        o.aselect(self.triG[:], self.triG[:], [[1, 128]], ALU.is_ge, 0.0, 0, -1)
        self.triS = A([128, 128], F32, "triS")
        o.memset(self.triS[:], 1.0)
        o.aselect(self.triS[:], self.triS[:], [[-1, 128]], ALU.is_gt, 0.0, 0, 1)
        self.maskb = A([128, 128], F32, "maskb")
        o.memset(self.maskb[:], 0.0)
        o.aselect(self.maskb[:], self.maskb[:], [[1, 128]], ALU.is_ge, NEG, 0, -1)
        self.ones_f = A([128, 128], F32, "onesf")
        o.memset(self.ones_f[:], 1.0)
        self.triU4 = A([128, 4, 128], F32, "triU4")
        self.triS4 = A([128, 4, 128], F32, "triS4")
        self.ones4 = A([128, 4, 128], F32, "ones4")
        for s4 in range(4):
            o.copy(self.triU4[:, s4, :], self.triU[:], eng="pool")
            o.copy(self.triS4[:, s4, :], self.triS[:], eng="pool")
        o.memset(self.ones4[:], 1.0)
        self.sel = A([96, 4, 128], BF16, "sel")
        o.memset(self.sel[:], 0.0)
        for h in range(4):
            v = self.sel[0:96, h, :]
            o.ts(v, self.ones_f[0:96, :], self.ident_f[0:96, h:h + 1], None, ALU.mult, eng="pool")
            o.stt(v, self.ones_f[0:96, :], self.ident_f[0:96, 32 + h:33 + h], v, ALU.mult, ALU.add, eng="pool")
            o.stt(v, self.ones_f[0:96, :], self.ident_f[0:96, 64 + h:65 + h], v, ALU.mult, ALU.add, eng="pool")
        self.wden = A([65, 64], BF16, "wden")
        o.memset(self.wden[0:65, :], EPS)
        o.memset(self.wden[0:64, :], 1.0 / 64)

    def load_weight(self, dst, w_ap, nk, ncols, chan, rows0=0):
        wd = self.S.dram(w_ap, "wdram")
        for k in range(nk):
            src = w_ap[rows0 + k * 128: rows0 + (k + 1) * 128, :]
            self.o.dma(dst[:, k, :], V(wd, src), chan, eng="pool")

    def load_tables(self, cols_ap, tm_ap):
        self.cols = self.alloc([128, NCOLT], F32, "cols")
        self.tm = self.alloc([128, NTM], F32, "tm")
        self.o.dma(self.cols[:], V(self.S.dram(cols_ap, "colsd"), cols_ap), "cols")
        self.o.dma(self.tm[:], V(self.S.dram(tm_ap, "tmd"), tm_ap), "tm")

    def alloc_xnorm(self):
        self.xt = self.alloc([128, NKC, TT], F32, "xt")
        self.ht = self.alloc([128, NKC, TT], BF16, "ht")
        self.sq = self.alloc([128, 2, TT], BF16, "sq")
        self.rstd = self.alloc([128, TT], F32, "rstd")

    def rmsnorm_tile(self, kcol):
        o = self.o
        ps = self.bank()
        for c in range(NKC):
            o.act(self.sq[:, c % 2, :], self.xt[:, c, :], AF.Square)
            o.mm1(ps[:], self.ones_mean[1024][:], self.sq[:, c % 2, :], c == 0, c == NKC - 1)
        o.act(self.rstd[:], ps[:], AF.Sqrt, bias=self.eps_col[:])
        o.recip(self.rstd[:], self.rstd[:])
        for c in range(NKC):
            o.stt(self.ht[:, c, :], self.xt[:, c, :], self.cols[:, kcol + c:kcol + c + 1], self.rstd[:], ALU.mult, ALU.mult,
                  eng="dve" if c % 2 == 0 else "pool")

    def group_rms(self, y, blk, gain, out, tmp, post_mul=None, post_scale=None):
        o = self.o
        sqb = self.gsq
        o.act(sqb[:], y, AF.Square)
        ps = self.bank()
        o.mm(ps[:], [(self.ones_mean[blk][:], sqb[:])])
        o.act(self.grs[:], ps[:], AF.Sqrt, bias=self.eps_col[:])
        o.recip(self.grs[:], self.grs[:])
        if post_mul is None and post_scale is None:
            o.stt(out, y, gain, self.grs[:], ALU.mult, ALU.mult)
        else:
            o.stt(tmp, y, gain, self.grs[:], ALU.mult, ALU.mult)
            if post_mul is not None:
                o.tt(out, tmp, post_mul, ALU.mult)
            else:
                o.ts(out, tmp, post_scale, None, ALU.mult)

    def ffn_phase(self, x_in, x_out, cols_ap, tm_ap, kcol, wg_ap, wu_ap, wd_ap, h_out=None, h_gather=None):
        o = self.o
        self.phase_begin()
        self.ring = list(range(7))
        pstat = self.psum[7]
        HF = D_FF // 2
        NJ = NFC // 2
        wg = [self.alloc([128, NKC, HF], BF16, "wg%d" % i) for i in range(2)]
        wu = [self.alloc([128, NKC, HF], BF16, "wu%d" % i) for i in range(2)]
        wdn = self.alloc([128, NFC, D_MODEL], BF16, "wd")
        self.load_tables(cols_ap, tm_ap)
        wgd, wud = self.S.dram(wg_ap, "wgd"), self.S.dram(wu_ap, "wud")
        for i in range(2):
            for k in range(NKC):
                o.dma(wg[i][:, k, :], V(wgd, wg_ap[k * 128:(k + 1) * 128, i * HF:(i + 1) * HF]), "wA%d" % i, eng="pool")
            for k in range(NKC):
                o.dma(wu[i][:, k, :], V(wud, wu_ap[k * 128:(k + 1) * 128, i * HF:(i + 1) * HF]), "wB%d" % i, eng="pool")
        self.load_weight(wdn, wd_ap, NFC, D_MODEL, "wC")
        xc = [self.alloc([128, TT], F32, "xc%d" % c) for c in range(NKC)]
        self.ht = self.alloc([128, NKC, TT], BF16, "ht")
        self.sq = self.alloc([128, 2, TT], BF16, "sq")
        self.rstd = self.alloc([128, TT], F32, "rstd")
        act_t = self.alloc([128, NFC, TT], BF16, "actT")
        sg = [self.alloc([128, TT], F32, "sg%d" % i) for i in range(2)]
        xin_r = x_in.ap.rearrange("(c p) s -> p c s", p=128)
        xout_r = x_out.ap.rearrange("(c p) s -> p c s", p=128)
        nsq = 0

        def finish_norm(kc):
            o.act(self.rstd[:], pstat[:], AF.Sqrt, bias=self.eps_col[:])
            o.recip(self.rstd[:], self.rstd[:])
            for c in range(NKC):
                o.stt(self.ht[:, c, :], xc[c][:], self.cols[:, kc + c:kc + c + 1], self.rstd[:], ALU.mult, ALU.mult)

        def stat(c):
            nonlocal nsq
            sqv = self.sq[:, nsq % 2, :]
            nsq += 1
            o.act(sqv, xc[c][:], AF.Square)
            o.mm1(pstat[:], self.ones_mean[1024][:], sqv, c == 0, c == NKC - 1)

        NTL = self.S_tok // TT
        for c in range(NKC):
            o.dma(xc[c][:], V(x_in, xin_r[:, c, 0:TT]), "xt%d" % c)
        for t in range(NTL):
            ts = slice(t * TT, (t + 1) * TT)
            for c in range(NKC):
                stat(c)
            finish_norm(kcol)
            for j in range(NFC):
                pg = self.bank()
                pu = self.bank()
                i, jj = j // NJ, j % NJ
                o.mm(pg[:], [(wg[i][:, k, jj * 128:(jj + 1) * 128], self.ht[:, k, :]) for k in range(NKC)])
                o.mm(pu[:], [(wu[i][:, k, jj * 128:(jj + 1) * 128], self.ht[:, k, :]) for k in range(NKC)])
                o.act(sg[j % 2][:], pg[:], AF.Silu)
                o.tt(act_t[:, j, :], sg[j % 2][:], pu[:], ALU.mult)
            for c in range(NKC):
                py = self.bank()
                o.mm(py[:], [(wdn[:, j, c * 128:(c + 1) * 128], act_t[:, j, :]) for j in range(NFC)])
                o.stt(xc[c][:], py[:], 0.5, xc[c][:], ALU.mult, ALU.add)
                o.dma(V(x_out, xout_r[:, c, ts]), xc[c][:], "xo%d" % c)
                if h_out is not None:
                    stat(c)
                elif t + 1 < NTL:
                    o.dma(xc[c][:], V(x_in, xin_r[:, c, (t + 1) * TT:(t + 2) * TT]), "xt%d" % c)
            if h_out is not None:
                finish_norm(K_MIX)
                hb, hv = h_out[(t * TT) // CH]
                c0 = (t * TT) % CH
                o.dma(V(hb, hv.rearrange("(c p) s -> p c s", p=128)[:, :, c0:c0 + TT]), self.ht[:], "ho")
                if t + 1 < NTL:
                    for c in range(NKC):
                        o.dma(xc[c][:], V(x_in, xin_r[:, c, (t + 1) * TT:(t + 2) * TT]), "xt%d" % c)
                if c0 + TT == CH and h_gather is not None:
                    k_ = (t * TT) // CH
                    self.all_gather(h_out[k_][0], h_gather[k_][0])

    def all_gather(self, src, dst):
        self.S.op("pool", lambda e: e.collective_compute("AllGather", ALU.bypass, replica_groups=GROUPS,
                                                         ins=[src.ap.opt()], outs=[dst.ap.opt()]),
                  reads=[src], writes=[dst], dma="ccag")

    def mix_a(self, part, hfull, cols_ap, tm_ap, w_ap, wup_ap, scr):
        o = self.o
        A = self.alloc
        NT = SEQ // TT
        self.phase_begin()
        self.ring = [1, 2, 3, 4, 5, 6, 7]
        acc = self.psum[0]
        VA_, EB_ = (A([128, SEQ // 128, 2, 66], BF16, "VA"), A([128, SEQ // 128, 2], F32, "EB"))
        if part == 2:
            self.VA, self.EB = VA_, EB_
        self.mixb_base = self.poff
        NW = P1_N if part == 1 else P2_N
        win = A([128, NKC, NW], BF16, "win")
        self.load_weight(win, w_ap, NKC, NW, "wA")
        self.load_tables(cols_ap, tm_ap)
        cols, tm = self.cols, self.tm
        self.ht = A([128, NKC, TT], BF16, "ht")
        self.gsq = A([128, TT], BF16, "gsq")
        self.grs = A([128, TT], F32, "grs")
        T = [A([128, TT], F32, "T%d" % i) for i in range(6)]
        Yt = A([128, 4, TT], BF16, "Yt")
        sm = [A([128, 2], F32, "sm%d" % i) for i in range(6)]
        Q = [A([128, 128], F32, "Q%d" % i) for i in range(4)]
        Qb = [A([128, 128], BF16, "Qb%d" % i) for i in range(4)]
        if part == 1:
            scu = A([128, TT + 2], F32, "scu")
            o.memset(scu[:, 0:2], 0.0)
            wup = A([16, 128], F32, "wup")
            o.dma(wup[:], V(self.S.dram(wup_ap, "wupd"), wup_ap), "wup")
            GS32 = A([128, 64], F32, "GS32")
            GSb = A([128, 64], BF16, "GSb")
            o.memset(GS32[:], 0.0)
            o.memset(GSb[:], 0.0)
            la = A([128, 4, 128], F32, "la")
            vtm = A([128, 4, 128], BF16, "vtm")
            qdb = A([128, TT], BF16, "qdb")
            kdb = A([128, TT], BF16, "kdb")
            kdt = A([128, 4, 128], BF16, "kdt")
            attm = [A([128, 4, 128], BF16, "attm%d" % i) for i in range(2)]
            glr = A([32, TT], F32, "glr")
        else:
            o.memset(self.VA[:], 1.0)
            aneg4 = A([128, 8], F32, "aneg4")
            o.act(aneg4[:], tm[:, M_ALOG4:M_ALOG4 + 8], AF.Exp)
            acs4 = A([128, 8], F32, "acs4")
            dte4 = A([128, 8], F32, "dte4")
            eal4 = A([128, 8], F32, "eal4")
            Xd4 = A([128, 4, 128], BF16, "Xd4")
            Xdd4 = A([128, 4, 128], BF16, "Xdd4")
            Btm4 = A([128, 4, 128], BF16, "Btm4")
            lseg4 = [A([128, 4, 128], F32, "lseg4_%d" % h) for h in range(2)]
            abc4 = [A([128, 4, 128], F32, "abc4_%d" % h) for h in range(2)]
            LT4 = [A([128, TT], F32, "LT4_%d" % h) for h in range(2)]
            Ecs4 = [A([128, TT], F32, "Ecs4_%d" % h) for h in range(2)]
            WT4 = [A([128, TT], BF16, "WT4_%d" % h) for h in range(2)]
            CsT4 = [A([128, TT], BF16, "CsT4_%d" % h) for h in range(2)]
            raw = [A([128, TT + 3], F32, "raw%d" % c) for c in range(3)]
            for c in range(3):
                o.memset(raw[c][:, 0:3], 0.0)
            xs32 = A([128, TT], F32, "xs32")
            xsb = A([128, TT], BF16, "xsb")
            BT = A([128, TT], BF16, "BT")
            CT = A([128, TT], BF16, "CT")
            zs = A([128, TT], F32, "zs")
            dt_tm = A([128, 4, 2], F32, "dt_tm")
            a_tm = A([128, 4, 2], F32, "a_tm")
            Xd = A([128, 128], BF16, "Xd")
            Xdd = A([128, 128], BF16, "Xdd")
            Btm = A([128, 128], BF16, "Btm")
            ST32 = [A([128, 64], F32, "ST32_%d" % h) for h in range(2)]
            STb = [A([128, 64], BF16, "STb%d" % h) for h in range(2)]
            for h in range(2):
                o.memset(ST32[h][:], 0.0)
                o.memset(STb[h][:], 0.0)
            gcar_bc = A([128, 2], F32, "gcar_bc")
            gcar_T = A([96, 1], F32, "gcar_T")
            o.memset(gcar_bc[:], 0.0)
            o.memset(gcar_T[:], 0.0)
            nl = A([128, 4, 2], F32, "nl")
            nl3 = A([128, 4, 96], F32, "nl3")
            o.memset(nl3[:], 0.0)
            Fp = A([96, TT], BF16, "Fp")
            F1 = [A([96, TT], F32, "F1_%d" % i) for i in range(3)]
            F1b = [A([96, TT], BF16, "F1b_%d" % i) for i in range(3)]
            qb = A([128, TT], BF16, "qb")
            kb = A([128, TT], BF16, "kb")
            o.memset(Fp[:], 0.0)

        hrs = [(b_, v_.rearrange("(r c p) s -> p r c s", r=2, p=128)) for b_, v_ in hfull]
        Yrs = [(b_, v_.rearrange("(hf c p) s -> p hf c s", hf=2, p=128)) for b_, v_ in scr["Y"]]

        def proj_fm(col0, n=128):
            ps = self.bank()
            o.mm(ps[0:n, :], [(win[:, k, col0:col0 + n], self.ht[:, k, :]) for k in range(NKC)])
            return ps

        def proj_tm(sub, col0, n):
            ps = self.bank()
            o.mm(ps[:, 0:n], [(self.ht[:, k, sub * 128:(sub + 1) * 128], win[:, k, col0:col0 + n]) for k in range(NKC)])
            return ps

        for t in range(NT):
            ts_ = slice(t * TT, (t + 1) * TT)
            hf, tl = t // (NT // 2), t % (NT // 2)
            ck = (tl * TT) // CH
            tsl = slice((tl * TT) % CH, (tl * TT) % CH + TT)
            hfb, hr = hrs[ck]
            Yb_, Yr = Yrs[ck]
            o.dma(self.ht[:], V(hfb, hr[:, hf, :, tsl]), "ht")
            if part == 1:
                pc = proj_fm(P1_DC)
                o.copy(T[0][:], pc[:], eng="act")
                pv = proj_fm(P1_DV)
                o.tt(scu[:, 2:TT + 2], T[0][:], pv[:], ALU.mult)
                o.ts(T[1][:], scu[:, 0:TT], cols[:, K_SCW:K_SCW + 1], None, ALU.mult, eng="pool")
                o.stt(T[1][:], scu[:, 1:TT + 1], cols[:, K_SCW + 1:K_SCW + 2], T[1][:], ALU.mult, ALU.add)
                o.stt(T[1][:], scu[:, 2:TT + 2], cols[:, K_SCW + 2:K_SCW + 3], T[1][:], ALU.mult, ALU.add)
                o.copy(scu[:, 0:2], scu[:, TT:TT + 2], eng="pool")
                pb = proj_fm(P1_DB)
                o.tt(T[2][:], T[1][:], pb[:], ALU.mult)
                self.group_rms(T[2][:], 64, cols[:, K_SCN:K_SCN + 1], Yt[:, 3, :], None)
                pl = proj_fm(P1_GLR, 16)
                o.copy(glr[0:16, :], pl[0:16, :], eng="act")
                PZ = self.bank()
                for s_ in range(4):
                    o.mm(PZ[:, s_ * 128:(s_ + 1) * 128], [(glr[0:16, s_ * 128:(s_ + 1) * 128], wup[:])])
                laf = la.ap.rearrange("p a b -> p (a b)")
                laf = V(la, laf)
                o.tt(laf, PZ[:], tm[:, M_BG4:M_BG4 + 512], ALU.add)
                o.act(laf, laf, AF.Exp, scale=-1.0)
                o.act(laf, laf, AF.Ln, bias=1.0)
                PV_ = self.bank()
                for s_ in range(4):
                    o.mm(PV_[:, s_ * 128:(s_ + 1) * 128],
                         [(self.ht[:, k, s_ * 128:(s_ + 1) * 128], win[:, k, P1_GV:P1_GV + 128]) for k in range(NKC)])
                o.copy(V(vtm, vtm.ap.rearrange("p a b -> p (a b)")), PV_[:], eng="act")
                pq = proj_fm(P1_GQ)
                o.copy(T[0][:], pq[:], eng="act")
                pk = proj_fm(P1_GK)
                o.copy(T[1][:], pk[:], eng="act")
                pgo = proj_fm(P1_GG)
                o.act(T[2][:], pgo[:], AF.Silu)
                PB = self.bank()
                for s_ in range(4):
                    o.mm(PB[:, s_ * 128:(s_ + 1) * 128], [(la[:, s_, :], self.triG[:])])
                eb, enb = T[5], T[4]
                o.act(eb[:], PB[:], AF.Exp)
                o.act(enb[:], PB[:], AF.Exp, scale=-1.0)
                o.stt(qdb[:], T[0][:], 0.125, eb[:], ALU.mult, ALU.mult)
                o.tt(kdb[:], T[1][:], enb[:], ALU.mult)
                PTr = self.bank()
                PTb = V(PTr, PTr.ap[:, 0:256].bitcast(BF16))
                for s_ in range(4):
                    o.transpose(V(PTr, PTb.ap[:, s_ * 128:(s_ + 1) * 128]), kdb[:, s_ * 128:(s_ + 1) * 128], self.ident_b[:])
                o.copy(V(kdt, kdt.ap.rearrange("p a b -> p (a b)")), PTb)
                for hh in range(2):
                    R = slice(hh * 64, (hh + 1) * 64)
                    PA = self.bank()
                    for s_ in range(4):
                        cs = slice(s_ * 128, (s_ + 1) * 128)
                        o.mm(PA[:, cs], [(kdb[R, cs], qdb[R, cs])])
                    o.tt(V(attm[hh], attm[hh].ap.rearrange("p a b -> p (a b)")), PA[:],
                         V(self.triU4, self.triU4.ap.rearrange("p a b -> p (a b)")), ALU.mult)
                PM = self.bank()
                for s_ in range(4):
                    for hh in range(2):
                        R = slice(hh * 64, (hh + 1) * 64)
                        o.mm(PM[R, s_ * 64:(s_ + 1) * 64], [(kdt[:, s_, R], vtm[:, s_, hh * 64:(hh + 1) * 64])])
                for s_ in range(4):
                    cs = slice(s_ * 128, (s_ + 1) * 128)
                    for hh in range(2):
                        R = slice(hh * 64, (hh + 1) * 64)
                        o.mm(acc[R, cs], [(vtm[:, s_, hh * 64:(hh + 1) * 64], attm[hh][:, s_, :]), (GSb[R, :], qdb[R, cs])])
                    eL = eb[:, s_ * 128 + 127:s_ * 128 + 128]
                    o.ts(GS32[:], GS32[:], eL, None, ALU.mult)
                    o.stt(GS32[:], PM[:, s_ * 64:(s_ + 1) * 64], eL, GS32[:], ALU.mult, ALU.add)
                    o.copy(GSb[:], GS32[:], eng="pool")
                o.copy(T[3][:], acc[:], eng="act")
                self.group_rms(T[3][:], 64, cols[:, K_GLAN:K_GLAN + 1], Yt[:, 0, :], T[4][:], post_mul=T[2][:])
                o.dma(V(Yb_, Yr[:, hf, 0, tsl]), Yt[:, 0, :], "yo")
                o.dma(V(Yb_, Yr[:, hf, 3, tsl]), Yt[:, 3, :], "yo2")
            else:
                for c in range(3):
                    pr = proj_fm(P2_CX + c * 128)
                    o.copy(raw[c][:, 3:TT + 3], pr[:], eng="act")
                    tc_ = T[0] if c % 2 == 0 else T[1]
                    o.ts(tc_[:], raw[c][:, 0:TT], cols[:, K_CW + c:K_CW + c + 1], None, ALU.mult, eng="pool")
                    for k in range(1, 4):
                        o.stt(tc_[:], raw[c][:, k:TT + k], cols[:, K_CW + 3 * k + c:K_CW + 3 * k + c + 1], tc_[:], ALU.mult, ALU.add)
                    o.copy(raw[c][:, 0:3], raw[c][:, TT:TT + 3], eng="pool")
                    bcol = cols[:, K_CB + c:K_CB + c + 1]
                    if c == 0:
                        o.act(xs32[:], tc_[:], AF.Silu, bias=bcol)
                        o.copy(xsb[:], xs32[:], eng="pool")
                    elif c == 1:
                        o.act(BT[:], tc_[:], AF.Silu, bias=bcol)
                    else:
                        o.act(CT[:], tc_[:], AF.Silu, bias=bcol)
                pz = proj_fm(P2_CZ)
                o.act(zs[:], pz[:], AF.Silu)
                PD = self.bank()
                for s_ in range(4):
                    o.mm(PD[:, s_ * 2:(s_ + 1) * 2],
                         [(self.ht[:, k, s_ * 128:(s_ + 1) * 128], win[:, k, P2_CDT:P2_CDT + 2]) for k in range(NKC)])
                dt4 = V(dt_tm, dt_tm.ap.rearrange("p a b -> p (a b)"))
                a4 = V(a_tm, a_tm.ap.rearrange("p a b -> p (a b)"))
                o.tt(dt4, PD[:, 0:8], tm[:, M_DTB4:M_DTB4 + 8], ALU.add)
                o.act(dt4, dt4, AF.Exp)
                o.act(dt4, dt4, AF.Ln, bias=1.0)
                o.stt(a4, dt4, -1.0, aneg4[:], ALU.mult, ALU.mult)
                PCS = self.bank()
                o.mm(PCS[:, 0:8], [(self.triU[:], a4)])
                o.mm(PCS[:, 8:16], [(self.ones_f[:], a4)])
                o.copy(acs4[:], PCS[:, 0:8])
                o.tt(dte4[:], PCS[:, 8:16], acs4[:], ALU.subtract)
                o.act(dte4[:], dte4[:], AF.Exp)
                o.act(eal4[:], PCS[:, 8:16], AF.Exp)
                PX = self.bank()
                PXb = V(PX, PX.ap[:, 0:256].bitcast(BF16))
                for s_ in range(4):
                    o.transpose(V(PX, PXb.ap[:, s_ * 128:(s_ + 1) * 128]), xsb[:, s_ * 128:(s_ + 1) * 128], self.ident_b[:])
                Xd8 = V(Xd4, Xd4.ap.rearrange("p a (h e) -> p (a h) e", h=2))
                Xdd8 = V(Xdd4, Xdd4.ap.rearrange("p a (h e) -> p (a h) e", h=2))
                o.tt(Xd8, V(PX, PXb.ap.rearrange("p (g e) -> p g e", e=64)),
                     V(dt_tm, dt4.ap.unsqueeze(2).broadcast_to([128, 8, 64])), ALU.mult)
                o.tt(Xdd8, Xd8, V(dte4, dte4.ap.unsqueeze(2).broadcast_to([128, 8, 64])), ALU.mult, eng="pool")
                PBt = self.bank()
                PBb = V(PBt, PBt.ap[:, 0:256].bitcast(BF16))
                for s_ in range(4):
                    o.transpose(V(PBt, PBb.ap[:, s_ * 128:(s_ + 1) * 128]), BT[:, s_ * 128:(s_ + 1) * 128], self.ident_b[:])
                o.copy(V(Btm4, Btm4.ap.rearrange("p a b -> p (a b)")), PBb)
                PSC = self.bank()
                for s_ in range(4):
                    cs = slice(s_ * 128, (s_ + 1) * 128)
                    o.mm(PSC[:, cs], [(BT[:, cs], CT[:, cs])])
                for h in range(2):
                    ah = V(a_tm, a_tm.ap[:, :, h:h + 1].broadcast_to([128, 4, 128]))
                    o.tt(lseg4[h][:], self.triS4[:], ah, ALU.mult, eng="pool")
                    o.tt(abc4[h][:], self.ones4[:], ah, ALU.mult, eng="pool")
                    PSEG = self.bank()
                    PAB = self.bank()
                    for s_ in range(4):
                        cs = slice(s_ * 128, (s_ + 1) * 128)
                        o.mm(PSEG[:, cs], [(lseg4[h][:, s_, :], self.triU[:]), (self.ident_f[:], self.maskb[:])])
                        o.mm(PAB[:, cs], [(abc4[h][:, s_, :], self.triU[:])])
                    o.act(LT4[h][:], PSEG[:], AF.Exp)
                    o.tt(WT4[h][:], PSC[:], LT4[h][:], ALU.mult)
                    o.act(Ecs4[h][:], PAB[:], AF.Exp)
                    o.tt(CsT4[h][:], CT[:], Ecs4[h][:], ALU.mult)
                for s_ in range(4):
                    cs = slice(s_ * 128, (s_ + 1) * 128)
                    for h in range(2):
                        hc = slice(h * 64, (h + 1) * 64)
                        o.mm(acc[hc, cs], [(Xd4[:, s_, hc], WT4[h][:, cs]), (STb[h][:], CsT4[h][:, cs])])
                        pst = self.bank()
                        o.mm(pst[:, 0:64], [(Btm4[:, s_, :], Xdd4[:, s_, hc])])
                        o.stt(ST32[h][:], ST32[h][:], eal4[:, s_ * 2 + h:s_ * 2 + h + 1], pst[:, 0:64], ALU.mult, ALU.add)
                        o.copy(STb[h][:], ST32[h][:], eng="pool")
                o.stt(T[2][:], xs32[:], cols[:, K_D:K_D + 1], acc[:], ALU.mult, ALU.add)
                o.tt(T[3][:], T[2][:], zs[:], ALU.mult)
                self.group_rms(T[3][:], 128, cols[:, K_SN:K_SN + 1], Yt[:, 2, :], None)
                o.dma(V(Yb_, Yr[:, hf, 2, tsl]), Yt[:, 2, :], "yo")
                pq = proj_fm(P2_BQ)
                o.copy(T[0][:], pq[:], eng="act")
                self.group_rms(T[0][:], 64, cols[:, K_FQN:K_FQN + 1], qb[:], T[4][:], post_scale=0.125)
                o.dma(V(scr["Q"], scr["Q"].ap[:, ts_]), qb[:], "qo")
                pk = proj_fm(P2_BK)
                o.copy(T[1][:], pk[:], eng="act")
                self.group_rms(T[1][:], 64, cols[:, K_FKN:K_FKN + 1], kb[:], None)
                o.dma(V(scr["K"], scr["K"].ap[:, ts_]), kb[:], "ko")
                for s_ in range(4):
                    kt = t * 4 + s_
                    pv = proj_tm(s_, P2_BV, 130)
                    for h in range(2):
                        o.copy(self.VA[:, kt, h, 0:64], pv[:, h * 64:(h + 1) * 64], eng="act" if h % 2 == 0 else "dve")
                    o.tt(nl[:, s_, :], pv[:, 128:130], tm[:, M_FB:M_FB + 2], ALU.add)
                    o.act(nl[:, s_, :], nl[:, s_, :], AF.Exp, scale=-1.0)
                    o.act(nl[:, s_, :], nl[:, s_, :], AF.Ln, bias=1.0)
                    for g in range(3):
                        o.copy(nl3[:, s_, g * 32:g * 32 + 2], nl[:, s_, :], eng="pool")
                    pg_ = self.bank()
                    o.mm(pg_[:, 0:2], [(self.ones_f[:], nl[:, s2, :]) for s2 in range(s_)] + [(self.triU[:], nl[:, s_, :])])
                    o.tt(sm[3][:], pg_[:, 0:2], gcar_bc[:], ALU.add)
                    o.ts(self.EB[:, kt, :], sm[3][:], -FOX_SHIFT, None, ALU.add)
                ptot = self.bank()
                o.mm(ptot[:, 0:2], [(self.ones_f[:], nl[:, s2, :]) for s2 in range(4)])
                o.tt(gcar_bc[:], gcar_bc[:], ptot[:, 0:2], ALU.add)
                pgt = self.bank()
                for s_ in range(4):
                    cs = slice(s_ * 128, (s_ + 1) * 128)
                    o.mm(pgt[0:96, cs], [(nl3[:, s2, :], self.ones_f[:]) for s2 in range(s_)] + [(nl3[:, s_, :], self.triU[:])])
                GT = F1[0]
                o.ts(GT[:], pgt[0:96, :], gcar_T[:, 0:1], None, ALU.add)
                o.copy(gcar_T[:], GT[:, TT - 1:TT])
                hi, mid, lo = F1b[0], F1b[1], F1b[2]
                r1, r2 = F1[1], F1[2]
                o.ts(hi[:], GT[:], -1.0, None, ALU.mult)
                o.stt(r1[:], GT[:], -1.0, hi[:], ALU.mult, ALU.subtract)
                o.copy(mid[:], r1[:])
                o.tt(r2[:], r1[:], mid[:], ALU.subtract)
                o.copy(lo[:], r2[:])
                o.copy(Fp[0:4, :], hi[0:4, :])
                o.copy(Fp[32:36, :], mid[32:36, :])
                o.copy(Fp[64:68, :], lo[64:68, :])
                o.dma(V(scr["FP"], scr["FP"].ap[:, ts_]), Fp[:], "fpo")

    def mix_b(self, cols_ap, scr, y_gather=None):
        o = self.o
        A = self.alloc
        NT = SEQ // TT
        self.S.barrier()
        self.poff = self.mixb_base
        self.ring = [2, 3, 4, 5, 6, 7]
        oacc = [self.psum[0], self.psum[1]]
        cols = A([128, NCOLT], F32, "cols_b")
        o.dma(cols[:], V(self.S.dram(cols_ap, "colsd2"), cols_ap), "cols")
        KTh = [A([67, SEQ], BF16, "KTh%d" % h) for h in range(2)]
        for h in range(2):
            o.memset(KTh[h][64:67, :], 1.0)
            for q4 in range(4):
                qs = slice(q4 * (SEQ // 4), (q4 + 1) * (SEQ // 4))
                o.dma(KTh[h][0:64, qs], V(scr["K"], scr["K"].ap[h * 64:(h + 1) * 64, qs]), "kth%d" % h)
        Qh = [[A([67, TT], BF16, "Qh%d_%d" % (i, h)) for h in range(2)] for i in range(2)]
        FPr = scr["FP"].ap.rearrange("(g x) s -> x g s", x=32)
        PT = [A([128, TT], BF16, "PT%d" % i) for i in range(4)]
        Oa = A([65, TT], F32, "Oa")
        Osq = A([65, TT], BF16, "Osq")
        rs = A([64, TT], F32, "rs_b")
        Yb = [A([64, TT], BF16, "Yb%d" % h) for h in range(4)]
        Yrows = [(b_, v_.rearrange("(hf c g p) s -> p hf c g s", hf=2, c=4, g=2)) for b_, v_ in scr["Y"]]
        LAG = 2
        npt = 0
        nyb = 0

        def load_qf(qt):
            ts2 = slice(qt * TT, (qt + 1) * TT)
            for h in range(2):
                o.dma(Qh[qt % 2][h][0:64, :], V(scr["Q"], scr["Q"].ap[h * 64:(h + 1) * 64, ts2]), "qi%d_%d" % (qt % 2, h))
                o.dma(Qh[qt % 2][h][64:67, :], V(scr["FP"], FPr[h, :, ts2]), "qi%d_%d" % (qt % 2, h))
        load_qf(0)
        for qt in range(NT):
            hf, tl = qt // (NT // 2), qt % (NT // 2)
            tsl = slice((tl * TT) % CH, (tl * TT) % CH + TT)
            Ybuf, Yrow = Yrows[(tl * TT) // CH]
            if qt + 1 < NT:
                load_qf(qt + 1)
            Qc = Qh[qt % 2]
            nk = (qt + 1) * 4
            blocks = [(h, kt) for h in range(2) for kt in range(nk)]
            pend = []

            def stage2(item):
                nonlocal nyb
                h, kt, col0, pt = item
                oa = oacc[h % 2]
                o.mm1(oa[0:65, col0:TT], self.VA[:, kt, h, 0:65], pt[:, col0:TT], kt == 0, kt == nk - 1)
                if kt == nk - 1:
                    o.copy(Oa[:], oa[0:65, :], eng="act")
                    o.act(Osq[:], Oa[:], AF.Square)
                    pd = self.bank()
                    o.mm(pd[0:64, :], [(self.wden[:], Osq[:])])
                    o.act(rs[:], pd[0:64, :], AF.Sqrt)
                    o.recip(rs[:], rs[:])
                    yb = Yb[nyb % 4]
                    nyb += 1
                    o.stt(yb[:], Oa[0:64, :], cols[0:64, K_FON + h:K_FON + h + 1], rs[:], ALU.mult, ALU.mult)
                    o.dma(V(Ybuf, Yrow[:, hf, 1, h, tsl]), yb[:], "ybo%d" % (nyb % 4))

            for (h, kt) in blocks:
                R = slice(h * 64, (h + 1) * 64)
                c = kt - qt * 4
                col0 = max(c, 0) * 128
                sT = self.bank()
                o.mm(sT[:, col0:TT], [(KTh[h][:, kt * 128:(kt + 1) * 128], Qc[h][:, col0:TT])])
                if c >= 0:
                    o.tt(sT[:, col0:col0 + 128], sT[:, col0:col0 + 128], self.maskb[:], ALU.add)
                pt = PT[npt % 4]
                npt += 1
                o.act(pt[:, col0:TT], sT[:, col0:TT], AF.Exp, bias=self.EB[:, kt, h:h + 1])
                pend.append((h, kt, col0, pt))
                if len(pend) > LAG:
                    stage2(pend.pop(0))
            while pend:
                stage2(pend.pop(0))
            if y_gather is not None and hf == 1 and (tl * TT) % CH + TT == CH:
                k_ = (tl * TT) // CH
                self.all_gather(scr["Y"][k_][0], y_gather[k_][0])

    def outproj_phase(self, x_in, x_out, yfull, cols_ap, tm_ap, w_out_ap):
        o = self.o
        A = self.alloc
        self.phase_begin()
        self.ring = list(range(8))
        wo = A([128, NKC, D_MODEL], BF16, "wo")
        self.load_weight(wo, w_out_ap, NKC, D_MODEL, "wA")
        self.load_tables(cols_ap, tm_ap)
        cols = self.cols
        xt = A([128, NKC, TT], F32, "xt_o")
        Y0 = A([128, NKC, TT], BF16, "Y0")
        Y1 = A([128, NKC, TT], BF16, "Y1")
        Ys = A([128, NKC, TT], BF16, "Ys")
        xin_r = x_in.ap.rearrange("(c p) s -> p c s", p=128)
        xout_r = x_out.ap.rearrange("(c p) s -> p c s", p=128)
        yrs = [(b_, v_.rearrange("(r hf c p) s -> p hf r c s", r=2, hf=2, p=128)) for b_, v_ in yfull]
        for t in range(HALF // TT):
            ts_ = slice(t * TT, (t + 1) * TT)
            o.dma(xt[:], V(x_in, xin_r[:, :, ts_]), "xt")
            yfb, yr = yrs[(t * TT) // CH]
            tsc = slice((t * TT) % CH, (t * TT) % CH + TT)
            for rr in range(2):
                o.dma(Y0[:, rr * 4:(rr + 1) * 4, :], V(yfb, yr[:, 0, rr, :, tsc]), "y0")
                o.dma(Y1[:, rr * 4:(rr + 1) * 4, :], V(yfb, yr[:, 1, rr, :, tsc]), "y1")
            o.ts(Ys[:], Y0[:], cols[:, K_SEL:K_SEL + 1], None, ALU.mult)
            o.stt(Ys[:], Y1[:], cols[:, K_SEL + 1:K_SEL + 2], Ys[:], ALU.mult, ALU.add)
            for oc in range(NKC):
                py = self.bank()
                o.mm(py[:], [(wo[:, k, oc * 128:(oc + 1) * 128], Ys[:, k, :]) for k in range(NKC)])
                o.tt(xt[:, oc, :], xt[:, oc, :], py[:], ALU.add)
            o.dma(V(x_out, xout_r[:, :, ts_]), xt[:], "xo")


def build_program(depth, debug=None):
    nc = bass.Bass("TRN2", target_bir_lowering=False)
    stack = ExitStack()

    def din(name, shape, dt=F32):
        return nc.dram_tensor(name, list(shape), dt, kind="ExternalInput").ap()
    xT = din("xT", [D_MODEL, HALF])
    w = {}
    for l in range(depth):
        for f in (1, 2):
            w["g%d_%d" % (f, l)] = din("ffn%d_w_gate_%d" % (f, l), [D_MODEL, D_FF])
            w["u%d_%d" % (f, l)] = din("ffn%d_w_up_%d" % (f, l), [D_MODEL, D_FF])
            w["d%d_%d" % (f, l)] = din("ffn%d_w_down_%d" % (f, l), [D_FF, D_MODEL])
        w["win1_%d" % l] = din("w_in1_%d" % l, [D_MODEL, P1_N])
        w["win2_%d" % l] = din("w_in2_%d" % l, [D_MODEL, P2_N])
        w["wout_%d" % l] = din("w_out_%d" % l, [D_MODEL, D_MODEL])
        w["wup_%d" % l] = din("gla_wup_%d" % l, [16, 128])
        w["cols_%d" % l] = din("cols_%d" % l, [128, NCOLT])
        w["tm_%d" % l] = din("tm_%d" % l, [128, NTM])
    outT = nc.dram_tensor("outT", [D_MODEL, HALF], F32, kind="ExternalOutput").ap()
    with stack:
        B = Builder(nc, stack, HALF)
        S = B.S

        def scratch(name, shape, dt):
            return S.dram(nc.dram_tensor(name, list(shape), dt).ap(), name)
        xa = scratch("xa", [D_MODEL, HALF], F32)
        xb = scratch("xb", [D_MODEL, HALF], F32)
        NCH = HALF // CH

        def cc_pair(name):
            srcs, dsts = [], []
            for k in range(NCH):
                sb_ = scratch("%s_s%d" % (name, k), [128, 8 * CH], BF16)
                db_ = scratch("%s_d%d" % (name, k), [256, 8 * CH], BF16)
                srcs.append((sb_, sb_.ap.rearrange("p (a s) -> (p a) s", s=CH)))
                dsts.append((db_, db_.ap.rearrange("p (a s) -> (p a) s", s=CH)))
            return srcs, dsts
        hsrc, hfull = cc_pair("h")
        ysrc, yfull = cc_pair("y")
        scr = {"Y": ysrc,
               "Q": scratch("scrQ", [128, SEQ], BF16),
               "K": scratch("scrK", [128, SEQ], BF16),
               "FP": scratch("scrFP", [96, SEQ], BF16)}
        cur = S.dram(xT, "xT")
        xout = S.dram(outT, "outT")
        for l in range(depth):
            last = l == depth - 1
            ct = (w["cols_%d" % l], w["tm_%d" % l])
            B.ffn_phase(cur, xa, ct[0], ct[1], K_FFN1, w["g1_%d" % l], w["u1_%d" % l], w["d1_%d" % l], h_out=hsrc, h_gather=hfull)
            B.mix_a(1, hfull, ct[0], ct[1], w["win1_%d" % l], w["wup_%d" % l], scr)
            B.mix_a(2, hfull, ct[0], ct[1], w["win2_%d" % l], w["wup_%d" % l], scr)
            B.mix_b(ct[0], scr, y_gather=yfull)
            B.outproj_phase(xa, xb, yfull, ct[0], ct[1], w["wout_%d" % l])
            dst = xout if last else xa
            B.ffn_phase(xb, dst, ct[0], ct[1], K_FFN2, w["g2_%d" % l], w["u2_%d" % l], w["d2_%d" % l])
            cur = dst
        S.barrier()
        S.emit()
        build_program.nsem = S.nsem
    return nc


def make_inputs(inp, depth, r):
    d = {}
    for l in range(depth):
        ffn = {1: (inp["ffn1_w_gate"], inp["ffn1_w_up"], inp["ffn1_w_down"]),
               2: (inp["ffn2_w_gate"], inp["ffn2_w_up"], inp["ffn2_w_down"])}
        for f in (1, 2):
            d["ffn%d_w_gate_%d" % (f, l)] = np.ascontiguousarray(ffn[f][0][l], dtype=np.float32)
            d["ffn%d_w_up_%d" % (f, l)] = np.ascontiguousarray(ffn[f][1][l], dtype=np.float32)
            d["ffn%d_w_down_%d" % (f, l)] = np.ascontiguousarray(ffn[f][2][l], dtype=np.float32)
        p1, p2 = host_win(inp, l, r)
        d["w_in1_%d" % l] = p1
        d["w_in2_%d" % l] = p2
        d["w_out_%d" % l] = host_wout(inp, l)
        d["gla_wup_%d" % l] = np.ascontiguousarray(np.asarray(inp["gla_w_gate_up"][l], np.float32)[:, r * 128:(r + 1) * 128])
        cols, tm = host_tables(inp, l, r)
        d["cols_%d" % l] = cols
        d["tm_%d" % l] = tm
    return d


_PROG_CACHE = {}


def kernel(**inputs):
    x = np.asarray(inputs["x"], np.float32)
    Bsz, L, D = x.shape
    depth = inputs["w_in"].shape[0]
    assert (Bsz, L, D) == (4, SEQ, D_MODEL)
    if depth not in _PROG_CACHE:
        _PROG_CACHE[depth] = build_program(depth)
    nc = _PROG_CACHE[depth]
    shared = [make_inputs(inputs, depth, r) for r in range(2)]
    in_maps = []
    for c in range(8):
        b, r = c // 2, c % 2
        m = dict(shared[r])
        m["xT"] = np.ascontiguousarray(x[b, r * HALF:(r + 1) * HALF].T)
        in_maps.append(m)
    res = run_bass_kernel_spmd(nc, in_maps, core_ids=list(range(8)))
    out = np.empty((Bsz, L, D), np.float32)
    for c in range(8):
        b, r = c // 2, c % 2
        out[b, r * HALF:(r + 1) * HALF] = res.results[c]["outT"].T
    return out
```

```python
import numpy as np
from contextlib import ExitStack
import concourse.bass as bass
import concourse.mybir as mybir
from concourse.bass_utils import run_bass_kernel_spmd

F32 = mybir.dt.float32
BF16 = mybir.dt.bfloat16
ALU = mybir.AluOpType
AF = mybir.ActivationFunctionType
AX = mybir.AxisListType

SEM_CAP = 30000


class Buf:
    __slots__ = ("ap", "name", "lw", "rd")

    def __init__(self, ap, name):
        self.ap = ap
        self.name = name
        self.lw = None
        self.rd = {}

    def __getitem__(self, k):
        return V(self, self.ap[k])


class V:
    __slots__ = ("b", "ap")

    def __init__(self, b, ap):
        self.b = b
        self.ap = ap


def _bufs(*vs):
    return [v.b for v in vs if isinstance(v, V)]


def _ap(v):
    return v.ap if isinstance(v, V) else v


class Sched:
    ENGS = ("pe", "act", "dve", "pool", "sp")

    def __init__(self, nc, stack):
        self.nc = nc
        self.stack = stack
        self.streams = {e: [] for e in self.ENGS}
        self.dom = {e: [] for e in self.ENGS}
        self.waited = {e: {} for e in self.ENGS}
        self.nbuf = 0

    def sb(self, shape, dtype, name=None):
        self.nbuf += 1
        name = "%s_%d" % (name or "sb", self.nbuf)
        t = self.stack.enter_context(self.nc.sbuf_tensor(name, list(shape), dtype))
        return Buf(t, name)

    def ps(self, shape, dtype=F32, name=None):
        self.nbuf += 1
        name = "%s_%d" % (name or "ps", self.nbuf)
        t = self.stack.enter_context(self.nc.psum_tensor(name, list(shape), dtype))
        return Buf(t, name)

    def dram(self, ap, name):
        return Buf(ap, name)

    def _deps(self, eng, reads, writes):
        deps = {}

        def add(tok):
            if tok is None:
                return
            d, s = tok
            if d == "pe" and eng == "pe":
                return
            if deps.get(d, -1) < s:
                deps[d] = s

        for b in reads:
            add(b.lw)
        for b in writes:
            add(b.lw)
            for d, s in b.rd.items():
                add((d, s))
        w = self.waited[eng]
        out = []
        for d, s in deps.items():
            if w.get(d, -1) >= s:
                continue
            w[d] = s
            out.append((d, s))
        return out

    def op(self, eng, fn, reads=(), writes=(), dma=None):
        deps = self._deps(eng, reads, writes)
        dom = eng if dma is None else "dma:" + dma
        if dom not in self.dom:
            self.dom[dom] = []
        rec = {"fn": fn, "deps": deps, "sig": dma is not None, "dom": dom,
               "seq": len(self.dom[dom])}
        self.dom[dom].append(rec)
        self.streams[eng].append(rec)
        tok = (dom, rec["seq"])
        for b in reads:
            if b.rd.get(dom, -1) < rec["seq"]:
                b.rd[dom] = rec["seq"]
        for b in writes:
            b.lw = tok
            b.rd = {}
        return tok

    def barrier(self):
        last = {d: len(r) - 1 for d, r in self.dom.items() if r}
        for e in self.ENGS:
            deps = []
            for d, s in last.items():
                if self.waited[e].get(d, -1) >= s:
                    continue
                self.waited[e][d] = s
                deps.append((d, s))
            rec = {"fn": (lambda eng: eng.nop()), "deps": deps, "sig": False, "dom": e,
                   "seq": len(self.dom[e])}
            self.dom[e].append(rec)
            self.streams[e].append(rec)

    def emit(self):
        nc = self.nc
        for e in self.ENGS:
            for rec in self.streams[e]:
                for d, s in rec["deps"]:
                    self.dom[d][s]["sig"] = True
        semtab = {}
        for d, recs in self.dom.items():
            step = 16 if (d.startswith("dma:") and not d.startswith("dma:cc")) else 1
            cap = SEM_CAP // step
            c = 0
            for rec in recs:
                if rec["sig"]:
                    rec["sv"] = (c // cap, (c % cap + 1) * step)
                    c += 1
            nep = (c + cap - 1) // cap
            semtab[d] = [self.stack.enter_context(nc.semaphore("s_%s_%d" % (d.replace(":", "_"), i)))
                         for i in range(max(nep, 0))]
        self.nsem = sum(len(v) for v in semtab.values())
        block = self.stack.enter_context(nc.Block())

        def run(engname):
            def body(eng):
                for rec in self.streams[engname]:
                    for d, s in rec["deps"]:
                        ep, val = self.dom[d][s]["sv"]
                        eng.wait_ge(semtab[d][ep], val)
                    ins = rec["fn"](eng)
                    if rec["sig"]:
                        ep, val = rec["sv"]
                        step = 16 if (rec["dom"].startswith("dma:") and not rec["dom"].startswith("dma:cc")) else 1
                        ins.then_inc(semtab[rec["dom"]][ep], step)
            return body

        block.tensor(run("pe"))
        block.scalar(run("act"))
        block.vector(run("dve"))
        block.gpsimd(run("pool"))
        block.sync(run("sp"))


class Ops:
    def __init__(self, S):
        self.S = S

    def dma(self, out, in_, chan, eng="sp"):
        return self.S.op(eng, lambda e: e.dma_start(out=out.ap, in_=in_.ap), reads=[in_.b], writes=[out.b], dma=chan)

    def act(self, out, in_, func, bias=0.0, scale=1.0, eng="act"):
        return self.S.op(eng, lambda e: e.activation(out=out.ap, in_=in_.ap, func=func, bias=_ap(bias), scale=_ap(scale)),
                         reads=_bufs(in_, bias, scale), writes=[out.b])

    def tt(self, out, in0, in1, op, eng="dve"):
        return self.S.op(eng, lambda e: e.tensor_tensor(out=out.ap, in0=in0.ap, in1=in1.ap, op=op),
                         reads=_bufs(in0, in1), writes=[out.b])

    def ts(self, out, in0, s1, s2, op0, op1=None, eng="dve"):
        if op1 is None:
            return self.S.op(eng, lambda e: e.tensor_scalar(out=out.ap, in0=in0.ap, scalar1=_ap(s1), scalar2=None, op0=op0),
                             reads=_bufs(in0, s1), writes=[out.b])
        return self.S.op(eng, lambda e: e.tensor_scalar(out=out.ap, in0=in0.ap, scalar1=_ap(s1), scalar2=_ap(s2), op0=op0, op1=op1),
                         reads=_bufs(in0, s1, s2), writes=[out.b])

    def stt(self, out, in0, scalar, in1, op0, op1, eng="dve"):
        eng = "dve"
        return self.S.op(eng, lambda e: e.scalar_tensor_tensor(out=out.ap, in0=in0.ap, scalar=_ap(scalar), in1=in1.ap, op0=op0, op1=op1),
                         reads=_bufs(in0, scalar, in1), writes=[out.b])

    def copy(self, out, in_, eng="dve"):
        if eng == "act":
            return self.act(out, in_, AF.Copy)
        return self.S.op(eng, lambda e: e.tensor_copy(out=out.ap, in_=in_.ap), reads=[in_.b], writes=[out.b])

    def recip(self, out, in_):
        return self.S.op("dve", lambda e: e.reciprocal(out=out.ap, in_=in_.ap), reads=[in_.b], writes=[out.b])

    def memset(self, out, val, eng="pool"):
        return self.S.op(eng, lambda e: e.memset(out.ap, val), writes=[out.b])

    def aselect(self, out, in_, pattern, cmp, fill, base, cm):
        return self.S.op("pool", lambda e: e.affine_select(out=out.ap, in_=in_.ap, pattern=pattern, compare_op=cmp,
                                                           fill=fill, base=base, channel_multiplier=cm),
                         reads=[in_.b], writes=[out.b])

    def mm(self, out, pairs, extra_reads=()):
        n = len(pairs)

        def fn(e):
            ins = None
            for i, (l, r) in enumerate(pairs):
                ins = e.matmul(out.ap, lhsT=l.ap, rhs=r.ap, start=(i == 0), stop=(i == n - 1))
            return ins
        rd = []
        for l, r in pairs:
            rd.append(l.b)
            rd.append(r.b)
        rd.extend(extra_reads)
        return self.S.op("pe", fn, reads=rd, writes=[out.b])

    def mm1(self, out, l, r, start, stop):
        return self.S.op("pe", lambda e: e.matmul(out.ap, lhsT=l.ap, rhs=r.ap, start=start, stop=stop),
                         reads=[l.b, r.b], writes=[out.b])

    def transpose(self, out, in_, ident):
        return self.S.op("pe", lambda e: e.transpose(out.ap, in_.ap, ident.ap), reads=[in_.b, ident.b], writes=[out.b])


D_MODEL = 1024
D_FF = 2816
NKC = D_MODEL // 128
NFC = D_FF // 128
IN_COLS = 3608
TT = 512
EPS = 1e-6
FOX_SHIFT = 10.0
NEG = -30000.0
SEQ = 8192
HALF = SEQ // 2
CH = 1024
GROUPS = [[0, 1], [2, 3], [4, 5], [6, 7]]

C_AQ, C_AK, C_AV, C_AG, C_ALR = 0, 256, 512, 768, 1024
C_BQ, C_BK, C_BV, C_BF = 1040, 1296, 1552, 1808
C_CZ, C_CX, C_CDT = 1812, 2068, 2836
C_DB, C_DC, C_DV = 2840, 3096, 3352
P1_GQ, P1_GK, P1_GV, P1_GG, P1_GLR, P1_DB, P1_DC, P1_DV, P1_N = 0, 128, 256, 384, 512, 528, 656, 784, 912
P2_BQ, P2_BK, P2_BV, P2_CZ, P2_CX, P2_CB, P2_CC, P2_CDT, P2_N = 0, 128, 256, 386, 514, 642, 770, 898, 900

K_FFN1, K_MIX, K_FFN2 = 0, 8, 16
K_GLAN = 24
K_FQN, K_FKN = 25, 26
K_FON = 27
K_CB = 29
K_CW = 32
K_D = 44
K_SN = 45
K_SCW = 46
K_SCN = 49
K_SEL = 50
NCOLT = 52
M_ALOG, M_FB, M_DTB, M_BG = 0, 2, 4, 6
M_ALOG4, M_DTB4, M_BG4 = 134, 142, 150
NTM = 150 + 512


def host_tables(inp, l, r):
    f32 = np.float32
    cols = np.zeros((128, NCOLT), f32)
    hs = slice(r * 128, (r + 1) * 128)

    def put(k, vec):
        v = np.asarray(vec, f32).reshape(-1, 128)
        for c in range(v.shape[0]):
            cols[:, k + c] = v[c]
    put(K_FFN1, inp["ffn1_norm"][l]); put(K_MIX, inp["mix_norm"][l]); put(K_FFN2, inp["ffn2_norm"][l])
    put(K_GLAN, inp["gla_norm"][l][hs])
    put(K_FQN, np.tile(inp["fox_q_norm"][l], 2)); put(K_FKN, np.tile(inp["fox_k_norm"][l], 2))
    fo = np.asarray(inp["fox_out_norm"][l], f32).reshape(4, 64)
    for h in range(2):
        cols[0:64, K_FON + h] = fo[2 * r + h]
    cb = np.asarray(inp["ssm_conv_b"][l], f32)
    cw = np.asarray(inp["ssm_conv_w"][l], f32)
    chs = [slice(r * 128, (r + 1) * 128), slice(256 + r * 128, 256 + (r + 1) * 128), slice(512 + r * 128, 512 + (r + 1) * 128)]
    for c in range(3):
        cols[:, K_CB + c] = cb[chs[c]]
        for k in range(4):
            cols[:, K_CW + 3 * k + c] = cw[k][chs[c]]
    put(K_D, np.repeat(np.asarray(inp["ssm_D"][l], f32)[2 * r:2 * r + 2], 64))
    put(K_SN, inp["ssm_norm"][l][hs])
    for k in range(3):
        put(K_SCW + k, inp["sc_conv_w"][l][k][hs])
    put(K_SCN, inp["sc_out_norm"][l][hs])
    cols[:, K_SEL + r] = 1.0
    tm = np.zeros((128, NTM), f32)
    tm[:, M_ALOG:M_ALOG + 2] = np.asarray(inp["ssm_A_log"][l], f32)[None, 2 * r:2 * r + 2]
    tm[:, M_FB:M_FB + 2] = np.asarray(inp["fox_b_forget"][l], f32)[None, 2 * r:2 * r + 2]
    tm[:, M_DTB:M_DTB + 2] = np.asarray(inp["ssm_dt_bias"][l], f32)[None, 2 * r:2 * r + 2]
    tm[:, M_BG:M_BG + 128] = np.asarray(inp["gla_b_gate"][l], f32)[None, hs]
    tm[:, M_ALOG4:M_ALOG4 + 8] = np.tile(tm[:, M_ALOG:M_ALOG + 2], (1, 4))
    tm[:, M_DTB4:M_DTB4 + 8] = np.tile(tm[:, M_DTB:M_DTB + 2], (1, 4))
    tm[:, M_BG4:M_BG4 + 512] = np.tile(tm[:, M_BG:M_BG + 128], (1, 4))
    return cols, tm


def host_win(inp, l, r):
    w = np.asarray(inp["w_in"][l], np.float32)
    h = lambda c0: w[:, c0 + r * 128: c0 + (r + 1) * 128]
    p1 = np.concatenate([h(C_AQ), h(C_AK), h(C_AV), h(C_AG), w[:, C_ALR:C_ALR + 16], h(C_DB), h(C_DC), h(C_DV)], axis=1)
    p2 = np.concatenate([h(C_BQ), h(C_BK), h(C_BV), w[:, C_BF + 2 * r:C_BF + 2 * r + 2], h(C_CZ),
                         h(C_CX), h(C_CX + 256), h(C_CX + 512), w[:, C_CDT + 2 * r:C_CDT + 2 * r + 2]], axis=1)
    assert p1.shape[1] == P1_N and p2.shape[1] == P2_N
    return np.ascontiguousarray(p1), np.ascontiguousarray(p2)


def host_wout(inp, l):
    w = np.asarray(inp["w_out"][l], np.float32)
    rows = []
    for rr in range(2):
        for blk in range(4):
            rows.append(w[blk * 256 + rr * 128: blk * 256 + (rr + 1) * 128])
    return np.ascontiguousarray(np.concatenate(rows, axis=0))


import os
DBG = set(os.environ.get("MIXDBG", "sc,gla,ssd,fox,fox2,fox3,b").split(","))
ARENA_UNITS = 212000 // 2


class Builder:
    def __init__(self, nc, stack, S_tok):
        self.nc = nc
        self.S_tok = S_tok
        self.S = Sched(nc, stack)
        self.o = Ops(self.S)
        S = self.S
        self.arena = stack.enter_context(nc.sbuf_tensor("arena", [128, ARENA_UNITS], BF16))
        self.poff = 0
        self.nb = 0
        self.psum = [S.ps([128, TT], F32, "bank%d" % i) for i in range(8)]
        self.pi = 0
        self.ring = list(range(8))
        self.consts()
        self.pbase = self.poff

    def alloc(self, shape, dtype, name):
        n = 1
        for d in shape[1:]:
            n *= d
        units = n * (2 if dtype == F32 else 1)
        units = (units + 15) // 16 * 16
        assert self.poff + units <= ARENA_UNITS, ("SBUF arena overflow", name, self.poff, units)
        ap = self.arena[:, self.poff:self.poff + units]
        self.poff += units
        if dtype == F32:
            ap = ap.bitcast(F32)
            ap = ap[:, 0:n]
        else:
            ap = ap[:, 0:n]
        if len(shape) == 3:
            ap = ap.rearrange("p (a b) -> p a b", a=shape[1])
        elif len(shape) == 4:
            ap = ap.rearrange("p (a b c) -> p a b c", a=shape[1], b=shape[2])
        if shape[0] < 128:
            ap = ap[0:shape[0]]
        self.nb += 1
        return Buf(ap, "%s_%d" % (name, self.nb))

    def phase_begin(self):
        self.S.barrier()
        self.poff = self.pbase

    def bank(self):
        b = self.psum[self.ring[self.pi % len(self.ring)]]
        self.pi += 1
        return b

    def consts(self):
        o = self.o
        A = self.alloc
        self.ones_mean = {}
        for blk in (1024, 128, 64):
            t = A([128, 128], BF16, "ones%d" % blk)
            o.memset(t[:], 1.0 / blk)
            if blk == 64:
                o.memset(t[0:64, 64:128], 0.0)
                o.memset(t[64:128, 0:64], 0.0)
            self.ones_mean[blk] = t
        self.eps_col = A([128, 1], F32, "epscol")
        o.memset(self.eps_col[:], EPS)
        self.ident_b = A([128, 128], BF16, "identb")
        o.memset(self.ident_b[:], 1.0)
        o.aselect(self.ident_b[:], self.ident_b[:], [[-1, 128]], ALU.is_equal, 0.0, 0, 1)
        self.ident_f = A([128, 128], F32, "identf")
        o.memset(self.ident_f[:], 1.0)
        o.aselect(self.ident_f[:], self.ident_f[:], [[-1, 128]], ALU.is_equal, 0.0, 0, 1)
        self.triU = A([128, 128], F32, "triU")
        o.memset(self.triU[:], 1.0)
        o.aselect(self.triU[:], self.triU[:], [[1, 128]], ALU.is_ge, 0.0, 0, -1)
        self.triU_b = A([128, 128], BF16, "triUb")
        o.copy(self.triU_b[:], self.triU[:], eng="pool")
        self.triG = A([128, 128], F32, "triG")
        o.memset(self.triG[:], -1.0 / 16.0)
        o.aselect(self.triG[:], self.triG[:], [[1, 128]], ALU.is_ge, 0.0, 0, -1)
        self.triS = A([128, 128], F32, "triS")
        o.memset(self.triS[:], 1.0)
        o.aselect(self.triS[:], self.triS[:], [[-1, 128]], ALU.is_gt, 0.0, 0, 1)
        self.maskb = A([128, 128], F32, "maskb")
        o.memset(self.maskb[:], 0.0)
        o.aselect(self.maskb[:], self.maskb[:], [[1, 128]], ALU.is_ge, NEG, 0, -1)
        self.ones_f = A([128, 128], F32, "onesf")
        o.memset(self.ones_f[:], 1.0)
        self.triU4 = A([128, 4, 128], F32, "triU4")
        self.triS4 = A([128, 4, 128], F32, "triS4")
        self.ones4 = A([128, 4, 128], F32, "ones4")
        for s4 in range(4):
            o.copy(self.triU4[:, s4, :], self.triU[:], eng="pool")
            o.copy(self.triS4[:, s4, :], self.triS[:], eng="pool")
        o.memset(self.ones4[:], 1.0)
        self.sel = A([96, 4, 128], BF16, "sel")
        o.memset(self.sel[:], 0.0)
        for h in range(4):
            v = self.sel[0:96, h, :]
            o.ts(v, self.ones_f[0:96, :], self.ident_f[0:96, h:h + 1], None, ALU.mult, eng="pool")
            o.stt(v, self.ones_f[0:96, :], self.ident_f[0:96, 32 + h:33 + h], v, ALU.mult, ALU.add, eng="pool")
            o.stt(v, self.ones_f[0:96, :], self.ident_f[0:96, 64 + h:65 + h], v, ALU.mult, ALU.add, eng="pool")
        self.wden = A([65, 64], BF16, "wden")
        o.memset(self.wden[0:65, :], EPS)
        o.memset(self.wden[0:64, :], 1.0 / 64)

    def load_weight(self, dst, w_ap, nk, ncols, chan, rows0=0):
        wd = self.S.dram(w_ap, "wdram")
        for k in range(nk):
            src = w_ap[rows0 + k * 128: rows0 + (k + 1) * 128, :]
            self.o.dma(dst[:, k, :], V(wd, src), chan, eng="pool")

    def load_tables(self, cols_ap, tm_ap):
        self.cols = self.alloc([128, NCOLT], F32, "cols")
        self.tm = self.alloc([128, NTM], F32, "tm")
        self.o.dma(self.cols[:], V(self.S.dram(cols_ap, "colsd"), cols_ap), "cols")
        self.o.dma(self.tm[:], V(self.S.dram(tm_ap, "tmd"), tm_ap), "tm")

    def alloc_xnorm(self):
        self.xt = self.alloc([128, NKC, TT], F32, "xt")
        self.ht = self.alloc([128, NKC, TT], BF16, "ht")
        self.sq = self.alloc([128, 2, TT], BF16, "sq")
        self.rstd = self.alloc([128, TT], F32, "rstd")

    def rmsnorm_tile(self, kcol):
        o = self.o
        ps = self.bank()
        for c in range(NKC):
            o.act(self.sq[:, c % 2, :], self.xt[:, c, :], AF.Square)
            o.mm1(ps[:], self.ones_mean[1024][:], self.sq[:, c % 2, :], c == 0, c == NKC - 1)
        o.act(self.rstd[:], ps[:], AF.Sqrt, bias=self.eps_col[:])
        o.recip(self.rstd[:], self.rstd[:])
        for c in range(NKC):
            o.stt(self.ht[:, c, :], self.xt[:, c, :], self.cols[:, kcol + c:kcol + c + 1], self.rstd[:], ALU.mult, ALU.mult,
                  eng="dve" if c % 2 == 0 else "pool")

    def group_rms(self, y, blk, gain, out, tmp, post_mul=None, post_scale=None):
        o = self.o
        sqb = self.gsq
        o.act(sqb[:], y, AF.Square)
        ps = self.bank()
        o.mm(ps[:], [(self.ones_mean[blk][:], sqb[:])])
        o.act(self.grs[:], ps[:], AF.Sqrt, bias=self.eps_col[:])
        o.recip(self.grs[:], self.grs[:])
        if post_mul is None and post_scale is None:
            o.stt(out, y, gain, self.grs[:], ALU.mult, ALU.mult)
        else:
            o.stt(tmp, y, gain, self.grs[:], ALU.mult, ALU.mult)
            if post_mul is not None:
                o.tt(out, tmp, post_mul, ALU.mult)
            else:
                o.ts(out, tmp, post_scale, None, ALU.mult)

    def ffn_phase(self, x_in, x_out, cols_ap, tm_ap, kcol, wg_ap, wu_ap, wd_ap, h_out=None, h_gather=None):
        o = self.o
        self.phase_begin()
        self.ring = list(range(7))
        pstat = self.psum[7]
        HF = D_FF // 2
        NJ = NFC // 2
        wg = [self.alloc([128, NKC, HF], BF16, "wg%d" % i) for i in range(2)]
        wu = [self.alloc([128, NKC, HF], BF16, "wu%d" % i) for i in range(2)]
        wdn = self.alloc([128, NFC, D_MODEL], BF16, "wd")
        self.load_tables(cols_ap, tm_ap)
        wgd, wud = self.S.dram(wg_ap, "wgd"), self.S.dram(wu_ap, "wud")
        for i in range(2):
            for k in range(NKC):
                o.dma(wg[i][:, k, :], V(wgd, wg_ap[k * 128:(k + 1) * 128, i * HF:(i + 1) * HF]), "wA%d" % i, eng="pool")
            for k in range(NKC):
                o.dma(wu[i][:, k, :], V(wud, wu_ap[k * 128:(k + 1) * 128, i * HF:(i + 1) * HF]), "wB%d" % i, eng="pool")
        self.load_weight(wdn, wd_ap, NFC, D_MODEL, "wC")
        xc = [self.alloc([128, TT], F32, "xc%d" % c) for c in range(NKC)]
        self.ht = self.alloc([128, NKC, TT], BF16, "ht")
        self.sq = self.alloc([128, 2, TT], BF16, "sq")
        self.rstd = self.alloc([128, TT], F32, "rstd")
        act_t = self.alloc([128, NFC, TT], BF16, "actT")
        sg = [self.alloc([128, TT], F32, "sg%d" % i) for i in range(2)]
        xin_r = x_in.ap.rearrange("(c p) s -> p c s", p=128)
        xout_r = x_out.ap.rearrange("(c p) s -> p c s", p=128)
        nsq = 0

        def finish_norm(kc):
            o.act(self.rstd[:], pstat[:], AF.Sqrt, bias=self.eps_col[:])
            o.recip(self.rstd[:], self.rstd[:])
            for c in range(NKC):
                o.stt(self.ht[:, c, :], xc[c][:], self.cols[:, kc + c:kc + c + 1], self.rstd[:], ALU.mult, ALU.mult)

        def stat(c):
            nonlocal nsq
            sqv = self.sq[:, nsq % 2, :]
            nsq += 1
            o.act(sqv, xc[c][:], AF.Square)
            o.mm1(pstat[:], self.ones_mean[1024][:], sqv, c == 0, c == NKC - 1)

        NTL = self.S_tok // TT
        for c in range(NKC):
            o.dma(xc[c][:], V(x_in, xin_r[:, c, 0:TT]), "xt%d" % c)
        for t in range(NTL):
            ts = slice(t * TT, (t + 1) * TT)
            for c in range(NKC):
                stat(c)
            finish_norm(kcol)
            for j in range(NFC):
                pg = self.bank()
                pu = self.bank()
                i, jj = j // NJ, j % NJ
                o.mm(pg[:], [(wg[i][:, k, jj * 128:(jj + 1) * 128], self.ht[:, k, :]) for k in range(NKC)])
                o.mm(pu[:], [(wu[i][:, k, jj * 128:(jj + 1) * 128], self.ht[:, k, :]) for k in range(NKC)])
                o.act(sg[j % 2][:], pg[:], AF.Silu)
                o.tt(act_t[:, j, :], sg[j % 2][:], pu[:], ALU.mult)
            for c in range(NKC):
                py = self.bank()
                o.mm(py[:], [(wdn[:, j, c * 128:(c + 1) * 128], act_t[:, j, :]) for j in range(NFC)])
                o.stt(xc[c][:], py[:], 0.5, xc[c][:], ALU.mult, ALU.add)
                o.dma(V(x_out, xout_r[:, c, ts]), xc[c][:], "xo%d" % c)
                if h_out is not None:
                    stat(c)
                elif t + 1 < NTL:
                    o.dma(xc[c][:], V(x_in, xin_r[:, c, (t + 1) * TT:(t + 2) * TT]), "xt%d" % c)
            if h_out is not None:
                finish_norm(K_MIX)
                hb, hv = h_out[(t * TT) // CH]
                c0 = (t * TT) % CH
                o.dma(V(hb, hv.rearrange("(c p) s -> p c s", p=128)[:, :, c0:c0 + TT]), self.ht[:], "ho")
                if t + 1 < NTL:
                    for c in range(NKC):
                        o.dma(xc[c][:], V(x_in, xin_r[:, c, (t + 1) * TT:(t + 2) * TT]), "xt%d" % c)
                if c0 + TT == CH and h_gather is not None:
                    k_ = (t * TT) // CH
                    self.all_gather(h_out[k_][0], h_gather[k_][0])

    def all_gather(self, src, dst):
        self.S.op("pool", lambda e: e.collective_compute("AllGather", ALU.bypass, replica_groups=GROUPS,
                                                         ins=[src.ap.opt()], outs=[dst.ap.opt()]),
                  reads=[src], writes=[dst], dma="ccag")

    def mix_a(self, part, hfull, cols_ap, tm_ap, w_ap, wup_ap, scr):
        o = self.o
        A = self.alloc
        NT = SEQ // TT
        self.phase_begin()
        self.ring = [1, 2, 3, 4, 5, 6, 7]
        acc = self.psum[0]
        VA_, EB_ = (A([128, SEQ // 128, 2, 66], BF16, "VA"), A([128, SEQ // 128, 2], F32, "EB"))
        if part == 2:
            self.VA, self.EB = VA_, EB_
        self.mixb_base = self.poff
        NW = P1_N if part == 1 else P2_N
        win = A([128, NKC, NW], BF16, "win")
        self.load_weight(win, w_ap, NKC, NW, "wA")
        self.load_tables(cols_ap, tm_ap)
        cols, tm = self.cols, self.tm
        hts = [A([128, NKC, TT], BF16, "ht%d" % i) for i in range(2)]
        self.gsq = A([128, TT], BF16, "gsq")
        self.grs = A([128, TT], F32, "grs")
        T = [A([128, TT], F32, "T%d" % i) for i in range(6)]
        Yt = A([128, 4, TT], BF16, "Yt")
        sm = [A([128, 2], F32, "sm%d" % i) for i in range(6)]
        Q = [A([128, 128], F32, "Q%d" % i) for i in range(4)]
        Qb = [A([128, 128], BF16, "Qb%d" % i) for i in range(4)]
        if part == 1:
            scu = A([128, TT + 2], F32, "scu")
            o.memset(scu[:, 0:2], 0.0)
            wup = A([16, 128], F32, "wup")
            o.dma(wup[:], V(self.S.dram(wup_ap, "wupd"), wup_ap), "wup")
            GS32 = A([128, 64], F32, "GS32")
            GSb = A([128, 64], BF16, "GSb")
            o.memset(GS32[:], 0.0)
            o.memset(GSb[:], 0.0)
            la = A([128, 4, 128], F32, "la")
            vtm = A([128, 4, 128], BF16, "vtm")
            qdb = A([128, TT], BF16, "qdb")
            kdb = A([128, TT], BF16, "kdb")
            kdt = A([128, 4, 128], BF16, "kdt")
            attm = [A([128, 4, 128], BF16, "attm%d" % i) for i in range(2)]
            glr = A([32, TT], F32, "glr")
        else:
            o.memset(self.VA[:], 1.0)
            aneg4 = A([128, 8], F32, "aneg4")
            o.act(aneg4[:], tm[:, M_ALOG4:M_ALOG4 + 8], AF.Exp)
            acs4 = A([128, 8], F32, "acs4")
            dte4 = A([128, 8], F32, "dte4")
            eal4 = A([128, 8], F32, "eal4")
            Xd4 = A([128, 4, 128], BF16, "Xd4")
            Xdd4 = A([128, 4, 128], BF16, "Xdd4")
            Btm4 = A([128, 4, 128], BF16, "Btm4")
            lseg4 = [A([128, 4, 128], F32, "lseg4_%d" % h) for h in range(2)]
            abc4 = [A([128, 4, 128], F32, "abc4_%d" % h) for h in range(2)]
            LT4 = [A([128, TT], F32, "LT4_%d" % h) for h in range(2)]
            Ecs4 = [A([128, TT], F32, "Ecs4_%d" % h) for h in range(2)]
            WT4 = [A([128, TT], BF16, "WT4_%d" % h) for h in range(2)]
            CsT4 = [A([128, TT], BF16, "CsT4_%d" % h) for h in range(2)]
            raw = [A([128, TT + 3], F32, "raw%d" % c) for c in range(3)]
            for c in range(3):
                o.memset(raw[c][:, 0:3], 0.0)
            xs32 = A([128, TT], F32, "xs32")
            xsb = A([128, TT], BF16, "xsb")
            BT = A([128, TT], BF16, "BT")
            CT = A([128, TT], BF16, "CT")
            zs = A([128, TT], F32, "zs")
            dt_tm = A([128, 4, 2], F32, "dt_tm")
            a_tm = A([128, 4, 2], F32, "a_tm")
            Xd = A([128, 128], BF16, "Xd")
            Xdd = A([128, 128], BF16, "Xdd")
            Btm = A([128, 128], BF16, "Btm")
            ST32 = [A([128, 64], F32, "ST32_%d" % h) for h in range(2)]
            STb = [A([128, 64], BF16, "STb%d" % h) for h in range(2)]
            for h in range(2):
                o.memset(ST32[h][:], 0.0)
                o.memset(STb[h][:], 0.0)
            gcar_bc = A([128, 2], F32, "gcar_bc")
            gcar_T = A([96, 1], F32, "gcar_T")
            o.memset(gcar_bc[:], 0.0)
            o.memset(gcar_T[:], 0.0)
            nl = A([128, 4, 2], F32, "nl")
            nl3 = A([128, 4, 96], F32, "nl3")
            o.memset(nl3[:], 0.0)
            Fp = A([96, TT], BF16, "Fp")
            F1 = [A([96, TT], F32, "F1_%d" % i) for i in range(3)]
            F1b = [A([96, TT], BF16, "F1b_%d" % i) for i in range(3)]
            qb = A([128, TT], BF16, "qb")
            kb = A([128, TT], BF16, "kb")
            o.memset(Fp[:], 0.0)

        hrs = [(b_, v_.rearrange("(r c p) s -> p r c s", r=2, p=128)) for b_, v_ in hfull]
        Yrs = [(b_, v_.rearrange("(hf c p) s -> p hf c s", hf=2, p=128)) for b_, v_ in scr["Y"]]

        def proj_fm(col0, n=128):
            ps = self.bank()
            o.mm(ps[0:n, :], [(win[:, k, col0:col0 + n], self.ht[:, k, :]) for k in range(NKC)])
            return ps

        def proj_tm(sub, col0, n):
            ps = self.bank()
            o.mm(ps[:, 0:n], [(self.ht[:, k, sub * 128:(sub + 1) * 128], win[:, k, col0:col0 + n]) for k in range(NKC)])
            return ps

        def load_h(t):
            hf_, tl_ = t // (NT // 2), t % (NT // 2)
            hfb_, hr_ = hrs[(tl_ * TT) // CH]
            c0_ = (tl_ * TT) % CH
            o.dma(hts[t % 2][:], V(hfb_, hr_[:, hf_, :, c0_:c0_ + TT]), "ht%d" % (t % 2))
        load_h(0)
        for t in range(NT):
            ts_ = slice(t * TT, (t + 1) * TT)
            hf, tl = t // (NT // 2), t % (NT // 2)
            ck = (tl * TT) // CH
            tsl = slice((tl * TT) % CH, (tl * TT) % CH + TT)
            Yb_, Yr = Yrs[ck]
            self.ht = hts[t % 2]
            if t + 1 < NT:
                load_h(t + 1)
            if part == 1:
                pc = proj_fm(P1_DC)
                o.copy(T[0][:], pc[:], eng="act")
                pv = proj_fm(P1_DV)
                o.tt(scu[:, 2:TT + 2], T[0][:], pv[:], ALU.mult)
                o.ts(T[1][:], scu[:, 0:TT], cols[:, K_SCW:K_SCW + 1], None, ALU.mult)
                o.stt(T[1][:], scu[:, 1:TT + 1], cols[:, K_SCW + 1:K_SCW + 2], T[1][:], ALU.mult, ALU.add)
                o.stt(T[1][:], scu[:, 2:TT + 2], cols[:, K_SCW + 2:K_SCW + 3], T[1][:], ALU.mult, ALU.add)
                o.copy(scu[:, 0:2], scu[:, TT:TT + 2], eng="pool")
                pb = proj_fm(P1_DB)
                o.tt(T[2][:], T[1][:], pb[:], ALU.mult)
                self.group_rms(T[2][:], 64, cols[:, K_SCN:K_SCN + 1], Yt[:, 3, :], None)
                pl = proj_fm(P1_GLR, 16)
                o.copy(glr[0:16, :], pl[0:16, :], eng="act")
                PZ = self.bank()
                for s_ in range(4):
                    o.mm(PZ[:, s_ * 128:(s_ + 1) * 128], [(glr[0:16, s_ * 128:(s_ + 1) * 128], wup[:])])
                laf = la.ap.rearrange("p a b -> p (a b)")
                laf = V(la, laf)
                o.tt(laf, PZ[:], tm[:, M_BG4:M_BG4 + 512], ALU.add)
                o.act(laf, laf, AF.Exp, scale=-1.0)
                o.act(laf, laf, AF.Ln, bias=1.0)
                PV_ = self.bank()
                for s_ in range(4):
                    o.mm(PV_[:, s_ * 128:(s_ + 1) * 128],
                         [(self.ht[:, k, s_ * 128:(s_ + 1) * 128], win[:, k, P1_GV:P1_GV + 128]) for k in range(NKC)])
                o.copy(V(vtm, vtm.ap.rearrange("p a b -> p (a b)")), PV_[:], eng="act")
                pq = proj_fm(P1_GQ)
                o.copy(T[0][:], pq[:], eng="act")
                pk = proj_fm(P1_GK)
                o.copy(T[1][:], pk[:], eng="act")
                pgo = proj_fm(P1_GG)
                o.act(T[2][:], pgo[:], AF.Silu)
                PB = self.bank()
                for s_ in range(4):
                    o.mm(PB[:, s_ * 128:(s_ + 1) * 128], [(la[:, s_, :], self.triG[:])])
                eb, enb = T[5], T[4]
                o.act(eb[:], PB[:], AF.Exp)
                o.act(enb[:], PB[:], AF.Exp, scale=-1.0)
                o.stt(qdb[:], T[0][:], 0.125, eb[:], ALU.mult, ALU.mult)
                o.tt(kdb[:], T[1][:], enb[:], ALU.mult)
                PTr = self.bank()
                PTb = V(PTr, PTr.ap[:, 0:256].bitcast(BF16))
                for s_ in range(4):
                    o.transpose(V(PTr, PTb.ap[:, s_ * 128:(s_ + 1) * 128]), kdb[:, s_ * 128:(s_ + 1) * 128], self.ident_b[:])
                o.copy(V(kdt, kdt.ap.rearrange("p a b -> p (a b)")), PTb)
                for hh in range(2):
                    R = slice(hh * 64, (hh + 1) * 64)
                    PA = self.bank()
                    for s_ in range(4):
                        cs = slice(s_ * 128, (s_ + 1) * 128)
                        o.mm(PA[:, cs], [(kdb[R, cs], qdb[R, cs])])
                    o.tt(V(attm[hh], attm[hh].ap.rearrange("p a b -> p (a b)")), PA[:],
                         V(self.triU4, self.triU4.ap.rearrange("p a b -> p (a b)")), ALU.mult)
                PM = self.bank()
                for s_ in range(4):
                    for hh in range(2):
                        R = slice(hh * 64, (hh + 1) * 64)
                        o.mm(PM[R, s_ * 64:(s_ + 1) * 64], [(kdt[:, s_, R], vtm[:, s_, hh * 64:(hh + 1) * 64])])
                for s_ in range(4):
                    cs = slice(s_ * 128, (s_ + 1) * 128)
                    for hh in range(2):
                        R = slice(hh * 64, (hh + 1) * 64)
                        o.mm(acc[R, cs], [(vtm[:, s_, hh * 64:(hh + 1) * 64], attm[hh][:, s_, :]), (GSb[R, :], qdb[R, cs])])
                    eL = eb[:, s_ * 128 + 127:s_ * 128 + 128]
                    o.ts(GS32[:], GS32[:], eL, None, ALU.mult)
                    o.stt(GS32[:], PM[:, s_ * 64:(s_ + 1) * 64], eL, GS32[:], ALU.mult, ALU.add)
                    o.copy(GSb[:], GS32[:], eng="pool")
                o.copy(T[3][:], acc[:], eng="act")
                self.group_rms(T[3][:], 64, cols[:, K_GLAN:K_GLAN + 1], Yt[:, 0, :], T[4][:], post_mul=T[2][:])
                o.dma(V(Yb_, Yr[:, hf, 0, tsl]), Yt[:, 0, :], "yo")
                o.dma(V(Yb_, Yr[:, hf, 3, tsl]), Yt[:, 3, :], "yo2")
            else:
                for c in range(3):
                    pr = proj_fm(P2_CX + c * 128)
                    o.copy(raw[c][:, 3:TT + 3], pr[:], eng="act")
                    tc_ = T[0] if c % 2 == 0 else T[1]
                    o.ts(tc_[:], raw[c][:, 0:TT], cols[:, K_CW + c:K_CW + c + 1], None, ALU.mult)
                    for k in range(1, 4):
                        o.stt(tc_[:], raw[c][:, k:TT + k], cols[:, K_CW + 3 * k + c:K_CW + 3 * k + c + 1], tc_[:], ALU.mult, ALU.add)
                    o.copy(raw[c][:, 0:3], raw[c][:, TT:TT + 3], eng="pool")
                    bcol = cols[:, K_CB + c:K_CB + c + 1]
                    if c == 0:
                        o.act(xs32[:], tc_[:], AF.Silu, bias=bcol)
                        o.copy(xsb[:], xs32[:], eng="pool")
                    elif c == 1:
                        o.act(BT[:], tc_[:], AF.Silu, bias=bcol)
                    else:
                        o.act(CT[:], tc_[:], AF.Silu, bias=bcol)
                pq = proj_fm(P2_BQ)
                o.copy(T[5][:], pq[:], eng="act")
                self.group_rms(T[5][:], 64, cols[:, K_FQN:K_FQN + 1], qb[:], T[4][:], post_scale=0.125)
                o.dma(V(scr["Q"], scr["Q"].ap[:, ts_]), qb[:], "qo")
                pk = proj_fm(P2_BK)
                o.copy(T[3][:], pk[:], eng="act")
                self.group_rms(T[3][:], 64, cols[:, K_FKN:K_FKN + 1], kb[:], None)
                o.dma(V(scr["K"], scr["K"].ap[:, ts_]), kb[:], "ko")
                for s_ in range(4):
                    kt = t * 4 + s_
                    pv = proj_tm(s_, P2_BV, 130)
                    for h in range(2):
                        o.copy(self.VA[:, kt, h, 0:64], pv[:, h * 64:(h + 1) * 64], eng="act" if h % 2 == 0 else "dve")
                    o.tt(nl[:, s_, :], pv[:, 128:130], tm[:, M_FB:M_FB + 2], ALU.add)
                    o.act(nl[:, s_, :], nl[:, s_, :], AF.Exp, scale=-1.0)
                    o.act(nl[:, s_, :], nl[:, s_, :], AF.Ln, bias=1.0)
                    for g in range(3):
                        o.copy(nl3[:, s_, g * 32:g * 32 + 2], nl[:, s_, :], eng="pool")
                    pg_ = self.bank()
                    o.mm(pg_[:, 0:2], [(self.ones_f[:], nl[:, s2, :]) for s2 in range(s_)] + [(self.triU[:], nl[:, s_, :])])
                    o.tt(sm[3][:], pg_[:, 0:2], gcar_bc[:], ALU.add)
                    o.ts(self.EB[:, kt, :], sm[3][:], -FOX_SHIFT, None, ALU.add)
                ptot = self.bank()
                o.mm(ptot[:, 0:2], [(self.ones_f[:], nl[:, s2, :]) for s2 in range(4)])
                o.tt(gcar_bc[:], gcar_bc[:], ptot[:, 0:2], ALU.add)
                pgt = self.bank()
                for s_ in range(4):
                    cs = slice(s_ * 128, (s_ + 1) * 128)
                    o.mm(pgt[0:96, cs], [(nl3[:, s2, :], self.ones_f[:]) for s2 in range(s_)] + [(nl3[:, s_, :], self.triU[:])])
                GT = F1[0]
                o.ts(GT[:], pgt[0:96, :], gcar_T[:, 0:1], None, ALU.add)
                o.copy(gcar_T[:], GT[:, TT - 1:TT])
                hi, mid, lo = F1b[0], F1b[1], F1b[2]
                r1, r2 = F1[1], F1[2]
                o.ts(hi[:], GT[:], -1.0, None, ALU.mult)
                o.stt(r1[:], GT[:], -1.0, hi[:], ALU.mult, ALU.subtract)
                o.copy(mid[:], r1[:])
                o.tt(r2[:], r1[:], mid[:], ALU.subtract)
                o.copy(lo[:], r2[:])
                o.copy(Fp[0:4, :], hi[0:4, :])
                o.copy(Fp[32:36, :], mid[32:36, :])
                o.copy(Fp[64:68, :], lo[64:68, :])
                o.dma(V(scr["FP"], scr["FP"].ap[:, ts_]), Fp[:], "fpo")
                pz = proj_fm(P2_CZ)
                o.act(zs[:], pz[:], AF.Silu)
                PD = self.bank()
                for s_ in range(4):
                    o.mm(PD[:, s_ * 2:(s_ + 1) * 2],
                         [(self.ht[:, k, s_ * 128:(s_ + 1) * 128], win[:, k, P2_CDT:P2_CDT + 2]) for k in range(NKC)])
                dt4 = V(dt_tm, dt_tm.ap.rearrange("p a b -> p (a b)"))
                a4 = V(a_tm, a_tm.ap.rearrange("p a b -> p (a b)"))
                o.tt(dt4, PD[:, 0:8], tm[:, M_DTB4:M_DTB4 + 8], ALU.add)
                o.act(dt4, dt4, AF.Exp)
                o.act(dt4, dt4, AF.Ln, bias=1.0)
                o.stt(a4, dt4, -1.0, aneg4[:], ALU.mult, ALU.mult)
                PCS = self.bank()
                o.mm(PCS[:, 0:8], [(self.triU[:], a4)])
                o.mm(PCS[:, 8:16], [(self.ones_f[:], a4)])
                o.copy(acs4[:], PCS[:, 0:8])
                o.tt(dte4[:], PCS[:, 8:16], acs4[:], ALU.subtract)
                o.act(dte4[:], dte4[:], AF.Exp)
                o.act(eal4[:], PCS[:, 8:16], AF.Exp)
                PX = self.bank()
                PXb = V(PX, PX.ap[:, 0:256].bitcast(BF16))
                for s_ in range(4):
                    o.transpose(V(PX, PXb.ap[:, s_ * 128:(s_ + 1) * 128]), xsb[:, s_ * 128:(s_ + 1) * 128], self.ident_b[:])
                Xd8 = V(Xd4, Xd4.ap.rearrange("p a (h e) -> p (a h) e", h=2))
                Xdd8 = V(Xdd4, Xdd4.ap.rearrange("p a (h e) -> p (a h) e", h=2))
                o.tt(Xd8, V(PX, PXb.ap.rearrange("p (g e) -> p g e", e=64)),
                     V(dt_tm, dt4.ap.unsqueeze(2).broadcast_to([128, 8, 64])), ALU.mult)
                o.tt(Xdd8, Xd8, V(dte4, dte4.ap.unsqueeze(2).broadcast_to([128, 8, 64])), ALU.mult, eng="pool")
                PBt = self.bank()
                PBb = V(PBt, PBt.ap[:, 0:256].bitcast(BF16))
                for s_ in range(4):
                    o.transpose(V(PBt, PBb.ap[:, s_ * 128:(s_ + 1) * 128]), BT[:, s_ * 128:(s_ + 1) * 128], self.ident_b[:])
                o.copy(V(Btm4, Btm4.ap.rearrange("p a b -> p (a b)")), PBb)
                PSC = self.bank()
                for s_ in range(4):
                    cs = slice(s_ * 128, (s_ + 1) * 128)
                    o.mm(PSC[:, cs], [(BT[:, cs], CT[:, cs])])
                for h in range(2):
                    ah = V(a_tm, a_tm.ap[:, :, h:h + 1].broadcast_to([128, 4, 128]))
                    o.tt(lseg4[h][:], self.triS4[:], ah, ALU.mult, eng="pool")
                    o.tt(abc4[h][:], self.ones4[:], ah, ALU.mult, eng="pool")
                    PSEG = self.bank()
                    PAB = self.bank()
                    for s_ in range(4):
                        cs = slice(s_ * 128, (s_ + 1) * 128)
                        o.mm(PSEG[:, cs], [(lseg4[h][:, s_, :], self.triU[:]), (self.ident_f[:], self.maskb[:])])
                        o.mm(PAB[:, cs], [(abc4[h][:, s_, :], self.triU[:])])
                    o.act(LT4[h][:], PSEG[:], AF.Exp)
                    o.tt(WT4[h][:], PSC[:], LT4[h][:], ALU.mult)
                    o.act(Ecs4[h][:], PAB[:], AF.Exp)
                    o.tt(CsT4[h][:], CT[:], Ecs4[h][:], ALU.mult)
                for s_ in range(4):
                    cs = slice(s_ * 128, (s_ + 1) * 128)
                    for h in range(2):
                        hc = slice(h * 64, (h + 1) * 64)
                        o.mm(acc[hc, cs], [(Xd4[:, s_, hc], WT4[h][:, cs]), (STb[h][:], CsT4[h][:, cs])])
                        pst = self.bank()
                        o.mm(pst[:, 0:64], [(Btm4[:, s_, :], Xdd4[:, s_, hc])])
                        o.stt(ST32[h][:], ST32[h][:], eal4[:, s_ * 2 + h:s_ * 2 + h + 1], pst[:, 0:64], ALU.mult, ALU.add)
                        o.copy(STb[h][:], ST32[h][:], eng="pool")
                o.stt(T[2][:], xs32[:], cols[:, K_D:K_D + 1], acc[:], ALU.mult, ALU.add)
                o.tt(T[3][:], T[2][:], zs[:], ALU.mult)
                self.group_rms(T[3][:], 128, cols[:, K_SN:K_SN + 1], Yt[:, 2, :], None)
                o.dma(V(Yb_, Yr[:, hf, 2, tsl]), Yt[:, 2, :], "yo")

    def mix_b(self, cols_ap, scr, y_gather=None):
        o = self.o
        A = self.alloc
        NT = SEQ // TT
        self.S.barrier()
        self.poff = self.mixb_base
        self.ring = [2, 3, 4, 5, 6, 7]
        oacc = [self.psum[0], self.psum[1]]
        cols = A([128, NCOLT], F32, "cols_b")
        o.dma(cols[:], V(self.S.dram(cols_ap, "colsd2"), cols_ap), "cols")
        KTh = [A([67, SEQ], BF16, "KTh%d" % h) for h in range(2)]
        for h in range(2):
            o.memset(KTh[h][64:67, :], 1.0)
            for q4 in range(4):
                qs = slice(q4 * (SEQ // 4), (q4 + 1) * (SEQ // 4))
                o.dma(KTh[h][0:64, qs], V(scr["K"], scr["K"].ap[h * 64:(h + 1) * 64, qs]), "kth%d" % h)
        Qh = [[A([67, TT], BF16, "Qh%d_%d" % (i, h)) for h in range(2)] for i in range(2)]
        FPr = scr["FP"].ap.rearrange("(g x) s -> x g s", x=32)
        PT = [A([128, TT], BF16, "PT%d" % i) for i in range(4)]
        Oa = A([65, TT], F32, "Oa")
        Osq = A([65, TT], BF16, "Osq")
        rs = A([64, TT], F32, "rs_b")
        Yb = [A([64, TT], BF16, "Yb%d" % h) for h in range(4)]
        Yrows = [(b_, v_.rearrange("(hf c g p) s -> p hf c g s", hf=2, c=4, g=2)) for b_, v_ in scr["Y"]]
        LAG = 2
        npt = 0
        nyb = 0

        def load_qf(qt):
            ts2 = slice(qt * TT, (qt + 1) * TT)
            for h in range(2):
                o.dma(Qh[qt % 2][h][0:64, :], V(scr["Q"], scr["Q"].ap[h * 64:(h + 1) * 64, ts2]), "qi%d_%d" % (qt % 2, h))
                o.dma(Qh[qt % 2][h][64:67, :], V(scr["FP"], FPr[h, :, ts2]), "qi%d_%d" % (qt % 2, h))
        load_qf(0)
        for qt in range(NT):
            hf, tl = qt // (NT // 2), qt % (NT // 2)
            tsl = slice((tl * TT) % CH, (tl * TT) % CH + TT)
            Ybuf, Yrow = Yrows[(tl * TT) // CH]
            if qt + 1 < NT:
                load_qf(qt + 1)
            Qc = Qh[qt % 2]
            nk = (qt + 1) * 4
            blocks = [(h, kt) for h in range(2) for kt in range(nk)]
            pend = []

            def stage2(item):
                nonlocal nyb
                h, kt, col0, pt = item
                oa = oacc[h % 2]
                o.mm1(oa[0:65, col0:TT], self.VA[:, kt, h, 0:65], pt[:, col0:TT], kt == 0, kt == nk - 1)
                if kt == nk - 1:
                    o.copy(Oa[:], oa[0:65, :], eng="act")
                    o.act(Osq[:], Oa[:], AF.Square)
                    pd = self.bank()
                    o.mm(pd[0:64, :], [(self.wden[:], Osq[:])])
                    o.act(rs[:], pd[0:64, :], AF.Sqrt)
                    o.recip(rs[:], rs[:])
                    yb = Yb[nyb % 4]
                    nyb += 1
                    o.stt(yb[:], Oa[0:64, :], cols[0:64, K_FON + h:K_FON + h + 1], rs[:], ALU.mult, ALU.mult)
                    o.dma(V(Ybuf, Yrow[:, hf, 1, h, tsl]), yb[:], "ybo%d" % (nyb % 4))

            for (h, kt) in blocks:
                R = slice(h * 64, (h + 1) * 64)
                c = kt - qt * 4
                col0 = max(c, 0) * 128
                sT = self.bank()
                o.mm(sT[:, col0:TT], [(KTh[h][:, kt * 128:(kt + 1) * 128], Qc[h][:, col0:TT])])
                if c >= 0:
                    o.tt(sT[:, col0:col0 + 128], sT[:, col0:col0 + 128], self.maskb[:], ALU.add)
                pt = PT[npt % 4]
                npt += 1
                o.act(pt[:, col0:TT], sT[:, col0:TT], AF.Exp, bias=self.EB[:, kt, h:h + 1])
                pend.append((h, kt, col0, pt))
                if len(pend) > LAG:
                    stage2(pend.pop(0))
            while pend:
                stage2(pend.pop(0))
            if y_gather is not None and hf == 1 and (tl * TT) % CH + TT == CH:
                k_ = (tl * TT) // CH
                self.all_gather(scr["Y"][k_][0], y_gather[k_][0])

    def outproj_phase(self, x_in, x_out, yfull, cols_ap, tm_ap, w_out_ap):
        o = self.o
        A = self.alloc
        self.phase_begin()
        self.ring = list(range(8))
        wo = A([128, NKC, D_MODEL], BF16, "wo")
        self.load_weight(wo, w_out_ap, NKC, D_MODEL, "wA")
        self.load_tables(cols_ap, tm_ap)
        cols = self.cols
        xt = A([128, NKC, TT], F32, "xt_o")
        Y0 = A([128, NKC, TT], BF16, "Y0")
        Y1 = A([128, NKC, TT], BF16, "Y1")
        Ys = A([128, NKC, TT], BF16, "Ys")
        xin_r = x_in.ap.rearrange("(c p) s -> p c s", p=128)
        xout_r = x_out.ap.rearrange("(c p) s -> p c s", p=128)
        yrs = [(b_, v_.rearrange("(r hf c p) s -> p hf r c s", r=2, hf=2, p=128)) for b_, v_ in yfull]
        for t in range(HALF // TT):
            ts_ = slice(t * TT, (t + 1) * TT)
            o.dma(xt[:], V(x_in, xin_r[:, :, ts_]), "xt")
            yfb, yr = yrs[(t * TT) // CH]
            tsc = slice((t * TT) % CH, (t * TT) % CH + TT)
            for rr in range(2):
                o.dma(Y0[:, rr * 4:(rr + 1) * 4, :], V(yfb, yr[:, 0, rr, :, tsc]), "y0")
                o.dma(Y1[:, rr * 4:(rr + 1) * 4, :], V(yfb, yr[:, 1, rr, :, tsc]), "y1")
            o.ts(Ys[:], Y0[:], cols[:, K_SEL:K_SEL + 1], None, ALU.mult)
            o.stt(Ys[:], Y1[:], cols[:, K_SEL + 1:K_SEL + 2], Ys[:], ALU.mult, ALU.add)
            for oc in range(NKC):
                py = self.bank()
                o.mm(py[:], [(wo[:, k, oc * 128:(oc + 1) * 128], Ys[:, k, :]) for k in range(NKC)])
                o.tt(xt[:, oc, :], xt[:, oc, :], py[:], ALU.add)
            o.dma(V(x_out, xout_r[:, :, ts_]), xt[:], "xo")


def build_program(depth, debug=None):
    nc = bass.Bass("TRN2", target_bir_lowering=False)
    stack = ExitStack()

    def din(name, shape, dt=F32):
        return nc.dram_tensor(name, list(shape), dt, kind="ExternalInput").ap()
    xT = din("xT", [D_MODEL, HALF])
    w = {}
    for l in range(depth):
        for f in (1, 2):
            w["g%d_%d" % (f, l)] = din("ffn%d_w_gate_%d" % (f, l), [D_MODEL, D_FF])
            w["u%d_%d" % (f, l)] = din("ffn%d_w_up_%d" % (f, l), [D_MODEL, D_FF])
            w["d%d_%d" % (f, l)] = din("ffn%d_w_down_%d" % (f, l), [D_FF, D_MODEL])
        w["win1_%d" % l] = din("w_in1_%d" % l, [D_MODEL, P1_N])
        w["win2_%d" % l] = din("w_in2_%d" % l, [D_MODEL, P2_N])
        w["wout_%d" % l] = din("w_out_%d" % l, [D_MODEL, D_MODEL])
        w["wup_%d" % l] = din("gla_wup_%d" % l, [16, 128])
        w["cols_%d" % l] = din("cols_%d" % l, [128, NCOLT])
        w["tm_%d" % l] = din("tm_%d" % l, [128, NTM])
    outT = nc.dram_tensor("outT", [D_MODEL, HALF], F32, kind="ExternalOutput").ap()
    with stack:
        B = Builder(nc, stack, HALF)
        S = B.S

        def scratch(name, shape, dt):
            return S.dram(nc.dram_tensor(name, list(shape), dt).ap(), name)
        xa = scratch("xa", [D_MODEL, HALF], F32)
        xb = scratch("xb", [D_MODEL, HALF], F32)
        NCH = HALF // CH

        def cc_pair(name):
            srcs, dsts = [], []
            for k in range(NCH):
                sb_ = scratch("%s_s%d" % (name, k), [128, 8 * CH], BF16)
                db_ = scratch("%s_d%d" % (name, k), [256, 8 * CH], BF16)
                srcs.append((sb_, sb_.ap.rearrange("p (a s) -> (p a) s", s=CH)))
                dsts.append((db_, db_.ap.rearrange("p (a s) -> (p a) s", s=CH)))
            return srcs, dsts
        hsrc, hfull = cc_pair("h")
        ysrc, yfull = cc_pair("y")
        scr = {"Y": ysrc,
               "Q": scratch("scrQ", [128, SEQ], BF16),
               "K": scratch("scrK", [128, SEQ], BF16),
               "FP": scratch("scrFP", [96, SEQ], BF16)}
        cur = S.dram(xT, "xT")
        xout = S.dram(outT, "outT")
        for l in range(depth):
            last = l == depth - 1
            ct = (w["cols_%d" % l], w["tm_%d" % l])
            B.ffn_phase(cur, xa, ct[0], ct[1], K_FFN1, w["g1_%d" % l], w["u1_%d" % l], w["d1_%d" % l], h_out=hsrc, h_gather=hfull)
            B.mix_a(1, hfull, ct[0], ct[1], w["win1_%d" % l], w["wup_%d" % l], scr)
            B.mix_a(2, hfull, ct[0], ct[1], w["win2_%d" % l], w["wup_%d" % l], scr)
            B.mix_b(ct[0], scr, y_gather=yfull)
            B.outproj_phase(xa, xb, yfull, ct[0], ct[1], w["wout_%d" % l])
            dst = xout if last else xa
            B.ffn_phase(xb, dst, ct[0], ct[1], K_FFN2, w["g2_%d" % l], w["u2_%d" % l], w["d2_%d" % l])
            cur = dst
        S.barrier()
        S.emit()
        build_program.nsem = S.nsem
    return nc


def make_inputs(inp, depth, r):
    d = {}
    for l in range(depth):
        ffn = {1: (inp["ffn1_w_gate"], inp["ffn1_w_up"], inp["ffn1_w_down"]),
               2: (inp["ffn2_w_gate"], inp["ffn2_w_up"], inp["ffn2_w_down"])}
        for f in (1, 2):
            d["ffn%d_w_gate_%d" % (f, l)] = np.ascontiguousarray(ffn[f][0][l], dtype=np.float32)
            d["ffn%d_w_up_%d" % (f, l)] = np.ascontiguousarray(ffn[f][1][l], dtype=np.float32)
            d["ffn%d_w_down_%d" % (f, l)] = np.ascontiguousarray(ffn[f][2][l], dtype=np.float32)
        p1, p2 = host_win(inp, l, r)
        d["w_in1_%d" % l] = p1
        d["w_in2_%d" % l] = p2
        d["w_out_%d" % l] = host_wout(inp, l)
        d["gla_wup_%d" % l] = np.ascontiguousarray(np.asarray(inp["gla_w_gate_up"][l], np.float32)[:, r * 128:(r + 1) * 128])
        cols, tm = host_tables(inp, l, r)
        d["cols_%d" % l] = cols
        d["tm_%d" % l] = tm
    return d


_PROG_CACHE = {}


def kernel(**inputs):
    x = np.asarray(inputs["x"], np.float32)
    Bsz, L, D = x.shape
    depth = inputs["w_in"].shape[0]
    assert (Bsz, L, D) == (4, SEQ, D_MODEL)
    if depth not in _PROG_CACHE:
        _PROG_CACHE[depth] = build_program(depth)
    nc = _PROG_CACHE[depth]
    shared = [make_inputs(inputs, depth, r) for r in range(2)]
    in_maps = []
    for c in range(8):
        b, r = c // 2, c % 2
        m = dict(shared[r])
        m["xT"] = np.ascontiguousarray(x[b, r * HALF:(r + 1) * HALF].T)
        in_maps.append(m)
    res = run_bass_kernel_spmd(nc, in_maps, core_ids=list(range(8)))
    out = np.empty((Bsz, L, D), np.float32)
    for c in range(8):
        b, r = c // 2, c % 2
        out[b, r * HALF:(r + 1) * HALF] = res.results[c]["outT"].T
    return out
```

```python
import numpy as np
from contextlib import ExitStack
import concourse.bass as bass
import concourse.mybir as mybir
from concourse.bass_utils import run_bass_kernel_spmd

F32 = mybir.dt.float32
BF16 = mybir.dt.bfloat16
ALU = mybir.AluOpType
AF = mybir.ActivationFunctionType
AX = mybir.AxisListType

SEM_CAP = 30000


class Buf:
    __slots__ = ("ap", "name", "lw", "rd")

    def __init__(self, ap, name):
        self.ap = ap
        self.name = name
        self.lw = None
        self.rd = {}

    def __getitem__(self, k):
        return V(self, self.ap[k])


class V:
    __slots__ = ("b", "ap")

    def __init__(self, b, ap):
        self.b = b
        self.ap = ap


def _bufs(*vs):
    return [v.b for v in vs if isinstance(v, V)]


def _ap(v):
    return v.ap if isinstance(v, V) else v


class Sched:
    ENGS = ("pe", "act", "dve", "pool", "sp")

    def __init__(self, nc, stack):
        self.nc = nc
        self.stack = stack
        self.streams = {e: [] for e in self.ENGS}
        self.dom = {e: [] for e in self.ENGS}
        self.waited = {e: {} for e in self.ENGS}
        self.nbuf = 0

    def sb(self, shape, dtype, name=None):
        self.nbuf += 1
        name = "%s_%d" % (name or "sb", self.nbuf)
        t = self.stack.enter_context(self.nc.sbuf_tensor(name, list(shape), dtype))
        return Buf(t, name)

    def ps(self, shape, dtype=F32, name=None):
        self.nbuf += 1
        name = "%s_%d" % (name or "ps", self.nbuf)
        t = self.stack.enter_context(self.nc.psum_tensor(name, list(shape), dtype))
        return Buf(t, name)

    def dram(self, ap, name):
        return Buf(ap, name)

    def _deps(self, eng, reads, writes):
        deps = {}

        def add(tok):
            if tok is None:
                return
            d, s = tok
            if d == "pe" and eng == "pe":
                return
            if deps.get(d, -1) < s:
                deps[d] = s

        for b in reads:
            add(b.lw)
        for b in writes:
            add(b.lw)
            for d, s in b.rd.items():
                add((d, s))
        w = self.waited[eng]
        out = []
        for d, s in deps.items():
            if w.get(d, -1) >= s:
                continue
            w[d] = s
            out.append((d, s))
        return out

    def op(self, eng, fn, reads=(), writes=(), dma=None):
        deps = self._deps(eng, reads, writes)
        dom = eng if dma is None else "dma:" + dma
        if dom not in self.dom:
            self.dom[dom] = []
        rec = {"fn": fn, "deps": deps, "sig": dma is not None, "dom": dom,
               "seq": len(self.dom[dom])}
        self.dom[dom].append(rec)
        self.streams[eng].append(rec)
        tok = (dom, rec["seq"])
        for b in reads:
            if b.rd.get(dom, -1) < rec["seq"]:
                b.rd[dom] = rec["seq"]
        for b in writes:
            b.lw = tok
            b.rd = {}
        return tok

    def barrier(self, exclude=()):
        last = {d: len(r) - 1 for d, r in self.dom.items() if r and d not in exclude}
        for e in self.ENGS:
            deps = []
            for d, s in last.items():
                if self.waited[e].get(d, -1) >= s:
                    continue
                self.waited[e][d] = s
                deps.append((d, s))
            rec = {"fn": (lambda eng: eng.nop()), "deps": deps, "sig": False, "dom": e,
                   "seq": len(self.dom[e])}
            self.dom[e].append(rec)
            self.streams[e].append(rec)

    def emit(self):
        nc = self.nc
        for e in self.ENGS:
            for rec in self.streams[e]:
                for d, s in rec["deps"]:
                    self.dom[d][s]["sig"] = True
        semtab = {}
        for d, recs in self.dom.items():
            step = 16 if (d.startswith("dma:") and not d.startswith("dma:cc")) else 1
            cap = SEM_CAP // step
            c = 0
            for rec in recs:
                if rec["sig"]:
                    rec["sv"] = (c // cap, (c % cap + 1) * step)
                    c += 1
            nep = (c + cap - 1) // cap
            semtab[d] = [self.stack.enter_context(nc.semaphore("s_%s_%d" % (d.replace(":", "_"), i)))
                         for i in range(max(nep, 0))]
        self.nsem = sum(len(v) for v in semtab.values())
        block = self.stack.enter_context(nc.Block())

        def run(engname):
            def body(eng):
                for rec in self.streams[engname]:
                    for d, s in rec["deps"]:
                        ep, val = self.dom[d][s]["sv"]
                        eng.wait_ge(semtab[d][ep], val)
                    ins = rec["fn"](eng)
                    if rec["sig"]:
                        ep, val = rec["sv"]
                        step = 16 if (rec["dom"].startswith("dma:") and not rec["dom"].startswith("dma:cc")) else 1
                        ins.then_inc(semtab[rec["dom"]][ep], step)
            return body

        block.tensor(run("pe"))
        block.scalar(run("act"))
        block.vector(run("dve"))
        block.gpsimd(run("pool"))
        block.sync(run("sp"))


class Ops:
    def __init__(self, S):
        self.S = S

    def dma(self, out, in_, chan, eng="sp"):
        return self.S.op(eng, lambda e: e.dma_start(out=out.ap, in_=in_.ap), reads=[in_.b], writes=[out.b], dma=chan)

    def act(self, out, in_, func, bias=0.0, scale=1.0, eng="act"):
        return self.S.op(eng, lambda e: e.activation(out=out.ap, in_=in_.ap, func=func, bias=_ap(bias), scale=_ap(scale)),
                         reads=_bufs(in_, bias, scale), writes=[out.b])

    def tt(self, out, in0, in1, op, eng="dve"):
        return self.S.op(eng, lambda e: e.tensor_tensor(out=out.ap, in0=in0.ap, in1=in1.ap, op=op),
                         reads=_bufs(in0, in1), writes=[out.b])

    def ts(self, out, in0, s1, s2, op0, op1=None, eng="dve"):
        if op1 is None:
            return self.S.op(eng, lambda e: e.tensor_scalar(out=out.ap, in0=in0.ap, scalar1=_ap(s1), scalar2=None, op0=op0),
                             reads=_bufs(in0, s1), writes=[out.b])
        return self.S.op(eng, lambda e: e.tensor_scalar(out=out.ap, in0=in0.ap, scalar1=_ap(s1), scalar2=_ap(s2), op0=op0, op1=op1),
                         reads=_bufs(in0, s1, s2), writes=[out.b])

    def stt(self, out, in0, scalar, in1, op0, op1, eng="dve"):
        eng = "dve"
        return self.S.op(eng, lambda e: e.scalar_tensor_tensor(out=out.ap, in0=in0.ap, scalar=_ap(scalar), in1=in1.ap, op0=op0, op1=op1),
                         reads=_bufs(in0, scalar, in1), writes=[out.b])

    def copy(self, out, in_, eng="dve"):
        if eng == "act":
            return self.act(out, in_, AF.Copy)
        return self.S.op(eng, lambda e: e.tensor_copy(out=out.ap, in_=in_.ap), reads=[in_.b], writes=[out.b])

    def recip(self, out, in_):
        return self.S.op("dve", lambda e: e.reciprocal(out=out.ap, in_=in_.ap), reads=[in_.b], writes=[out.b])

    def memset(self, out, val, eng="pool"):
        return self.S.op(eng, lambda e: e.memset(out.ap, val), writes=[out.b])

    def aselect(self, out, in_, pattern, cmp, fill, base, cm):
        return self.S.op("pool", lambda e: e.affine_select(out=out.ap, in_=in_.ap, pattern=pattern, compare_op=cmp,
                                                           fill=fill, base=base, channel_multiplier=cm),
                         reads=[in_.b], writes=[out.b])

    def mm(self, out, pairs, extra_reads=()):
        n = len(pairs)

        def fn(e):
            ins = None
            for i, (l, r) in enumerate(pairs):
                ins = e.matmul(out.ap, lhsT=l.ap, rhs=r.ap, start=(i == 0), stop=(i == n - 1))
            return ins
        rd = []
        for l, r in pairs:
            rd.append(l.b)
            rd.append(r.b)
        rd.extend(extra_reads)
        return self.S.op("pe", fn, reads=rd, writes=[out.b])

    def mm1(self, out, l, r, start, stop):
        return self.S.op("pe", lambda e: e.matmul(out.ap, lhsT=l.ap, rhs=r.ap, start=start, stop=stop),
                         reads=[l.b, r.b], writes=[out.b])

    def transpose(self, out, in_, ident):
        return self.S.op("pe", lambda e: e.transpose(out.ap, in_.ap, ident.ap), reads=[in_.b, ident.b], writes=[out.b])


D_MODEL = 1024
D_FF = 2816
NKC = D_MODEL // 128
NFC = D_FF // 128
IN_COLS = 3608
TT = 512
EPS = 1e-6
FOX_SHIFT = 10.0
NEG = -30000.0
SEQ = 8192
HALF = SEQ // 2
CH = 1024
GROUPS = [[0, 1], [2, 3], [4, 5], [6, 7]]

C_AQ, C_AK, C_AV, C_AG, C_ALR = 0, 256, 512, 768, 1024
C_BQ, C_BK, C_BV, C_BF = 1040, 1296, 1552, 1808
C_CZ, C_CX, C_CDT = 1812, 2068, 2836
C_DB, C_DC, C_DV = 2840, 3096, 3352
P1_GQ, P1_GK, P1_GV, P1_GG, P1_GLR, P1_DB, P1_DC, P1_DV, P1_N = 0, 128, 256, 384, 512, 528, 656, 784, 912
P2_BQ, P2_BK, P2_BV, P2_CZ, P2_CX, P2_CB, P2_CC, P2_CDT, P2_N = 0, 128, 256, 386, 514, 642, 770, 898, 900

K_FFN1, K_MIX, K_FFN2 = 0, 8, 16
K_GLAN = 24
K_FQN, K_FKN = 25, 26
K_FON = 27
K_CB = 29
K_CW = 32
K_D = 44
K_SN = 45
K_SCW = 46
K_SCN = 49
K_SEL = 50
NCOLT = 52
M_ALOG, M_FB, M_DTB, M_BG = 0, 2, 4, 6
M_ALOG4, M_DTB4, M_BG4 = 134, 142, 150
NTM = 150 + 512


def host_tables(inp, l, r):
    f32 = np.float32
    cols = np.zeros((128, NCOLT), f32)
    hs = slice(r * 128, (r + 1) * 128)

    def put(k, vec):
        v = np.asarray(vec, f32).reshape(-1, 128)
        for c in range(v.shape[0]):
            cols[:, k + c] = v[c]
    put(K_FFN1, inp["ffn1_norm"][l]); put(K_MIX, inp["mix_norm"][l]); put(K_FFN2, inp["ffn2_norm"][l])
    put(K_GLAN, inp["gla_norm"][l][hs])
    put(K_FQN, np.tile(inp["fox_q_norm"][l], 2)); put(K_FKN, np.tile(inp["fox_k_norm"][l], 2))
    fo = np.asarray(inp["fox_out_norm"][l], f32).reshape(4, 64)
    for h in range(2):
        cols[0:64, K_FON + h] = fo[2 * r + h]
    cb = np.asarray(inp["ssm_conv_b"][l], f32)
    cw = np.asarray(inp["ssm_conv_w"][l], f32)
    chs = [slice(r * 128, (r + 1) * 128), slice(256 + r * 128, 256 + (r + 1) * 128), slice(512 + r * 128, 512 + (r + 1) * 128)]
    for c in range(3):
        cols[:, K_CB + c] = cb[chs[c]]
        for k in range(4):
            cols[:, K_CW + 3 * k + c] = cw[k][chs[c]]
    put(K_D, np.repeat(np.asarray(inp["ssm_D"][l], f32)[2 * r:2 * r + 2], 64))
    put(K_SN, inp["ssm_norm"][l][hs])
    for k in range(3):
        put(K_SCW + k, inp["sc_conv_w"][l][k][hs])
    put(K_SCN, inp["sc_out_norm"][l][hs])
    cols[:, K_SEL + r] = 1.0
    tm = np.zeros((128, NTM), f32)
    tm[:, M_ALOG:M_ALOG + 2] = np.asarray(inp["ssm_A_log"][l], f32)[None, 2 * r:2 * r + 2]
    tm[:, M_FB:M_FB + 2] = np.asarray(inp["fox_b_forget"][l], f32)[None, 2 * r:2 * r + 2]
    tm[:, M_DTB:M_DTB + 2] = np.asarray(inp["ssm_dt_bias"][l], f32)[None, 2 * r:2 * r + 2]
    tm[:, M_BG:M_BG + 128] = np.asarray(inp["gla_b_gate"][l], f32)[None, hs]
    tm[:, M_ALOG4:M_ALOG4 + 8] = np.tile(tm[:, M_ALOG:M_ALOG + 2], (1, 4))
    tm[:, M_DTB4:M_DTB4 + 8] = np.tile(tm[:, M_DTB:M_DTB + 2], (1, 4))
    tm[:, M_BG4:M_BG4 + 512] = np.tile(tm[:, M_BG:M_BG + 128], (1, 4))
    return cols, tm


def host_win(inp, l, r):
    w = np.asarray(inp["w_in"][l], np.float32)
    h = lambda c0: w[:, c0 + r * 128: c0 + (r + 1) * 128]
    p1 = np.concatenate([h(C_AQ), h(C_AK), h(C_AV), h(C_AG), w[:, C_ALR:C_ALR + 16], h(C_DB), h(C_DC), h(C_DV)], axis=1)
    p2 = np.concatenate([h(C_BQ), h(C_BK), h(C_BV), w[:, C_BF + 2 * r:C_BF + 2 * r + 2], h(C_CZ),
                         h(C_CX), h(C_CX + 256), h(C_CX + 512), w[:, C_CDT + 2 * r:C_CDT + 2 * r + 2]], axis=1)
    assert p1.shape[1] == P1_N and p2.shape[1] == P2_N
    return np.ascontiguousarray(p1), np.ascontiguousarray(p2)


def host_wout(inp, l):
    w = np.asarray(inp["w_out"][l], np.float32)
    rows = []
    for rr in range(2):
        for blk in range(4):
            rows.append(w[blk * 256 + rr * 128: blk * 256 + (rr + 1) * 128])
    return np.ascontiguousarray(np.concatenate(rows, axis=0))


import os
DBG = set(os.environ.get("MIXDBG", "sc,gla,ssd,fox,fox2,fox3,b").split(","))
ARENA_UNITS = 212000 // 2


class Builder:
    def __init__(self, nc, stack, S_tok):
        self.nc = nc
        self.S_tok = S_tok
        self.S = Sched(nc, stack)
        self.o = Ops(self.S)
        S = self.S
        self.arena = stack.enter_context(nc.sbuf_tensor("arena", [128, ARENA_UNITS], BF16))
        self.poff = 0
        self.nb = 0
        self.psum = [S.ps([128, TT], F32, "bank%d" % i) for i in range(8)]
        self.pi = 0
        self.ring = list(range(8))
        self.consts()
        self.pbase = self.poff

    def alloc(self, shape, dtype, name):
        n = 1
        for d in shape[1:]:
            n *= d
        units = n * (2 if dtype == F32 else 1)
        units = (units + 15) // 16 * 16
        assert self.poff + units <= ARENA_UNITS, ("SBUF arena overflow", name, self.poff, units)
        lim = getattr(self, "top_limit", None)
        assert lim is None or self.poff >= lim or self.poff + units <= lim, ("collides with prefetched weights", name)
        ap = self.arena[:, self.poff:self.poff + units]
        self.poff += units
        if dtype == F32:
            ap = ap.bitcast(F32)
            ap = ap[:, 0:n]
        else:
            ap = ap[:, 0:n]
        if len(shape) == 3:
            ap = ap.rearrange("p (a b) -> p a b", a=shape[1])
        elif len(shape) == 4:
            ap = ap.rearrange("p (a b c) -> p a b c", a=shape[1], b=shape[2])
        if shape[0] < 128:
            ap = ap[0:shape[0]]
        self.nb += 1
        return Buf(ap, "%s_%d" % (name, self.nb))

    def alloc_at(self, off, shape, dtype, name):
        save = self.poff
        self.poff = off
        b = self.alloc(shape, dtype, name)
        self.poff = save
        return b

    def ffn_weight_bufs(self):
        HF = D_FF // 2
        top = ARENA_UNITS - 4 * NKC * HF
        wg = [self.alloc_at(top + (2 * i) * NKC * HF, [128, NKC, HF], BF16, "wg%d" % i) for i in range(2)]
        wu = [self.alloc_at(top + (2 * i + 1) * NKC * HF, [128, NKC, HF], BF16, "wu%d" % i) for i in range(2)]
        return top, wg, wu

    def load_gate_up(self, wg, wu, wg_ap, wu_ap):
        o = self.o
        HF = D_FF // 2
        wgd, wud = self.S.dram(wg_ap, "wgd"), self.S.dram(wu_ap, "wud")
        for i in range(2):
            for k in range(NKC):
                o.dma(wg[i][:, k, :], V(wgd, wg_ap[k * 128:(k + 1) * 128, i * HF:(i + 1) * HF]), "wA%d" % i, eng="pool")
            for k in range(NKC):
                o.dma(wu[i][:, k, :], V(wud, wu_ap[k * 128:(k + 1) * 128, i * HF:(i + 1) * HF]), "wB%d" % i, eng="pool")

    def prefetch_ffn(self, wg_ap, wu_ap):
        top, wg, wu = self.ffn_weight_bufs()
        self.load_gate_up(wg, wu, wg_ap, wu_ap)
        self.pref = (wg, wu)
        self.top_limit = top
        self.bar_excl = {"dma:wA0", "dma:wA1", "dma:wB0", "dma:wB1"}

    def phase_begin(self):
        self.S.barrier(exclude=getattr(self, "bar_excl", ()))
        self.poff = self.pbase

    def bank(self):
        b = self.psum[self.ring[self.pi % len(self.ring)]]
        self.pi += 1
        return b

    def consts(self):
        o = self.o
        A = self.alloc
        self.ones_mean = {}
        for blk in (1024, 128, 64):
            t = A([128, 128], BF16, "ones%d" % blk)
            o.memset(t[:], 1.0 / blk)
            if blk == 64:
                o.memset(t[0:64, 64:128], 0.0)
                o.memset(t[64:128, 0:64], 0.0)
            self.ones_mean[blk] = t
        self.eps_col = A([128, 1], F32, "epscol")
        o.memset(self.eps_col[:], EPS)
        self.ident_b = A([128, 128], BF16, "identb")
        o.memset(self.ident_b[:], 1.0)
        o.aselect(self.ident_b[:], self.ident_b[:], [[-1, 128]], ALU.is_equal, 0.0, 0, 1)
        self.ident_f = A([128, 128], F32, "identf")
        o.memset(self.ident_f[:], 1.0)
        o.aselect(self.ident_f[:], self.ident_f[:], [[-1, 128]], ALU.is_equal, 0.0, 0, 1)
        self.triU = A([128, 128], F32, "triU")
        o.memset(self.triU[:], 1.0)
        o.aselect(self.triU[:], self.triU[:], [[1, 128]], ALU.is_ge, 0.0, 0, -1)
        self.triU_b = A([128, 128], BF16, "triUb")
        o.copy(self.triU_b[:], self.triU[:], eng="pool")
        self.triG = A([128, 128], F32, "triG")
        o.memset(self.triG[:], -1.0 / 16.0)
        o.aselect(self.triG[:], self.triG[:], [[1, 128]], ALU.is_ge, 0.0, 0, -1)
        self.triS = A([128, 128], F32, "triS")
        o.memset(self.triS[:], 1.0)
        o.aselect(self.triS[:], self.triS[:], [[-1, 128]], ALU.is_gt, 0.0, 0, 1)
        self.maskb = A([128, 128], F32, "maskb")
        o.memset(self.maskb[:], 0.0)
        o.aselect(self.maskb[:], self.maskb[:], [[1, 128]], ALU.is_ge, NEG, 0, -1)
        self.ones_f = A([128, 128], F32, "onesf")
        o.memset(self.ones_f[:], 1.0)
        self.triU4 = A([128, 4, 128], F32, "triU4")
        self.triS4 = A([128, 4, 128], F32, "triS4")
        self.ones4 = A([128, 4, 128], F32, "ones4")
        for s4 in range(4):
            o.copy(self.triU4[:, s4, :], self.triU[:], eng="pool")
            o.copy(self.triS4[:, s4, :], self.triS[:], eng="pool")
        o.memset(self.ones4[:], 1.0)
        self.sel = A([96, 4, 128], BF16, "sel")
        o.memset(self.sel[:], 0.0)
        for h in range(4):
            v = self.sel[0:96, h, :]
            o.ts(v, self.ones_f[0:96, :], self.ident_f[0:96, h:h + 1], None, ALU.mult, eng="pool")
            o.stt(v, self.ones_f[0:96, :], self.ident_f[0:96, 32 + h:33 + h], v, ALU.mult, ALU.add, eng="pool")
            o.stt(v, self.ones_f[0:96, :], self.ident_f[0:96, 64 + h:65 + h], v, ALU.mult, ALU.add, eng="pool")
        self.wden = A([65, 64], BF16, "wden")
        o.memset(self.wden[0:65, :], EPS)
        o.memset(self.wden[0:64, :], 1.0 / 64)

    def load_weight(self, dst, w_ap, nk, ncols, chan, rows0=0):
        wd = self.S.dram(w_ap, "wdram")
        for k in range(nk):
            src = w_ap[rows0 + k * 128: rows0 + (k + 1) * 128, :]
            self.o.dma(dst[:, k, :], V(wd, src), chan, eng="pool")

    def load_tables(self, cols_ap, tm_ap):
        self.cols = self.alloc([128, NCOLT], F32, "cols")
        self.tm = self.alloc([128, NTM], F32, "tm")
        self.o.dma(self.cols[:], V(self.S.dram(cols_ap, "colsd"), cols_ap), "cols")
        self.o.dma(self.tm[:], V(self.S.dram(tm_ap, "tmd"), tm_ap), "tm")

    def alloc_xnorm(self):
        self.xt = self.alloc([128, NKC, TT], F32, "xt")
        self.ht = self.alloc([128, NKC, TT], BF16, "ht")
        self.sq = self.alloc([128, 2, TT], BF16, "sq")
        self.rstd = self.alloc([128, TT], F32, "rstd")

    def rmsnorm_tile(self, kcol):
        o = self.o
        ps = self.bank()
        for c in range(NKC):
            o.act(self.sq[:, c % 2, :], self.xt[:, c, :], AF.Square)
            o.mm1(ps[:], self.ones_mean[1024][:], self.sq[:, c % 2, :], c == 0, c == NKC - 1)
        o.act(self.rstd[:], ps[:], AF.Sqrt, bias=self.eps_col[:])
        o.recip(self.rstd[:], self.rstd[:])
        for c in range(NKC):
            o.stt(self.ht[:, c, :], self.xt[:, c, :], self.cols[:, kcol + c:kcol + c + 1], self.rstd[:], ALU.mult, ALU.mult,
                  eng="dve" if c % 2 == 0 else "pool")

    def group_rms(self, y, blk, gain, out, tmp, post_mul=None, post_scale=None):
        o = self.o
        sqb = self.gsq
        o.act(sqb[:], y, AF.Square)
        ps = self.bank()
        o.mm(ps[:], [(self.ones_mean[blk][:], sqb[:])])
        o.act(self.grs[:], ps[:], AF.Sqrt, bias=self.eps_col[:])
        o.recip(self.grs[:], self.grs[:])
        if post_mul is None and post_scale is None:
            o.stt(out, y, gain, self.grs[:], ALU.mult, ALU.mult)
        else:
            o.stt(tmp, y, gain, self.grs[:], ALU.mult, ALU.mult)
            if post_mul is not None:
                o.tt(out, tmp, post_mul, ALU.mult)
            else:
                o.ts(out, tmp, post_scale, None, ALU.mult)

    def ffn_phase(self, x_in, x_out, cols_ap, tm_ap, kcol, wg_ap, wu_ap, wd_ap, h_out=None, h_gather=None):
        o = self.o
        self.phase_begin()
        self.ring = list(range(7))
        pstat = self.psum[7]
        HF = D_FF // 2
        NJ = NFC // 2
        wdn = self.alloc([128, NFC, D_MODEL], BF16, "wd")
        self.load_tables(cols_ap, tm_ap)
        if getattr(self, "pref", None) is not None:
            wg, wu = self.pref
            self.pref = None
            self.bar_excl = set()
        else:
            top, wg, wu = self.ffn_weight_bufs()
            self.top_limit = top
            self.load_gate_up(wg, wu, wg_ap, wu_ap)
        self.load_weight(wdn, wd_ap, NFC, D_MODEL, "wC")
        xc = [self.alloc([128, TT], F32, "xc%d" % c) for c in range(NKC)]
        self.ht = self.alloc([128, NKC, TT], BF16, "ht")
        self.sq = self.alloc([128, 2, TT], BF16, "sq")
        self.rstd = self.alloc([128, TT], F32, "rstd")
        act_t = self.alloc([128, NFC, TT], BF16, "actT")
        sg = [self.alloc([128, TT], F32, "sg%d" % i) for i in range(2)]
        xin_r = x_in.ap.rearrange("(c p) s -> p c s", p=128)
        xout_r = x_out.ap.rearrange("(c p) s -> p c s", p=128)
        nsq = 0

        def finish_norm(kc):
            o.act(self.rstd[:], pstat[:], AF.Sqrt, bias=self.eps_col[:])
            o.recip(self.rstd[:], self.rstd[:])
            for c in range(NKC):
                o.stt(self.ht[:, c, :], xc[c][:], self.cols[:, kc + c:kc + c + 1], self.rstd[:], ALU.mult, ALU.mult)

        def stat(c):
            nonlocal nsq
            sqv = self.sq[:, nsq % 2, :]
            nsq += 1
            o.act(sqv, xc[c][:], AF.Square)
            o.mm1(pstat[:], self.ones_mean[1024][:], sqv, c == 0, c == NKC - 1)

        NTL = self.S_tok // TT
        for c in range(NKC):
            o.dma(xc[c][:], V(x_in, xin_r[:, c, 0:TT]), "xt%d" % c)
        for t in range(NTL):
            ts = slice(t * TT, (t + 1) * TT)
            for c in range(NKC):
                stat(c)
            finish_norm(kcol)
            for j in range(NFC):
                pg = self.bank()
                pu = self.bank()
                i, jj = j // NJ, j % NJ
                o.mm(pg[:], [(wg[i][:, k, jj * 128:(jj + 1) * 128], self.ht[:, k, :]) for k in range(NKC)])
                o.mm(pu[:], [(wu[i][:, k, jj * 128:(jj + 1) * 128], self.ht[:, k, :]) for k in range(NKC)])
                o.act(sg[j % 2][:], pg[:], AF.Silu)
                o.tt(act_t[:, j, :], sg[j % 2][:], pu[:], ALU.mult)
            for c in range(NKC):
                py = self.bank()
                o.mm(py[:], [(wdn[:, j, c * 128:(c + 1) * 128], act_t[:, j, :]) for j in range(NFC)])
                o.stt(xc[c][:], py[:], 0.5, xc[c][:], ALU.mult, ALU.add)
                o.dma(V(x_out, xout_r[:, c, ts]), xc[c][:], "xo%d" % c)
                if h_out is not None:
                    stat(c)
                elif t + 1 < NTL:
                    o.dma(xc[c][:], V(x_in, xin_r[:, c, (t + 1) * TT:(t + 2) * TT]), "xt%d" % c)
            if h_out is not None:
                finish_norm(K_MIX)
                hb, hv = h_out[(t * TT) // CH]
                c0 = (t * TT) % CH
                o.dma(V(hb, hv.rearrange("(c p) s -> p c s", p=128)[:, :, c0:c0 + TT]), self.ht[:], "ho")
                if t + 1 < NTL:
                    for c in range(NKC):
                        o.dma(xc[c][:], V(x_in, xin_r[:, c, (t + 1) * TT:(t + 2) * TT]), "xt%d" % c)
                if c0 + TT == CH and h_gather is not None:
                    k_ = (t * TT) // CH
                    self.all_gather(h_out[k_][0], h_gather[k_][0])
        self.top_limit = None

    def all_gather(self, src, dst):
        self.S.op("pool", lambda e: e.collective_compute("AllGather", ALU.bypass, replica_groups=GROUPS,
                                                         ins=[src.ap.opt()], outs=[dst.ap.opt()]),
                  reads=[src], writes=[dst], dma="ccag")

    def mix_a(self, part, hfull, cols_ap, tm_ap, w_ap, wup_ap, scr):
        o = self.o
        A = self.alloc
        NT = SEQ // TT
        self.phase_begin()
        self.ring = [1, 2, 3, 4, 5, 6, 7]
        acc = self.psum[0]
        VA_, EB_ = (A([128, SEQ // 128, 2, 66], BF16, "VA"), A([128, SEQ // 128, 2], F32, "EB"))
        if part == 2:
            self.VA, self.EB = VA_, EB_
        self.mixb_base = self.poff
        NW = P1_N if part == 1 else P2_N
        win = A([128, NKC, NW], BF16, "win")
        self.load_weight(win, w_ap, NKC, NW, "wA")
        self.load_tables(cols_ap, tm_ap)
        cols, tm = self.cols, self.tm
        hts = [A([128, NKC, TT], BF16, "ht%d" % i) for i in range(2)]
        self.gsq = A([128, TT], BF16, "gsq")
        self.grs = A([128, TT], F32, "grs")
        T = [A([128, TT], F32, "T%d" % i) for i in range(6)]
        Yt = A([128, 4, TT], BF16, "Yt")
        sm = [A([128, 2], F32, "sm%d" % i) for i in range(6)]
        Q = [A([128, 128], F32, "Q%d" % i) for i in range(4)]
        Qb = [A([128, 128], BF16, "Qb%d" % i) for i in range(4)]
        if part == 1:
            scu = A([128, TT + 2], F32, "scu")
            o.memset(scu[:, 0:2], 0.0)
            wup = A([16, 128], F32, "wup")
            o.dma(wup[:], V(self.S.dram(wup_ap, "wupd"), wup_ap), "wup")
            GS32 = A([128, 64], F32, "GS32")
            GSb = A([128, 64], BF16, "GSb")
            o.memset(GS32[:], 0.0)
            o.memset(GSb[:], 0.0)
            la = A([128, 4, 128], F32, "la")
            vtm = A([128, 4, 128], BF16, "vtm")
            qdb = A([128, TT], BF16, "qdb")
            kdb = A([128, TT], BF16, "kdb")
            kdt = A([128, 4, 128], BF16, "kdt")
            attm = [A([128, 4, 128], BF16, "attm%d" % i) for i in range(2)]
            glr = A([32, TT], F32, "glr")
        else:
            o.memset(self.VA[:], 1.0)
            aneg4 = A([128, 8], F32, "aneg4")
            o.act(aneg4[:], tm[:, M_ALOG4:M_ALOG4 + 8], AF.Exp)
            acs4 = A([128, 8], F32, "acs4")
            dte4 = A([128, 8], F32, "dte4")
            eal4 = A([128, 8], F32, "eal4")
            Xd4 = A([128, 4, 128], BF16, "Xd4")
            Xdd4 = A([128, 4, 128], BF16, "Xdd4")
            Btm4 = A([128, 4, 128], BF16, "Btm4")
            lseg4 = [A([128, 4, 128], F32, "lseg4_%d" % h) for h in range(2)]
            abc4 = [A([128, 4, 128], F32, "abc4_%d" % h) for h in range(2)]
            LT4 = [A([128, TT], F32, "LT4_%d" % h) for h in range(2)]
            Ecs4 = [A([128, TT], F32, "Ecs4_%d" % h) for h in range(2)]
            WT4 = [A([128, TT], BF16, "WT4_%d" % h) for h in range(2)]
            CsT4 = [A([128, TT], BF16, "CsT4_%d" % h) for h in range(2)]
            raw = [A([128, TT + 3], F32, "raw%d" % c) for c in range(3)]
            for c in range(3):
                o.memset(raw[c][:, 0:3], 0.0)
            xs32 = A([128, TT], F32, "xs32")
            xsb = A([128, TT], BF16, "xsb")
            BT = A([128, TT], BF16, "BT")
            CT = A([128, TT], BF16, "CT")
            zs = A([128, TT], F32, "zs")
            dt_tm = A([128, 4, 2], F32, "dt_tm")
            a_tm = A([128, 4, 2], F32, "a_tm")
            Xd = A([128, 128], BF16, "Xd")
            Xdd = A([128, 128], BF16, "Xdd")
            Btm = A([128, 128], BF16, "Btm")
            ST32 = [A([128, 64], F32, "ST32_%d" % h) for h in range(2)]
            STb = [A([128, 64], BF16, "STb%d" % h) for h in range(2)]
            for h in range(2):
                o.memset(ST32[h][:], 0.0)
                o.memset(STb[h][:], 0.0)
            gcar_bc = A([128, 2], F32, "gcar_bc")
            gcar_T = A([96, 1], F32, "gcar_T")
            o.memset(gcar_bc[:], 0.0)
            o.memset(gcar_T[:], 0.0)
            nl = A([128, 4, 2], F32, "nl")
            nl3 = A([128, 4, 96], F32, "nl3")
            o.memset(nl3[:], 0.0)
            Fp = A([96, TT], BF16, "Fp")
            F1 = [A([96, TT], F32, "F1_%d" % i) for i in range(3)]
            F1b = [A([96, TT], BF16, "F1b_%d" % i) for i in range(3)]
            qb = A([128, TT], BF16, "qb")
            kb = A([128, TT], BF16, "kb")
            o.memset(Fp[:], 0.0)

        hrs = [(b_, v_.rearrange("(r c p) s -> p r c s", r=2, p=128)) for b_, v_ in hfull]
        Yrs = [(b_, v_.rearrange("(hf c p) s -> p hf c s", hf=2, p=128)) for b_, v_ in scr["Y"]]

        def proj_fm(col0, n=128):
            ps = self.bank()
            o.mm(ps[0:n, :], [(win[:, k, col0:col0 + n], self.ht[:, k, :]) for k in range(NKC)])
            return ps

        def proj_tm(sub, col0, n):
            ps = self.bank()
            o.mm(ps[:, 0:n], [(self.ht[:, k, sub * 128:(sub + 1) * 128], win[:, k, col0:col0 + n]) for k in range(NKC)])
            return ps

        def load_h(t):
            hf_, tl_ = t // (NT // 2), t % (NT // 2)
            hfb_, hr_ = hrs[(tl_ * TT) // CH]
            c0_ = (tl_ * TT) % CH
            o.dma(hts[t % 2][:], V(hfb_, hr_[:, hf_, :, c0_:c0_ + TT]), "ht%d" % (t % 2))
        load_h(0)
        for t in range(NT):
            ts_ = slice(t * TT, (t + 1) * TT)
            hf, tl = t // (NT // 2), t % (NT // 2)
            ck = (tl * TT) // CH
            tsl = slice((tl * TT) % CH, (tl * TT) % CH + TT)
            Yb_, Yr = Yrs[ck]
            self.ht = hts[t % 2]
            if t + 1 < NT:
                load_h(t + 1)
            if part == 1:
                pc = proj_fm(P1_DC)
                o.copy(T[0][:], pc[:], eng="act")
                pv = proj_fm(P1_DV)
                o.tt(scu[:, 2:TT + 2], T[0][:], pv[:], ALU.mult)
                o.ts(T[1][:], scu[:, 0:TT], cols[:, K_SCW:K_SCW + 1], None, ALU.mult)
                o.stt(T[1][:], scu[:, 1:TT + 1], cols[:, K_SCW + 1:K_SCW + 2], T[1][:], ALU.mult, ALU.add)
                o.stt(T[1][:], scu[:, 2:TT + 2], cols[:, K_SCW + 2:K_SCW + 3], T[1][:], ALU.mult, ALU.add)
                o.copy(scu[:, 0:2], scu[:, TT:TT + 2], eng="pool")
                pb = proj_fm(P1_DB)
                o.tt(T[2][:], T[1][:], pb[:], ALU.mult)
                self.group_rms(T[2][:], 64, cols[:, K_SCN:K_SCN + 1], Yt[:, 3, :], None)
                pl = proj_fm(P1_GLR, 16)
                o.copy(glr[0:16, :], pl[0:16, :], eng="act")
                PZ = self.bank()
                for s_ in range(4):
                    o.mm(PZ[:, s_ * 128:(s_ + 1) * 128], [(glr[0:16, s_ * 128:(s_ + 1) * 128], wup[:])])
                laf = la.ap.rearrange("p a b -> p (a b)")
                laf = V(la, laf)
                o.tt(laf, PZ[:], tm[:, M_BG4:M_BG4 + 512], ALU.add)
                o.act(laf, laf, AF.Exp, scale=-1.0)
                o.act(laf, laf, AF.Ln, bias=1.0)
                PV_ = self.bank()
                for s_ in range(4):
                    o.mm(PV_[:, s_ * 128:(s_ + 1) * 128],
                         [(self.ht[:, k, s_ * 128:(s_ + 1) * 128], win[:, k, P1_GV:P1_GV + 128]) for k in range(NKC)])
                o.copy(V(vtm, vtm.ap.rearrange("p a b -> p (a b)")), PV_[:], eng="act")
                pq = proj_fm(P1_GQ)
                o.copy(T[0][:], pq[:], eng="act")
                pk = proj_fm(P1_GK)
                o.copy(T[1][:], pk[:], eng="act")
                pgo = proj_fm(P1_GG)
                o.act(T[2][:], pgo[:], AF.Silu)
                PB = self.bank()
                for s_ in range(4):
                    o.mm(PB[:, s_ * 128:(s_ + 1) * 128], [(la[:, s_, :], self.triG[:])])
                eb, enb = T[5], T[4]
                o.act(eb[:], PB[:], AF.Exp)
                o.act(enb[:], PB[:], AF.Exp, scale=-1.0)
                o.stt(qdb[:], T[0][:], 0.125, eb[:], ALU.mult, ALU.mult)
                o.tt(kdb[:], T[1][:], enb[:], ALU.mult)
                PTr = self.bank()
                PTb = V(PTr, PTr.ap[:, 0:256].bitcast(BF16))
                for s_ in range(4):
                    o.transpose(V(PTr, PTb.ap[:, s_ * 128:(s_ + 1) * 128]), kdb[:, s_ * 128:(s_ + 1) * 128], self.ident_b[:])
                o.copy(V(kdt, kdt.ap.rearrange("p a b -> p (a b)")), PTb)
                for hh in range(2):
                    R = slice(hh * 64, (hh + 1) * 64)
                    PA = self.bank()
                    for s_ in range(4):
                        cs = slice(s_ * 128, (s_ + 1) * 128)
                        o.mm(PA[:, cs], [(kdb[R, cs], qdb[R, cs])])
                    o.tt(V(attm[hh], attm[hh].ap.rearrange("p a b -> p (a b)")), PA[:],
                         V(self.triU4, self.triU4.ap.rearrange("p a b -> p (a b)")), ALU.mult)
                PM = self.bank()
                for s_ in range(4):
                    for hh in range(2):
                        R = slice(hh * 64, (hh + 1) * 64)
                        o.mm(PM[R, s_ * 64:(s_ + 1) * 64], [(kdt[:, s_, R], vtm[:, s_, hh * 64:(hh + 1) * 64])])
                for s_ in range(4):
                    cs = slice(s_ * 128, (s_ + 1) * 128)
                    for hh in range(2):
                        R = slice(hh * 64, (hh + 1) * 64)
                        o.mm(acc[R, cs], [(vtm[:, s_, hh * 64:(hh + 1) * 64], attm[hh][:, s_, :]), (GSb[R, :], qdb[R, cs])])
                    eL = eb[:, s_ * 128 + 127:s_ * 128 + 128]
                    o.ts(GS32[:], GS32[:], eL, None, ALU.mult)
                    o.stt(GS32[:], PM[:, s_ * 64:(s_ + 1) * 64], eL, GS32[:], ALU.mult, ALU.add)
                    o.copy(GSb[:], GS32[:], eng="pool")
                o.copy(T[3][:], acc[:], eng="act")
                self.group_rms(T[3][:], 64, cols[:, K_GLAN:K_GLAN + 1], Yt[:, 0, :], T[4][:], post_mul=T[2][:])
                o.dma(V(Yb_, Yr[:, hf, 0, tsl]), Yt[:, 0, :], "yo")
                o.dma(V(Yb_, Yr[:, hf, 3, tsl]), Yt[:, 3, :], "yo2")
            else:
                for c in range(3):
                    pr = proj_fm(P2_CX + c * 128)
                    o.copy(raw[c][:, 3:TT + 3], pr[:], eng="act")
                    tc_ = T[0] if c % 2 == 0 else T[1]
                    o.ts(tc_[:], raw[c][:, 0:TT], cols[:, K_CW + c:K_CW + c + 1], None, ALU.mult)
                    for k in range(1, 4):
                        o.stt(tc_[:], raw[c][:, k:TT + k], cols[:, K_CW + 3 * k + c:K_CW + 3 * k + c + 1], tc_[:], ALU.mult, ALU.add)
                    o.copy(raw[c][:, 0:3], raw[c][:, TT:TT + 3], eng="pool")
                    bcol = cols[:, K_CB + c:K_CB + c + 1]
                    if c == 0:
                        o.act(xs32[:], tc_[:], AF.Silu, bias=bcol)
                        o.copy(xsb[:], xs32[:], eng="pool")
                    elif c == 1:
                        o.act(BT[:], tc_[:], AF.Silu, bias=bcol)
                    else:
                        o.act(CT[:], tc_[:], AF.Silu, bias=bcol)
                pq = proj_fm(P2_BQ)
                o.copy(T[5][:], pq[:], eng="act")
                self.group_rms(T[5][:], 64, cols[:, K_FQN:K_FQN + 1], qb[:], T[4][:], post_scale=0.125)
                o.dma(V(scr["Q"], scr["Q"].ap[:, ts_]), qb[:], "qo")
                pk = proj_fm(P2_BK)
                o.copy(T[3][:], pk[:], eng="act")
                self.group_rms(T[3][:], 64, cols[:, K_FKN:K_FKN + 1], kb[:], None)
                o.dma(V(scr["K"], scr["K"].ap[:, ts_]), kb[:], "ko")
                for s_ in range(4):
                    kt = t * 4 + s_
                    pv = proj_tm(s_, P2_BV, 130)
                    for h in range(2):
                        o.copy(self.VA[:, kt, h, 0:64], pv[:, h * 64:(h + 1) * 64], eng="act" if h % 2 == 0 else "dve")
                    o.tt(nl[:, s_, :], pv[:, 128:130], tm[:, M_FB:M_FB + 2], ALU.add)
                    o.act(nl[:, s_, :], nl[:, s_, :], AF.Exp, scale=-1.0)
                    o.act(nl[:, s_, :], nl[:, s_, :], AF.Ln, bias=1.0)
                    for g in range(3):
                        o.copy(nl3[:, s_, g * 32:g * 32 + 2], nl[:, s_, :], eng="pool")
                    pg_ = self.bank()
                    o.mm(pg_[:, 0:2], [(self.ones_f[:], nl[:, s2, :]) for s2 in range(s_)] + [(self.triU[:], nl[:, s_, :])])
                    o.tt(sm[3][:], pg_[:, 0:2], gcar_bc[:], ALU.add)
                    o.ts(self.EB[:, kt, :], sm[3][:], -FOX_SHIFT, None, ALU.add)
                ptot = self.bank()
                o.mm(ptot[:, 0:2], [(self.ones_f[:], nl[:, s2, :]) for s2 in range(4)])
                o.tt(gcar_bc[:], gcar_bc[:], ptot[:, 0:2], ALU.add)
                pgt = self.bank()
                for s_ in range(4):
                    cs = slice(s_ * 128, (s_ + 1) * 128)
                    o.mm(pgt[0:96, cs], [(nl3[:, s2, :], self.ones_f[:]) for s2 in range(s_)] + [(nl3[:, s_, :], self.triU[:])])
                GT = F1[0]
                o.ts(GT[:], pgt[0:96, :], gcar_T[:, 0:1], None, ALU.add)
                o.copy(gcar_T[:], GT[:, TT - 1:TT])
                hi, mid, lo = F1b[0], F1b[1], F1b[2]
                r1, r2 = F1[1], F1[2]
                o.ts(hi[:], GT[:], -1.0, None, ALU.mult)
                o.stt(r1[:], GT[:], -1.0, hi[:], ALU.mult, ALU.subtract)
                o.copy(mid[:], r1[:])
                o.tt(r2[:], r1[:], mid[:], ALU.subtract)
                o.copy(lo[:], r2[:])
                o.copy(Fp[0:4, :], hi[0:4, :])
                o.copy(Fp[32:36, :], mid[32:36, :])
                o.copy(Fp[64:68, :], lo[64:68, :])
                o.dma(V(scr["FP"], scr["FP"].ap[:, ts_]), Fp[:], "fpo")
                pz = proj_fm(P2_CZ)
                o.act(zs[:], pz[:], AF.Silu)
                PD = self.bank()
                for s_ in range(4):
                    o.mm(PD[:, s_ * 2:(s_ + 1) * 2],
                         [(self.ht[:, k, s_ * 128:(s_ + 1) * 128], win[:, k, P2_CDT:P2_CDT + 2]) for k in range(NKC)])
                dt4 = V(dt_tm, dt_tm.ap.rearrange("p a b -> p (a b)"))
                a4 = V(a_tm, a_tm.ap.rearrange("p a b -> p (a b)"))
                o.tt(dt4, PD[:, 0:8], tm[:, M_DTB4:M_DTB4 + 8], ALU.add)
                o.act(dt4, dt4, AF.Exp)
                o.act(dt4, dt4, AF.Ln, bias=1.0)
                o.stt(a4, dt4, -1.0, aneg4[:], ALU.mult, ALU.mult)
                PCS = self.bank()
                o.mm(PCS[:, 0:8], [(self.triU[:], a4)])
                o.mm(PCS[:, 8:16], [(self.ones_f[:], a4)])
                o.copy(acs4[:], PCS[:, 0:8])
                o.tt(dte4[:], PCS[:, 8:16], acs4[:], ALU.subtract)
                o.act(dte4[:], dte4[:], AF.Exp)
                o.act(eal4[:], PCS[:, 8:16], AF.Exp)
                PX = self.bank()
                PXb = V(PX, PX.ap[:, 0:256].bitcast(BF16))
                for s_ in range(4):
                    o.transpose(V(PX, PXb.ap[:, s_ * 128:(s_ + 1) * 128]), xsb[:, s_ * 128:(s_ + 1) * 128], self.ident_b[:])
                Xd8 = V(Xd4, Xd4.ap.rearrange("p a (h e) -> p (a h) e", h=2))
                Xdd8 = V(Xdd4, Xdd4.ap.rearrange("p a (h e) -> p (a h) e", h=2))
                o.tt(Xd8, V(PX, PXb.ap.rearrange("p (g e) -> p g e", e=64)),
                     V(dt_tm, dt4.ap.unsqueeze(2).broadcast_to([128, 8, 64])), ALU.mult)
                o.tt(Xdd8, Xd8, V(dte4, dte4.ap.unsqueeze(2).broadcast_to([128, 8, 64])), ALU.mult, eng="pool")
                PBt = self.bank()
                PBb = V(PBt, PBt.ap[:, 0:256].bitcast(BF16))
                for s_ in range(4):
                    o.transpose(V(PBt, PBb.ap[:, s_ * 128:(s_ + 1) * 128]), BT[:, s_ * 128:(s_ + 1) * 128], self.ident_b[:])
                o.copy(V(Btm4, Btm4.ap.rearrange("p a b -> p (a b)")), PBb)
                PSC = self.bank()
                for s_ in range(4):
                    cs = slice(s_ * 128, (s_ + 1) * 128)
                    o.mm(PSC[:, cs], [(BT[:, cs], CT[:, cs])])
                for h in range(2):
                    ah = V(a_tm, a_tm.ap[:, :, h:h + 1].broadcast_to([128, 4, 128]))
                    o.tt(lseg4[h][:], self.triS4[:], ah, ALU.mult, eng="pool")
                    o.tt(abc4[h][:], self.ones4[:], ah, ALU.mult, eng="pool")
                    PSEG = self.bank()
                    PAB = self.bank()
                    for s_ in range(4):
                        cs = slice(s_ * 128, (s_ + 1) * 128)
                        o.mm(PSEG[:, cs], [(lseg4[h][:, s_, :], self.triU[:]), (self.ident_f[:], self.maskb[:])])
                        o.mm(PAB[:, cs], [(abc4[h][:, s_, :], self.triU[:])])
                    o.act(LT4[h][:], PSEG[:], AF.Exp)
                    o.tt(WT4[h][:], PSC[:], LT4[h][:], ALU.mult)
                    o.act(Ecs4[h][:], PAB[:], AF.Exp)
                    o.tt(CsT4[h][:], CT[:], Ecs4[h][:], ALU.mult)
                for s_ in range(4):
                    cs = slice(s_ * 128, (s_ + 1) * 128)
                    for h in range(2):
                        hc = slice(h * 64, (h + 1) * 64)
                        o.mm(acc[hc, cs], [(Xd4[:, s_, hc], WT4[h][:, cs]), (STb[h][:], CsT4[h][:, cs])])
                        pst = self.bank()
                        o.mm(pst[:, 0:64], [(Btm4[:, s_, :], Xdd4[:, s_, hc])])
                        o.stt(ST32[h][:], ST32[h][:], eal4[:, s_ * 2 + h:s_ * 2 + h + 1], pst[:, 0:64], ALU.mult, ALU.add)
                        o.copy(STb[h][:], ST32[h][:], eng="pool")
                o.stt(T[2][:], xs32[:], cols[:, K_D:K_D + 1], acc[:], ALU.mult, ALU.add)
                o.tt(T[3][:], T[2][:], zs[:], ALU.mult)
                self.group_rms(T[3][:], 128, cols[:, K_SN:K_SN + 1], Yt[:, 2, :], None)
                o.dma(V(Yb_, Yr[:, hf, 2, tsl]), Yt[:, 2, :], "yo")

    def mix_b(self, cols_ap, scr, y_gather=None, prefetch=None):
        o = self.o
        A = self.alloc
        NT = SEQ // TT
        self.S.barrier()
        self.poff = self.mixb_base
        if prefetch is not None:
            self.prefetch_ffn(*prefetch)
        self.ring = [2, 3, 4, 5, 6, 7]
        oacc = [self.psum[0], self.psum[1]]
        cols = A([128, NCOLT], F32, "cols_b")
        o.dma(cols[:], V(self.S.dram(cols_ap, "colsd2"), cols_ap), "cols")
        KTh = [A([67, SEQ], BF16, "KTh%d" % h) for h in range(2)]
        for h in range(2):
            o.memset(KTh[h][64:67, :], 1.0)
            for q4 in range(4):
                qs = slice(q4 * (SEQ // 4), (q4 + 1) * (SEQ // 4))
                o.dma(KTh[h][0:64, qs], V(scr["K"], scr["K"].ap[h * 64:(h + 1) * 64, qs]), "kth%d" % h)
        Qh = [[A([67, TT], BF16, "Qh%d_%d" % (i, h)) for h in range(2)] for i in range(2)]
        FPr = scr["FP"].ap.rearrange("(g x) s -> x g s", x=32)
        PT = [A([128, TT], BF16, "PT%d" % i) for i in range(4)]
        Oa = A([65, TT], F32, "Oa")
        Osq = A([65, TT], BF16, "Osq")
        rs = A([64, TT], F32, "rs_b")
        Yb = [A([64, TT], BF16, "Yb%d" % h) for h in range(4)]
        Yrows = [(b_, v_.rearrange("(hf c g p) s -> p hf c g s", hf=2, c=4, g=2)) for b_, v_ in scr["Y"]]
        LAG = 2
        npt = 0
        nyb = 0

        def load_qf(qt):
            ts2 = slice(qt * TT, (qt + 1) * TT)
            for h in range(2):
                o.dma(Qh[qt % 2][h][0:64, :], V(scr["Q"], scr["Q"].ap[h * 64:(h + 1) * 64, ts2]), "qi%d_%d" % (qt % 2, h))
                o.dma(Qh[qt % 2][h][64:67, :], V(scr["FP"], FPr[h, :, ts2]), "qi%d_%d" % (qt % 2, h))
        load_qf(0)
        for qt in range(NT):
            hf, tl = qt // (NT // 2), qt % (NT // 2)
            tsl = slice((tl * TT) % CH, (tl * TT) % CH + TT)
            Ybuf, Yrow = Yrows[(tl * TT) // CH]
            if qt + 1 < NT:
                load_qf(qt + 1)
            Qc = Qh[qt % 2]
            nk = (qt + 1) * 4
            blocks = [(h, kt) for h in range(2) for kt in range(nk)]
            pend = []

            def stage2(item):
                nonlocal nyb
                h, kt, col0, pt = item
                oa = oacc[h % 2]
                o.mm1(oa[0:65, col0:TT], self.VA[:, kt, h, 0:65], pt[:, col0:TT], kt == 0, kt == nk - 1)
                if kt == nk - 1:
                    o.copy(Oa[:], oa[0:65, :], eng="act")
                    o.act(Osq[:], Oa[:], AF.Square)
                    pd = self.bank()
                    o.mm(pd[0:64, :], [(self.wden[:], Osq[:])])
                    o.act(rs[:], pd[0:64, :], AF.Sqrt)
                    o.recip(rs[:], rs[:])
                    yb = Yb[nyb % 4]
                    nyb += 1
                    o.stt(yb[:], Oa[0:64, :], cols[0:64, K_FON + h:K_FON + h + 1], rs[:], ALU.mult, ALU.mult)
                    o.dma(V(Ybuf, Yrow[:, hf, 1, h, tsl]), yb[:], "ybo%d" % (nyb % 4))

            for (h, kt) in blocks:
                R = slice(h * 64, (h + 1) * 64)
                c = kt - qt * 4
                col0 = max(c, 0) * 128
                sT = self.bank()
                o.mm(sT[:, col0:TT], [(KTh[h][:, kt * 128:(kt + 1) * 128], Qc[h][:, col0:TT])])
                if c >= 0:
                    o.tt(sT[:, col0:col0 + 128], sT[:, col0:col0 + 128], self.maskb[:], ALU.add)
                pt = PT[npt % 4]
                npt += 1
                o.act(pt[:, col0:TT], sT[:, col0:TT], AF.Exp, bias=self.EB[:, kt, h:h + 1])
                pend.append((h, kt, col0, pt))
                if len(pend) > LAG:
                    stage2(pend.pop(0))
            while pend:
                stage2(pend.pop(0))
            if y_gather is not None and hf == 1 and (tl * TT) % CH + TT == CH:
                k_ = (tl * TT) // CH
                self.all_gather(scr["Y"][k_][0], y_gather[k_][0])

    def outproj_phase(self, x_in, x_out, yfull, cols_ap, tm_ap, w_out_ap):
        o = self.o
        A = self.alloc
        self.phase_begin()
        self.ring = list(range(8))
        wo = A([128, NKC, D_MODEL], BF16, "wo")
        self.load_weight(wo, w_out_ap, NKC, D_MODEL, "wA")
        self.load_tables(cols_ap, tm_ap)
        cols = self.cols
        xts = [A([128, NKC, TT], F32, "xt_o%d" % i) for i in range(2)]
        Y0s = [A([128, NKC, TT], BF16, "Y0_%d" % i) for i in range(2)]
        Y1s = [A([128, NKC, TT], BF16, "Y1_%d" % i) for i in range(2)]
        Ys = A([128, NKC, TT], BF16, "Ys")
        xin_r = x_in.ap.rearrange("(c p) s -> p c s", p=128)
        xout_r = x_out.ap.rearrange("(c p) s -> p c s", p=128)
        yrs = [(b_, v_.rearrange("(r hf c p) s -> p hf r c s", r=2, hf=2, p=128)) for b_, v_ in yfull]

        def loads(t):
            i = t % 2
            o.dma(xts[i][:], V(x_in, xin_r[:, :, t * TT:(t + 1) * TT]), "xt%d" % i)
            yfb, yr = yrs[(t * TT) // CH]
            tsc = slice((t * TT) % CH, (t * TT) % CH + TT)
            for rr in range(2):
                o.dma(Y0s[i][:, rr * 4:(rr + 1) * 4, :], V(yfb, yr[:, 0, rr, :, tsc]), "y0%d" % i)
                o.dma(Y1s[i][:, rr * 4:(rr + 1) * 4, :], V(yfb, yr[:, 1, rr, :, tsc]), "y1%d" % i)
        loads(0)
        NTL = HALF // TT
        for t in range(NTL):
            ts_ = slice(t * TT, (t + 1) * TT)
            if t + 1 < NTL:
                loads(t + 1)
            xt, Y0, Y1 = xts[t % 2], Y0s[t % 2], Y1s[t % 2]
            o.ts(Ys[:], Y0[:], cols[:, K_SEL:K_SEL + 1], None, ALU.mult)
            o.stt(Ys[:], Y1[:], cols[:, K_SEL + 1:K_SEL + 2], Ys[:], ALU.mult, ALU.add)
            for oc in range(NKC):
                py = self.bank()
                o.mm(py[:], [(wo[:, k, oc * 128:(oc + 1) * 128], Ys[:, k, :]) for k in range(NKC)])
                o.tt(xt[:, oc, :], xt[:, oc, :], py[:], ALU.add)
            o.dma(V(x_out, xout_r[:, :, ts_]), xt[:], "xo%d" % (t % 2))


def build_program(depth, debug=None):
    nc = bass.Bass("TRN2", target_bir_lowering=False)
    stack = ExitStack()

    def din(name, shape, dt=F32):
        return nc.dram_tensor(name, list(shape), dt, kind="ExternalInput").ap()
    xT = din("xT", [D_MODEL, HALF])
    w = {}
    for l in range(depth):
        for f in (1, 2):
            w["g%d_%d" % (f, l)] = din("ffn%d_w_gate_%d" % (f, l), [D_MODEL, D_FF])
            w["u%d_%d" % (f, l)] = din("ffn%d_w_up_%d" % (f, l), [D_MODEL, D_FF])
            w["d%d_%d" % (f, l)] = din("ffn%d_w_down_%d" % (f, l), [D_FF, D_MODEL])
        w["win1_%d" % l] = din("w_in1_%d" % l, [D_MODEL, P1_N])
        w["win2_%d" % l] = din("w_in2_%d" % l, [D_MODEL, P2_N])
        w["wout_%d" % l] = din("w_out_%d" % l, [D_MODEL, D_MODEL])
        w["wup_%d" % l] = din("gla_wup_%d" % l, [16, 128])
        w["cols_%d" % l] = din("cols_%d" % l, [128, NCOLT])
        w["tm_%d" % l] = din("tm_%d" % l, [128, NTM])
    outT = nc.dram_tensor("outT", [D_MODEL, HALF], F32, kind="ExternalOutput").ap()
    with stack:
        B = Builder(nc, stack, HALF)
        S = B.S

        def scratch(name, shape, dt):
            return S.dram(nc.dram_tensor(name, list(shape), dt).ap(), name)
        xa = scratch("xa", [D_MODEL, HALF], F32)
        xb = scratch("xb", [D_MODEL, HALF], F32)
        NCH = HALF // CH

        def cc_pair(name):
            srcs, dsts = [], []
            for k in range(NCH):
                sb_ = scratch("%s_s%d" % (name, k), [128, 8 * CH], BF16)
                db_ = scratch("%s_d%d" % (name, k), [256, 8 * CH], BF16)
                srcs.append((sb_, sb_.ap.rearrange("p (a s) -> (p a) s", s=CH)))
                dsts.append((db_, db_.ap.rearrange("p (a s) -> (p a) s", s=CH)))
            return srcs, dsts
        hsrc, hfull = cc_pair("h")
        ysrc, yfull = cc_pair("y")
        scr = {"Y": ysrc,
               "Q": scratch("scrQ", [128, SEQ], BF16),
               "K": scratch("scrK", [128, SEQ], BF16),
               "FP": scratch("scrFP", [96, SEQ], BF16)}
        cur = S.dram(xT, "xT")
        xout = S.dram(outT, "outT")
        for l in range(depth):
            last = l == depth - 1
            ct = (w["cols_%d" % l], w["tm_%d" % l])
            B.ffn_phase(cur, xa, ct[0], ct[1], K_FFN1, w["g1_%d" % l], w["u1_%d" % l], w["d1_%d" % l], h_out=hsrc, h_gather=hfull)
            B.mix_a(1, hfull, ct[0], ct[1], w["win1_%d" % l], w["wup_%d" % l], scr)
            B.mix_a(2, hfull, ct[0], ct[1], w["win2_%d" % l], w["wup_%d" % l], scr)
            B.mix_b(ct[0], scr, y_gather=yfull, prefetch=(w["g2_%d" % l], w["u2_%d" % l]))
            B.outproj_phase(xa, xb, yfull, ct[0], ct[1], w["wout_%d" % l])
            dst = xout if last else xa
            B.ffn_phase(xb, dst, ct[0], ct[1], K_FFN2, w["g2_%d" % l], w["u2_%d" % l], w["d2_%d" % l])
            cur = dst
        S.barrier()
        S.emit()
        build_program.nsem = S.nsem
    return nc


def make_inputs(inp, depth, r):
    d = {}
    for l in range(depth):
        ffn = {1: (inp["ffn1_w_gate"], inp["ffn1_w_up"], inp["ffn1_w_down"]),
               2: (inp["ffn2_w_gate"], inp["ffn2_w_up"], inp["ffn2_w_down"])}
        for f in (1, 2):
            d["ffn%d_w_gate_%d" % (f, l)] = np.ascontiguousarray(ffn[f][0][l], dtype=np.float32)
            d["ffn%d_w_up_%d" % (f, l)] = np.ascontiguousarray(ffn[f][1][l], dtype=np.float32)
            d["ffn%d_w_down_%d" % (f, l)] = np.ascontiguousarray(ffn[f][2][l], dtype=np.float32)
        p1, p2 = host_win(inp, l, r)
        d["w_in1_%d" % l] = p1
        d["w_in2_%d" % l] = p2
        d["w_out_%d" % l] = host_wout(inp, l)
        d["gla_wup_%d" % l] = np.ascontiguousarray(np.asarray(inp["gla_w_gate_up"][l], np.float32)[:, r * 128:(r + 1) * 128])
        cols, tm = host_tables(inp, l, r)
        d["cols_%d" % l] = cols
        d["tm_%d" % l] = tm
    return d


_PROG_CACHE = {}


def kernel(**inputs):
    x = np.asarray(inputs["x"], np.float32)
    Bsz, L, D = x.shape
    depth = inputs["w_in"].shape[0]
    assert (Bsz, L, D) == (4, SEQ, D_MODEL)
    if depth not in _PROG_CACHE:
        _PROG_CACHE[depth] = build_program(depth)
    nc = _PROG_CACHE[depth]
    shared = [make_inputs(inputs, depth, r) for r in range(2)]
    in_maps = []
    for c in range(8):
        b, r = c // 2, c % 2
        m = dict(shared[r])
        m["xT"] = np.ascontiguousarray(x[b, r * HALF:(r + 1) * HALF].T)
        in_maps.append(m)
    res = run_bass_kernel_spmd(nc, in_maps, core_ids=list(range(8)))
    out = np.empty((Bsz, L, D), np.float32)
    for c in range(8):
        b, r = c // 2, c % 2
        out[b, r * HALF:(r + 1) * HALF] = res.results[c]["outT"].T
    return out
```

```python
import numpy as np
from contextlib import ExitStack
import concourse.bass as bass
import concourse.mybir as mybir
from concourse.bass_utils import run_bass_kernel_spmd

F32 = mybir.dt.float32
BF16 = mybir.dt.bfloat16
ALU = mybir.AluOpType
AF = mybir.ActivationFunctionType
AX = mybir.AxisListType

SEM_CAP = 30000


class Buf:
    __slots__ = ("ap", "name", "lw", "rd")

    def __init__(self, ap, name):
        self.ap = ap
        self.name = name
        self.lw = None
        self.rd = {}

    def __getitem__(self, k):
        return V(self, self.ap[k])


class V:
    __slots__ = ("b", "ap")

    def __init__(self, b, ap):
        self.b = b
        self.ap = ap


def _bufs(*vs):
    return [v.b for v in vs if isinstance(v, V)]


def _ap(v):
    return v.ap if isinstance(v, V) else v


class Sched:
    ENGS = ("pe", "act", "dve", "pool", "sp")

    def __init__(self, nc, stack):
        self.nc = nc
        self.stack = stack
        self.streams = {e: [] for e in self.ENGS}
        self.dom = {e: [] for e in self.ENGS}
        self.waited = {e: {} for e in self.ENGS}
        self.nbuf = 0

    def sb(self, shape, dtype, name=None):
        self.nbuf += 1
        name = "%s_%d" % (name or "sb", self.nbuf)
        t = self.stack.enter_context(self.nc.sbuf_tensor(name, list(shape), dtype))
        return Buf(t, name)

    def ps(self, shape, dtype=F32, name=None):
        self.nbuf += 1
        name = "%s_%d" % (name or "ps", self.nbuf)
        t = self.stack.enter_context(self.nc.psum_tensor(name, list(shape), dtype))
        return Buf(t, name)

    def dram(self, ap, name):
        return Buf(ap, name)

    def _deps(self, eng, reads, writes):
        deps = {}

        def add(tok):
            if tok is None:
                return
            d, s = tok
            if d == "pe" and eng == "pe":
                return
            if deps.get(d, -1) < s:
                deps[d] = s

        for b in reads:
            add(b.lw)
        for b in writes:
            add(b.lw)
            for d, s in b.rd.items():
                add((d, s))
        w = self.waited[eng]
        out = []
        for d, s in deps.items():
            if w.get(d, -1) >= s:
                continue
            w[d] = s
            out.append((d, s))
        return out

    def op(self, eng, fn, reads=(), writes=(), dma=None):
        deps = self._deps(eng, reads, writes)
        dom = eng if dma is None else "dma:" + dma
        if dom not in self.dom:
            self.dom[dom] = []
        rec = {"fn": fn, "deps": deps, "sig": dma is not None, "dom": dom,
               "seq": len(self.dom[dom])}
        self.dom[dom].append(rec)
        self.streams[eng].append(rec)
        tok = (dom, rec["seq"])
        for b in reads:
            if b.rd.get(dom, -1) < rec["seq"]:
                b.rd[dom] = rec["seq"]
        for b in writes:
            b.lw = tok
            b.rd = {}
        return tok

    def barrier(self, exclude=()):
        last = {d: len(r) - 1 for d, r in self.dom.items() if r and d not in exclude}
        for e in self.ENGS:
            deps = []
            for d, s in last.items():
                if self.waited[e].get(d, -1) >= s:
                    continue
                self.waited[e][d] = s
                deps.append((d, s))
            rec = {"fn": (lambda eng: eng.nop()), "deps": deps, "sig": False, "dom": e,
                   "seq": len(self.dom[e])}
            self.dom[e].append(rec)
            self.streams[e].append(rec)

    def emit(self):
        nc = self.nc
        for e in self.ENGS:
            for rec in self.streams[e]:
                for d, s in rec["deps"]:
                    self.dom[d][s]["sig"] = True
        semtab = {}
        for d, recs in self.dom.items():
            step = 16 if (d.startswith("dma:") and not d.startswith("dma:cc")) else 1
            cap = SEM_CAP // step
            c = 0
            for rec in recs:
                if rec["sig"]:
                    rec["sv"] = (c // cap, (c % cap + 1) * step)
                    c += 1
            nep = (c + cap - 1) // cap
            semtab[d] = [self.stack.enter_context(nc.semaphore("s_%s_%d" % (d.replace(":", "_"), i)))
                         for i in range(max(nep, 0))]
        self.nsem = sum(len(v) for v in semtab.values())
        block = self.stack.enter_context(nc.Block())

        def run(engname):
            def body(eng):
                for rec in self.streams[engname]:
                    for d, s in rec["deps"]:
                        ep, val = self.dom[d][s]["sv"]
                        eng.wait_ge(semtab[d][ep], val)
                    ins = rec["fn"](eng)
                    if rec["sig"]:
                        ep, val = rec["sv"]
                        step = 16 if (rec["dom"].startswith("dma:") and not rec["dom"].startswith("dma:cc")) else 1
                        ins.then_inc(semtab[rec["dom"]][ep], step)
            return body

        block.tensor(run("pe"))
        block.scalar(run("act"))
        block.vector(run("dve"))
        block.gpsimd(run("pool"))
        block.sync(run("sp"))


class Ops:
    def __init__(self, S):
        self.S = S

    def dma(self, out, in_, chan, eng="sp"):
        return self.S.op(eng, lambda e: e.dma_start(out=out.ap, in_=in_.ap), reads=[in_.b], writes=[out.b], dma=chan)

    def act(self, out, in_, func, bias=0.0, scale=1.0, eng="act"):
        return self.S.op(eng, lambda e: e.activation(out=out.ap, in_=in_.ap, func=func, bias=_ap(bias), scale=_ap(scale)),
                         reads=_bufs(in_, bias, scale), writes=[out.b])

    def tt(self, out, in0, in1, op, eng="dve"):
        return self.S.op(eng, lambda e: e.tensor_tensor(out=out.ap, in0=in0.ap, in1=in1.ap, op=op),
                         reads=_bufs(in0, in1), writes=[out.b])

    def ts(self, out, in0, s1, s2, op0, op1=None, eng="dve"):
        if op1 is None:
            return self.S.op(eng, lambda e: e.tensor_scalar(out=out.ap, in0=in0.ap, scalar1=_ap(s1), scalar2=None, op0=op0),
                             reads=_bufs(in0, s1), writes=[out.b])
        return self.S.op(eng, lambda e: e.tensor_scalar(out=out.ap, in0=in0.ap, scalar1=_ap(s1), scalar2=_ap(s2), op0=op0, op1=op1),
                         reads=_bufs(in0, s1, s2), writes=[out.b])

    def stt(self, out, in0, scalar, in1, op0, op1, eng="dve"):
        eng = "dve"
        return self.S.op(eng, lambda e: e.scalar_tensor_tensor(out=out.ap, in0=in0.ap, scalar=_ap(scalar), in1=in1.ap, op0=op0, op1=op1),
                         reads=_bufs(in0, scalar, in1), writes=[out.b])

    def copy(self, out, in_, eng="dve"):
        if eng == "act":
            return self.act(out, in_, AF.Copy)
        return self.S.op(eng, lambda e: e.tensor_copy(out=out.ap, in_=in_.ap), reads=[in_.b], writes=[out.b])

    def recip(self, out, in_):
        return self.S.op("dve", lambda e: e.reciprocal(out=out.ap, in_=in_.ap), reads=[in_.b], writes=[out.b])

    def memset(self, out, val, eng="pool"):
        return self.S.op(eng, lambda e: e.memset(out.ap, val), writes=[out.b])

    def aselect(self, out, in_, pattern, cmp, fill, base, cm):
        return self.S.op("pool", lambda e: e.affine_select(out=out.ap, in_=in_.ap, pattern=pattern, compare_op=cmp,
                                                           fill=fill, base=base, channel_multiplier=cm),
                         reads=[in_.b], writes=[out.b])

    def mm(self, out, pairs, extra_reads=()):
        n = len(pairs)

        def fn(e):
            ins = None
            for i, (l, r) in enumerate(pairs):
                ins = e.matmul(out.ap, lhsT=l.ap, rhs=r.ap, start=(i == 0), stop=(i == n - 1))
            return ins
        rd = []
        for l, r in pairs:
            rd.append(l.b)
            rd.append(r.b)
        rd.extend(extra_reads)
        return self.S.op("pe", fn, reads=rd, writes=[out.b])

    def mm1(self, out, l, r, start, stop):
        return self.S.op("pe", lambda e: e.matmul(out.ap, lhsT=l.ap, rhs=r.ap, start=start, stop=stop),
                         reads=[l.b, r.b], writes=[out.b])

    def transpose(self, out, in_, ident):
        return self.S.op("pe", lambda e: e.transpose(out.ap, in_.ap, ident.ap), reads=[in_.b, ident.b], writes=[out.b])


D_MODEL = 1024
D_FF = 2816
NKC = D_MODEL // 128
NFC = D_FF // 128
IN_COLS = 3608
TT = 512
EPS = 1e-6
FOX_SHIFT = 10.0
NEG = -30000.0
SEQ = 8192
HALF = SEQ // 2
CH = 1024
GROUPS = [[0, 1], [2, 3], [4, 5], [6, 7]]

C_AQ, C_AK, C_AV, C_AG, C_ALR = 0, 256, 512, 768, 1024
C_BQ, C_BK, C_BV, C_BF = 1040, 1296, 1552, 1808
C_CZ, C_CX, C_CDT = 1812, 2068, 2836
C_DB, C_DC, C_DV = 2840, 3096, 3352
P1_GQ, P1_GK, P1_GV, P1_GG, P1_GLR, P1_DB, P1_DC, P1_DV, P1_N = 0, 128, 256, 384, 512, 528, 656, 784, 912
P2_BQ, P2_BK, P2_BV, P2_CZ, P2_CX, P2_CB, P2_CC, P2_CDT, P2_N = 0, 128, 256, 386, 514, 642, 770, 898, 900

K_FFN1, K_MIX, K_FFN2 = 0, 8, 16
K_GLAN = 24
K_FQN, K_FKN = 25, 26
K_FON = 27
K_CB = 29
K_CW = 32
K_D = 44
K_SN = 45
K_SCW = 46
K_SCN = 49
K_SEL = 50
NCOLT = 52
M_ALOG, M_FB, M_DTB, M_BG = 0, 2, 4, 6
M_ALOG4, M_DTB4, M_BG4 = 134, 142, 150
NTM = 150 + 512


def host_tables(inp, l, r):
    f32 = np.float32
    cols = np.zeros((128, NCOLT), f32)
    hs = slice(r * 128, (r + 1) * 128)

    def put(k, vec):
        v = np.asarray(vec, f32).reshape(-1, 128)
        for c in range(v.shape[0]):
            cols[:, k + c] = v[c]
    put(K_FFN1, inp["ffn1_norm"][l]); put(K_MIX, inp["mix_norm"][l]); put(K_FFN2, inp["ffn2_norm"][l])
    put(K_GLAN, inp["gla_norm"][l][hs])
    put(K_FQN, np.tile(inp["fox_q_norm"][l], 2)); put(K_FKN, np.tile(inp["fox_k_norm"][l], 2))
    fo = np.asarray(inp["fox_out_norm"][l], f32).reshape(4, 64)
    for h in range(2):
        cols[0:64, K_FON + h] = fo[2 * r + h]
    cb = np.asarray(inp["ssm_conv_b"][l], f32)
    cw = np.asarray(inp["ssm_conv_w"][l], f32)
    chs = [slice(r * 128, (r + 1) * 128), slice(256 + r * 128, 256 + (r + 1) * 128), slice(512 + r * 128, 512 + (r + 1) * 128)]
    for c in range(3):
        cols[:, K_CB + c] = cb[chs[c]]
        for k in range(4):
            cols[:, K_CW + 3 * k + c] = cw[k][chs[c]]
    put(K_D, np.repeat(np.asarray(inp["ssm_D"][l], f32)[2 * r:2 * r + 2], 64))
    put(K_SN, inp["ssm_norm"][l][hs])
    for k in range(3):
        put(K_SCW + k, inp["sc_conv_w"][l][k][hs])
    put(K_SCN, inp["sc_out_norm"][l][hs])
    cols[:, K_SEL + r] = 1.0
    tm = np.zeros((128, NTM), f32)
    tm[:, M_ALOG:M_ALOG + 2] = np.asarray(inp["ssm_A_log"][l], f32)[None, 2 * r:2 * r + 2]
    tm[:, M_FB:M_FB + 2] = np.asarray(inp["fox_b_forget"][l], f32)[None, 2 * r:2 * r + 2]
    tm[:, M_DTB:M_DTB + 2] = np.asarray(inp["ssm_dt_bias"][l], f32)[None, 2 * r:2 * r + 2]
    tm[:, M_BG:M_BG + 128] = np.asarray(inp["gla_b_gate"][l], f32)[None, hs]
    tm[:, M_ALOG4:M_ALOG4 + 8] = np.tile(tm[:, M_ALOG:M_ALOG + 2], (1, 4))
    tm[:, M_DTB4:M_DTB4 + 8] = np.tile(tm[:, M_DTB:M_DTB + 2], (1, 4))
    tm[:, M_BG4:M_BG4 + 512] = np.tile(tm[:, M_BG:M_BG + 128], (1, 4))
    return cols, tm


def host_win(inp, l, r):
    w = np.asarray(inp["w_in"][l], np.float32)
    h = lambda c0: w[:, c0 + r * 128: c0 + (r + 1) * 128]
    p1 = np.concatenate([h(C_AQ), h(C_AK), h(C_AV), h(C_AG), w[:, C_ALR:C_ALR + 16], h(C_DB), h(C_DC), h(C_DV)], axis=1)
    p2 = np.concatenate([h(C_BQ), h(C_BK), h(C_BV), w[:, C_BF + 2 * r:C_BF + 2 * r + 2], h(C_CZ),
                         h(C_CX), h(C_CX + 256), h(C_CX + 512), w[:, C_CDT + 2 * r:C_CDT + 2 * r + 2]], axis=1)
    assert p1.shape[1] == P1_N and p2.shape[1] == P2_N
    return np.ascontiguousarray(p1), np.ascontiguousarray(p2)


def host_wout(inp, l):
    w = np.asarray(inp["w_out"][l], np.float32)
    rows = []
    for rr in range(2):
        for blk in range(4):
            rows.append(w[blk * 256 + rr * 128: blk * 256 + (rr + 1) * 128])
    return np.ascontiguousarray(np.concatenate(rows, axis=0))


import os
DBG = set(os.environ.get("MIXDBG", "sc,gla,ssd,fox,fox2,fox3,b").split(","))
ARENA_UNITS = 212000 // 2


class Builder:
    def __init__(self, nc, stack, S_tok):
        self.nc = nc
        self.S_tok = S_tok
        self.S = Sched(nc, stack)
        self.o = Ops(self.S)
        S = self.S
        self.arena = stack.enter_context(nc.sbuf_tensor("arena", [128, ARENA_UNITS], BF16))
        self.poff = 0
        self.nb = 0
        self.psum = [S.ps([128, TT], F32, "bank%d" % i) for i in range(8)]
        self.pi = 0
        self.ring = list(range(8))
        self.consts()
        self.pbase = self.poff

    def alloc(self, shape, dtype, name):
        n = 1
        for d in shape[1:]:
            n *= d
        units = n * (2 if dtype == F32 else 1)
        units = (units + 15) // 16 * 16
        assert self.poff + units <= ARENA_UNITS, ("SBUF arena overflow", name, self.poff, units)
        lim = getattr(self, "top_limit", None)
        assert lim is None or self.poff >= lim or self.poff + units <= lim, ("collides with prefetched weights", name)
        ap = self.arena[:, self.poff:self.poff + units]
        self.poff += units
        if dtype == F32:
            ap = ap.bitcast(F32)
            ap = ap[:, 0:n]
        else:
            ap = ap[:, 0:n]
        if len(shape) == 3:
            ap = ap.rearrange("p (a b) -> p a b", a=shape[1])
        elif len(shape) == 4:
            ap = ap.rearrange("p (a b c) -> p a b c", a=shape[1], b=shape[2])
        if shape[0] < 128:
            ap = ap[0:shape[0]]
        self.nb += 1
        return Buf(ap, "%s_%d" % (name, self.nb))

    def alloc_at(self, off, shape, dtype, name):
        save = self.poff
        self.poff = off
        b = self.alloc(shape, dtype, name)
        self.poff = save
        return b

    def ffn_weight_bufs(self):
        HF = D_FF // 2
        top = ARENA_UNITS - 4 * NKC * HF
        wg = [self.alloc_at(top + (2 * i) * NKC * HF, [128, NKC, HF], BF16, "wg%d" % i) for i in range(2)]
        wu = [self.alloc_at(top + (2 * i + 1) * NKC * HF, [128, NKC, HF], BF16, "wu%d" % i) for i in range(2)]
        return top, wg, wu

    def load_gate_up(self, wg, wu, wg_ap, wu_ap):
        o = self.o
        HF = D_FF // 2
        wgd, wud = self.S.dram(wg_ap, "wgd"), self.S.dram(wu_ap, "wud")
        for i in range(2):
            for k in range(NKC):
                o.dma(wg[i][:, k, :], V(wgd, wg_ap[k * 128:(k + 1) * 128, i * HF:(i + 1) * HF]), "wA%d" % i, eng="pool")
            for k in range(NKC):
                o.dma(wu[i][:, k, :], V(wud, wu_ap[k * 128:(k + 1) * 128, i * HF:(i + 1) * HF]), "wB%d" % i, eng="pool")

    def prefetch_ffn(self, wg_ap, wu_ap):
        top, wg, wu = self.ffn_weight_bufs()
        self.load_gate_up(wg, wu, wg_ap, wu_ap)
        self.pref = (wg, wu)
        self.top_limit = top
        self.bar_excl = {"dma:wA0", "dma:wA1", "dma:wB0", "dma:wB1"}

    def phase_begin(self):
        self.S.barrier(exclude=getattr(self, "bar_excl", ()))
        self.poff = self.pbase

    def bank(self):
        b = self.psum[self.ring[self.pi % len(self.ring)]]
        self.pi += 1
        return b

    def consts(self):
        o = self.o
        A = self.alloc
        self.ones_mean = {}
        for blk in (1024, 128, 64):
            t = A([128, 128], BF16, "ones%d" % blk)
            o.memset(t[:], 1.0 / blk)
            if blk == 64:
                o.memset(t[0:64, 64:128], 0.0)
                o.memset(t[64:128, 0:64], 0.0)
            self.ones_mean[blk] = t
        self.eps_col = A([128, 1], F32, "epscol")
        o.memset(self.eps_col[:], EPS)
        self.ident_b = A([128, 128], BF16, "identb")
        o.memset(self.ident_b[:], 1.0)
        o.aselect(self.ident_b[:], self.ident_b[:], [[-1, 128]], ALU.is_equal, 0.0, 0, 1)
        self.ident_f = A([128, 128], F32, "identf")
        o.memset(self.ident_f[:], 1.0)
        o.aselect(self.ident_f[:], self.ident_f[:], [[-1, 128]], ALU.is_equal, 0.0, 0, 1)
        self.triU = A([128, 128], F32, "triU")
        o.memset(self.triU[:], 1.0)
        o.aselect(self.triU[:], self.triU[:], [[1, 128]], ALU.is_ge, 0.0, 0, -1)
        self.triU_b = A([128, 128], BF16, "triUb")
        o.copy(self.triU_b[:], self.triU[:], eng="pool")
        self.triG = A([128, 128], F32, "triG")
        o.memset(self.triG[:], -1.0 / 16.0)
        o.aselect(self.triG[:], self.triG[:], [[1, 128]], ALU.is_ge, 0.0, 0, -1)
        self.triS = A([128, 128], F32, "triS")
        o.memset(self.triS[:], 1.0)
        o.aselect(self.triS[:], self.triS[:], [[-1, 128]], ALU.is_gt, 0.0, 0, 1)
        self.maskb = A([128, 128], F32, "maskb")
        o.memset(self.maskb[:], 0.0)
        o.aselect(self.maskb[:], self.maskb[:], [[1, 128]], ALU.is_ge, NEG, 0, -1)
        self.ones_f = A([128, 128], F32, "onesf")
        o.memset(self.ones_f[:], 1.0)
        self.triU4 = A([128, 4, 128], F32, "triU4")
        self.triS4 = A([128, 4, 128], F32, "triS4")
        self.ones4 = A([128, 4, 128], F32, "ones4")
        for s4 in range(4):
            o.copy(self.triU4[:, s4, :], self.triU[:], eng="pool")
            o.copy(self.triS4[:, s4, :], self.triS[:], eng="pool")
        o.memset(self.ones4[:], 1.0)
        self.sel = A([96, 4, 128], BF16, "sel")
        o.memset(self.sel[:], 0.0)
        for h in range(4):
            v = self.sel[0:96, h, :]
            o.ts(v, self.ones_f[0:96, :], self.ident_f[0:96, h:h + 1], None, ALU.mult, eng="pool")
            o.stt(v, self.ones_f[0:96, :], self.ident_f[0:96, 32 + h:33 + h], v, ALU.mult, ALU.add, eng="pool")
            o.stt(v, self.ones_f[0:96, :], self.ident_f[0:96, 64 + h:65 + h], v, ALU.mult, ALU.add, eng="pool")
        self.wden = A([65, 64], BF16, "wden")
        o.memset(self.wden[0:65, :], EPS)
        o.memset(self.wden[0:64, :], 1.0 / 64)

    def load_weight(self, dst, w_ap, nk, ncols, chan, rows0=0):
        wd = self.S.dram(w_ap, "wdram")
        for k in range(nk):
            src = w_ap[rows0 + k * 128: rows0 + (k + 1) * 128, :]
            self.o.dma(dst[:, k, :], V(wd, src), chan, eng="pool")

    def load_tables(self, cols_ap, tm_ap):
        self.cols = self.alloc([128, NCOLT], F32, "cols")
        self.tm = self.alloc([128, NTM], F32, "tm")
        self.o.dma(self.cols[:], V(self.S.dram(cols_ap, "colsd"), cols_ap), "cols")
        self.o.dma(self.tm[:], V(self.S.dram(tm_ap, "tmd"), tm_ap), "tm")

    def alloc_xnorm(self):
        self.xt = self.alloc([128, NKC, TT], F32, "xt")
        self.ht = self.alloc([128, NKC, TT], BF16, "ht")
        self.sq = self.alloc([128, 2, TT], BF16, "sq")
        self.rstd = self.alloc([128, TT], F32, "rstd")

    def rmsnorm_tile(self, kcol):
        o = self.o
        ps = self.bank()
        for c in range(NKC):
            o.act(self.sq[:, c % 2, :], self.xt[:, c, :], AF.Square)
            o.mm1(ps[:], self.ones_mean[1024][:], self.sq[:, c % 2, :], c == 0, c == NKC - 1)
        o.act(self.rstd[:], ps[:], AF.Sqrt, bias=self.eps_col[:])
        o.recip(self.rstd[:], self.rstd[:])
        for c in range(NKC):
            o.stt(self.ht[:, c, :], self.xt[:, c, :], self.cols[:, kcol + c:kcol + c + 1], self.rstd[:], ALU.mult, ALU.mult,
                  eng="dve" if c % 2 == 0 else "pool")

    def group_rms(self, y, blk, gain, out, tmp, post_mul=None, post_scale=None):
        o = self.o
        sqb = self.gsq
        o.act(sqb[:], y, AF.Square)
        ps = self.bank()
        o.mm(ps[:], [(self.ones_mean[blk][:], sqb[:])])
        o.act(self.grs[:], ps[:], AF.Sqrt, bias=self.eps_col[:])
        o.recip(self.grs[:], self.grs[:])
        if post_mul is None and post_scale is None:
            o.stt(out, y, gain, self.grs[:], ALU.mult, ALU.mult)
        else:
            o.stt(tmp, y, gain, self.grs[:], ALU.mult, ALU.mult)
            if post_mul is not None:
                o.tt(out, tmp, post_mul, ALU.mult)
            else:
                o.ts(out, tmp, post_scale, None, ALU.mult)

    def ffn_phase(self, x_in, x_out, cols_ap, tm_ap, kcol, wg_ap, wu_ap, wd_ap, h_out=None, h_gather=None):
        o = self.o
        self.phase_begin()
        self.ring = list(range(7))
        pstat = self.psum[7]
        HF = D_FF // 2
        NJ = NFC // 2
        wdn = self.alloc([128, NFC, D_MODEL], BF16, "wd")
        self.load_tables(cols_ap, tm_ap)
        if getattr(self, "pref", None) is not None:
            wg, wu = self.pref
            self.pref = None
            self.bar_excl = set()
        else:
            top, wg, wu = self.ffn_weight_bufs()
            self.top_limit = top
            self.load_gate_up(wg, wu, wg_ap, wu_ap)
        self.load_weight(wdn, wd_ap, NFC, D_MODEL, "wC")
        xc = [self.alloc([128, TT], F32, "xc%d" % c) for c in range(NKC)]
        self.ht = self.alloc([128, NKC, TT], BF16, "ht")
        self.sq = self.alloc([128, 2, TT], BF16, "sq")
        self.rstd = self.alloc([128, TT], F32, "rstd")
        act_t = self.alloc([128, NFC, TT], BF16, "actT")
        sg = [self.alloc([128, TT], F32, "sg%d" % i) for i in range(2)]
        xin_r = x_in.ap.rearrange("(c p) s -> p c s", p=128)
        xout_r = x_out.ap.rearrange("(c p) s -> p c s", p=128)
        nsq = 0

        def finish_norm(kc):
            o.act(self.rstd[:], pstat[:], AF.Sqrt, bias=self.eps_col[:])
            o.recip(self.rstd[:], self.rstd[:])
            for c in range(NKC):
                o.stt(self.ht[:, c, :], xc[c][:], self.cols[:, kc + c:kc + c + 1], self.rstd[:], ALU.mult, ALU.mult)

        def stat(c):
            nonlocal nsq
            sqv = self.sq[:, nsq % 2, :]
            nsq += 1
            o.act(sqv, xc[c][:], AF.Square)
            o.mm1(pstat[:], self.ones_mean[1024][:], sqv, c == 0, c == NKC - 1)

        NTL = self.S_tok // TT
        for c in range(NKC):
            o.dma(xc[c][:], V(x_in, xin_r[:, c, 0:TT]), "xt%d" % c)
        for t in range(NTL):
            ts = slice(t * TT, (t + 1) * TT)
            for c in range(NKC):
                stat(c)
            finish_norm(kcol)
            for j in range(NFC):
                pg = self.bank()
                pu = self.bank()
                i, jj = j // NJ, j % NJ
                o.mm(pg[:], [(wg[i][:, k, jj * 128:(jj + 1) * 128], self.ht[:, k, :]) for k in range(NKC)])
                o.mm(pu[:], [(wu[i][:, k, jj * 128:(jj + 1) * 128], self.ht[:, k, :]) for k in range(NKC)])
                o.act(sg[j % 2][:], pg[:], AF.Silu)
                o.tt(act_t[:, j, :], sg[j % 2][:], pu[:], ALU.mult)
            for c in range(NKC):
                py = self.bank()
                o.mm(py[:], [(wdn[:, j, c * 128:(c + 1) * 128], act_t[:, j, :]) for j in range(NFC)])
                o.stt(xc[c][:], py[:], 0.5, xc[c][:], ALU.mult, ALU.add)
                o.dma(V(x_out, xout_r[:, c, ts]), xc[c][:], "xo%d" % c)
                if h_out is not None:
                    if c > 0:
                        stat(c - 1)
                    if c == NKC - 1:
                        stat(c)
                elif t + 1 < NTL:
                    o.dma(xc[c][:], V(x_in, xin_r[:, c, (t + 1) * TT:(t + 2) * TT]), "xt%d" % c)
            if h_out is not None:
                finish_norm(K_MIX)
                hb, hv = h_out[(t * TT) // CH]
                c0 = (t * TT) % CH
                o.dma(V(hb, hv.rearrange("(c p) s -> p c s", p=128)[:, :, c0:c0 + TT]), self.ht[:], "ho")
                if t + 1 < NTL:
                    for c in range(NKC):
                        o.dma(xc[c][:], V(x_in, xin_r[:, c, (t + 1) * TT:(t + 2) * TT]), "xt%d" % c)
                if c0 + TT == CH and h_gather is not None:
                    k_ = (t * TT) // CH
                    self.all_gather(h_out[k_][0], h_gather[k_][0])
        self.top_limit = None

    def all_gather(self, src, dst):
        self.S.op("pool", lambda e: e.collective_compute("AllGather", ALU.bypass, replica_groups=GROUPS,
                                                         ins=[src.ap.opt()], outs=[dst.ap.opt()]),
                  reads=[src], writes=[dst], dma="ccag")

    def mix_a(self, part, hfull, cols_ap, tm_ap, w_ap, wup_ap, scr):
        o = self.o
        A = self.alloc
        NT = SEQ // TT
        self.phase_begin()
        self.ring = [1, 2, 3, 4, 5, 6, 7]
        acc = self.psum[0]
        VA_, EB_ = (A([128, SEQ // 128, 2, 66], BF16, "VA"), A([128, SEQ // 128, 2], F32, "EB"))
        if part == 2:
            self.VA, self.EB = VA_, EB_
        self.mixb_base = self.poff
        NW = P1_N if part == 1 else P2_N
        win = A([128, NKC, NW], BF16, "win")
        self.load_weight(win, w_ap, NKC, NW, "wA")
        self.load_tables(cols_ap, tm_ap)
        cols, tm = self.cols, self.tm
        hts = [A([128, NKC, TT], BF16, "ht%d" % i) for i in range(2)]
        self.gsq = A([128, TT], BF16, "gsq")
        self.grs = A([128, TT], F32, "grs")
        T = [A([128, TT], F32, "T%d" % i) for i in range(6)]
        Yt = A([128, 4, TT], BF16, "Yt")
        sm = [A([128, 2], F32, "sm%d" % i) for i in range(6)]
        Q = [A([128, 128], F32, "Q%d" % i) for i in range(4)]
        Qb = [A([128, 128], BF16, "Qb%d" % i) for i in range(4)]
        if part == 1:
            scu = A([128, TT + 2], F32, "scu")
            o.memset(scu[:, 0:2], 0.0)
            wup = A([16, 128], F32, "wup")
            o.dma(wup[:], V(self.S.dram(wup_ap, "wupd"), wup_ap), "wup")
            GS32 = A([128, 64], F32, "GS32")
            GSb = A([128, 64], BF16, "GSb")
            o.memset(GS32[:], 0.0)
            o.memset(GSb[:], 0.0)
            la = A([128, 4, 128], F32, "la")
            vtm = A([128, 4, 128], BF16, "vtm")
            qdb = A([128, TT], BF16, "qdb")
            kdb = A([128, TT], BF16, "kdb")
            kdt = A([128, 4, 128], BF16, "kdt")
            attm = [A([128, 4, 128], BF16, "attm%d" % i) for i in range(2)]
            glr = A([32, TT], F32, "glr")
        else:
            o.memset(self.VA[:], 1.0)
            aneg4 = A([128, 8], F32, "aneg4")
            o.act(aneg4[:], tm[:, M_ALOG4:M_ALOG4 + 8], AF.Exp)
            acs4 = A([128, 8], F32, "acs4")
            dte4 = A([128, 8], F32, "dte4")
            eal4 = A([128, 8], F32, "eal4")
            Xd4 = A([128, 4, 128], BF16, "Xd4")
            Xdd4 = A([128, 4, 128], BF16, "Xdd4")
            Btm4 = A([128, 4, 128], BF16, "Btm4")
            lseg4 = [A([128, 4, 128], F32, "lseg4_%d" % h) for h in range(2)]
            abc4 = [A([128, 4, 128], F32, "abc4_%d" % h) for h in range(2)]
            LT4 = [A([128, TT], F32, "LT4_%d" % h) for h in range(2)]
            Ecs4 = [A([128, TT], F32, "Ecs4_%d" % h) for h in range(2)]
            WT4 = [A([128, TT], BF16, "WT4_%d" % h) for h in range(2)]
            CsT4 = [A([128, TT], BF16, "CsT4_%d" % h) for h in range(2)]
            raw = [A([128, TT + 3], F32, "raw%d" % c) for c in range(3)]
            for c in range(3):
                o.memset(raw[c][:, 0:3], 0.0)
            xs32 = A([128, TT], F32, "xs32")
            xsb = A([128, TT], BF16, "xsb")
            BT = A([128, TT], BF16, "BT")
            CT = A([128, TT], BF16, "CT")
            zs = A([128, TT], F32, "zs")
            dt_tm = A([128, 4, 2], F32, "dt_tm")
            a_tm = A([128, 4, 2], F32, "a_tm")
            Xd = A([128, 128], BF16, "Xd")
            Xdd = A([128, 128], BF16, "Xdd")
            Btm = A([128, 128], BF16, "Btm")
            ST32 = [A([128, 64], F32, "ST32_%d" % h) for h in range(2)]
            STb = [A([128, 64], BF16, "STb%d" % h) for h in range(2)]
            for h in range(2):
                o.memset(ST32[h][:], 0.0)
                o.memset(STb[h][:], 0.0)
            gcar_bc = A([128, 2], F32, "gcar_bc")
            gcar_T = A([96, 1], F32, "gcar_T")
            o.memset(gcar_bc[:], 0.0)
            o.memset(gcar_T[:], 0.0)
            nl = A([128, 4, 2], F32, "nl")
            nl3 = A([128, 4, 96], F32, "nl3")
            o.memset(nl3[:], 0.0)
            Fp = A([96, TT], BF16, "Fp")
            F1 = [A([96, TT], F32, "F1_%d" % i) for i in range(3)]
            F1b = [A([96, TT], BF16, "F1b_%d" % i) for i in range(3)]
            qb = A([128, TT], BF16, "qb")
            kb = A([128, TT], BF16, "kb")
            o.memset(Fp[:], 0.0)

        hrs = [(b_, v_.rearrange("(r c p) s -> p r c s", r=2, p=128)) for b_, v_ in hfull]
        Yrs = [(b_, v_.rearrange("(hf c p) s -> p hf c s", hf=2, p=128)) for b_, v_ in scr["Y"]]

        def proj_fm(col0, n=128):
            ps = self.bank()
            o.mm(ps[0:n, :], [(win[:, k, col0:col0 + n], self.ht[:, k, :]) for k in range(NKC)])
            return ps

        def proj_tm(sub, col0, n):
            ps = self.bank()
            o.mm(ps[:, 0:n], [(self.ht[:, k, sub * 128:(sub + 1) * 128], win[:, k, col0:col0 + n]) for k in range(NKC)])
            return ps

        def load_h(t):
            hf_, tl_ = t // (NT // 2), t % (NT // 2)
            hfb_, hr_ = hrs[(tl_ * TT) // CH]
            c0_ = (tl_ * TT) % CH
            o.dma(hts[t % 2][:], V(hfb_, hr_[:, hf_, :, c0_:c0_ + TT]), "ht%d" % (t % 2))
        load_h(0)
        for t in range(NT):
            ts_ = slice(t * TT, (t + 1) * TT)
            hf, tl = t // (NT // 2), t % (NT // 2)
            ck = (tl * TT) // CH
            tsl = slice((tl * TT) % CH, (tl * TT) % CH + TT)
            Yb_, Yr = Yrs[ck]
            self.ht = hts[t % 2]
            if t + 1 < NT:
                load_h(t + 1)
            if part == 1:
                pc = proj_fm(P1_DC)
                o.copy(T[0][:], pc[:], eng="act")
                pv = proj_fm(P1_DV)
                o.tt(scu[:, 2:TT + 2], T[0][:], pv[:], ALU.mult)
                o.ts(T[1][:], scu[:, 0:TT], cols[:, K_SCW:K_SCW + 1], None, ALU.mult)
                o.stt(T[1][:], scu[:, 1:TT + 1], cols[:, K_SCW + 1:K_SCW + 2], T[1][:], ALU.mult, ALU.add)
                o.stt(T[1][:], scu[:, 2:TT + 2], cols[:, K_SCW + 2:K_SCW + 3], T[1][:], ALU.mult, ALU.add)
                o.copy(scu[:, 0:2], scu[:, TT:TT + 2], eng="pool")
                pb = proj_fm(P1_DB)
                o.tt(T[2][:], T[1][:], pb[:], ALU.mult)
                self.group_rms(T[2][:], 64, cols[:, K_SCN:K_SCN + 1], Yt[:, 3, :], None)
                pl = proj_fm(P1_GLR, 16)
                o.copy(glr[0:16, :], pl[0:16, :], eng="act")
                PZ = self.bank()
                for s_ in range(4):
                    o.mm(PZ[:, s_ * 128:(s_ + 1) * 128], [(glr[0:16, s_ * 128:(s_ + 1) * 128], wup[:])])
                laf = la.ap.rearrange("p a b -> p (a b)")
                laf = V(la, laf)
                o.tt(laf, PZ[:], tm[:, M_BG4:M_BG4 + 512], ALU.add)
                o.act(laf, laf, AF.Exp, scale=-1.0)
                o.act(laf, laf, AF.Ln, bias=1.0)
                PV_ = self.bank()
                for s_ in range(4):
                    o.mm(PV_[:, s_ * 128:(s_ + 1) * 128],
                         [(self.ht[:, k, s_ * 128:(s_ + 1) * 128], win[:, k, P1_GV:P1_GV + 128]) for k in range(NKC)])
                o.copy(V(vtm, vtm.ap.rearrange("p a b -> p (a b)")), PV_[:], eng="act")
                pq = proj_fm(P1_GQ)
                o.copy(T[0][:], pq[:], eng="act")
                pk = proj_fm(P1_GK)
                o.copy(T[1][:], pk[:], eng="act")
                pgo = proj_fm(P1_GG)
                o.act(T[2][:], pgo[:], AF.Silu)
                PB = self.bank()
                for s_ in range(4):
                    o.mm(PB[:, s_ * 128:(s_ + 1) * 128], [(la[:, s_, :], self.triG[:])])
                eb, enb = T[5], T[4]
                o.act(eb[:], PB[:], AF.Exp)
                o.act(enb[:], PB[:], AF.Exp, scale=-1.0)
                o.stt(qdb[:], T[0][:], 0.125, eb[:], ALU.mult, ALU.mult)
                o.tt(kdb[:], T[1][:], enb[:], ALU.mult)
                PTr = self.bank()
                PTb = V(PTr, PTr.ap[:, 0:256].bitcast(BF16))
                for s_ in range(4):
                    o.transpose(V(PTr, PTb.ap[:, s_ * 128:(s_ + 1) * 128]), kdb[:, s_ * 128:(s_ + 1) * 128], self.ident_b[:])
                o.copy(V(kdt, kdt.ap.rearrange("p a b -> p (a b)")), PTb)
                for hh in range(2):
                    R = slice(hh * 64, (hh + 1) * 64)
                    PA = self.bank()
                    for s_ in range(4):
                        cs = slice(s_ * 128, (s_ + 1) * 128)
                        o.mm(PA[:, cs], [(kdb[R, cs], qdb[R, cs])])
                    o.tt(V(attm[hh], attm[hh].ap.rearrange("p a b -> p (a b)")), PA[:],
                         V(self.triU4, self.triU4.ap.rearrange("p a b -> p (a b)")), ALU.mult)
                PM = self.bank()
                for s_ in range(4):
                    for hh in range(2):
                        R = slice(hh * 64, (hh + 1) * 64)
                        o.mm(PM[R, s_ * 64:(s_ + 1) * 64], [(kdt[:, s_, R], vtm[:, s_, hh * 64:(hh + 1) * 64])])
                for s_ in range(4):
                    cs = slice(s_ * 128, (s_ + 1) * 128)
                    for hh in range(2):
                        R = slice(hh * 64, (hh + 1) * 64)
                        o.mm(acc[R, cs], [(vtm[:, s_, hh * 64:(hh + 1) * 64], attm[hh][:, s_, :]), (GSb[R, :], qdb[R, cs])])
                    eL = eb[:, s_ * 128 + 127:s_ * 128 + 128]
                    o.ts(GS32[:], GS32[:], eL, None, ALU.mult)
                    o.stt(GS32[:], PM[:, s_ * 64:(s_ + 1) * 64], eL, GS32[:], ALU.mult, ALU.add)
                    o.copy(GSb[:], GS32[:], eng="pool")
                o.copy(T[3][:], acc[:], eng="act")
                self.group_rms(T[3][:], 64, cols[:, K_GLAN:K_GLAN + 1], Yt[:, 0, :], T[4][:], post_mul=T[2][:])
                o.dma(V(Yb_, Yr[:, hf, 0, tsl]), Yt[:, 0, :], "yo")
                o.dma(V(Yb_, Yr[:, hf, 3, tsl]), Yt[:, 3, :], "yo2")
            else:
                for c in range(3):
                    pr = proj_fm(P2_CX + c * 128)
                    o.copy(raw[c][:, 3:TT + 3], pr[:], eng="act")
                    tc_ = T[0] if c % 2 == 0 else T[1]
                    o.ts(tc_[:], raw[c][:, 0:TT], cols[:, K_CW + c:K_CW + c + 1], None, ALU.mult)
                    for k in range(1, 4):
                        o.stt(tc_[:], raw[c][:, k:TT + k], cols[:, K_CW + 3 * k + c:K_CW + 3 * k + c + 1], tc_[:], ALU.mult, ALU.add)
                    o.copy(raw[c][:, 0:3], raw[c][:, TT:TT + 3], eng="pool")
                    bcol = cols[:, K_CB + c:K_CB + c + 1]
                    if c == 0:
                        o.act(xs32[:], tc_[:], AF.Silu, bias=bcol)
                        o.copy(xsb[:], xs32[:], eng="pool")
                    elif c == 1:
                        o.act(BT[:], tc_[:], AF.Silu, bias=bcol)
                    else:
                        o.act(CT[:], tc_[:], AF.Silu, bias=bcol)
                pq = proj_fm(P2_BQ)
                o.copy(T[5][:], pq[:], eng="act")
                self.group_rms(T[5][:], 64, cols[:, K_FQN:K_FQN + 1], qb[:], T[4][:], post_scale=0.125)
                o.dma(V(scr["Q"], scr["Q"].ap[:, ts_]), qb[:], "qo")
                pk = proj_fm(P2_BK)
                o.copy(T[3][:], pk[:], eng="act")
                self.group_rms(T[3][:], 64, cols[:, K_FKN:K_FKN + 1], kb[:], None)
                o.dma(V(scr["K"], scr["K"].ap[:, ts_]), kb[:], "ko")
                for s_ in range(4):
                    kt = t * 4 + s_
                    pv = proj_tm(s_, P2_BV, 130)
                    for h in range(2):
                        o.copy(self.VA[:, kt, h, 0:64], pv[:, h * 64:(h + 1) * 64], eng="act" if h % 2 == 0 else "dve")
                    o.tt(nl[:, s_, :], pv[:, 128:130], tm[:, M_FB:M_FB + 2], ALU.add)
                    o.act(nl[:, s_, :], nl[:, s_, :], AF.Exp, scale=-1.0)
                    o.act(nl[:, s_, :], nl[:, s_, :], AF.Ln, bias=1.0)
                    for g in range(3):
                        o.copy(nl3[:, s_, g * 32:g * 32 + 2], nl[:, s_, :], eng="pool")
                    pg_ = self.bank()
                    o.mm(pg_[:, 0:2], [(self.ones_f[:], nl[:, s2, :]) for s2 in range(s_)] + [(self.triU[:], nl[:, s_, :])])
                    o.tt(sm[3][:], pg_[:, 0:2], gcar_bc[:], ALU.add)
                    o.ts(self.EB[:, kt, :], sm[3][:], -FOX_SHIFT, None, ALU.add)
                ptot = self.bank()
                o.mm(ptot[:, 0:2], [(self.ones_f[:], nl[:, s2, :]) for s2 in range(4)])
                o.tt(gcar_bc[:], gcar_bc[:], ptot[:, 0:2], ALU.add)
                pgt = self.bank()
                for s_ in range(4):
                    cs = slice(s_ * 128, (s_ + 1) * 128)
                    o.mm(pgt[0:96, cs], [(nl3[:, s2, :], self.ones_f[:]) for s2 in range(s_)] + [(nl3[:, s_, :], self.triU[:])])
                GT = F1[0]
                o.ts(GT[:], pgt[0:96, :], gcar_T[:, 0:1], None, ALU.add)
                o.copy(gcar_T[:], GT[:, TT - 1:TT])
                hi, mid, lo = F1b[0], F1b[1], F1b[2]
                r1, r2 = F1[1], F1[2]
                o.ts(hi[:], GT[:], -1.0, None, ALU.mult)
                o.stt(r1[:], GT[:], -1.0, hi[:], ALU.mult, ALU.subtract)
                o.copy(mid[:], r1[:])
                o.tt(r2[:], r1[:], mid[:], ALU.subtract)
                o.copy(lo[:], r2[:])
                o.copy(Fp[0:4, :], hi[0:4, :])
                o.copy(Fp[32:36, :], mid[32:36, :])
                o.copy(Fp[64:68, :], lo[64:68, :])
                o.dma(V(scr["FP"], scr["FP"].ap[:, ts_]), Fp[:], "fpo")
                pz = proj_fm(P2_CZ)
                o.act(zs[:], pz[:], AF.Silu)
                PD = self.bank()
                for s_ in range(4):
                    o.mm(PD[:, s_ * 2:(s_ + 1) * 2],
                         [(self.ht[:, k, s_ * 128:(s_ + 1) * 128], win[:, k, P2_CDT:P2_CDT + 2]) for k in range(NKC)])
                dt4 = V(dt_tm, dt_tm.ap.rearrange("p a b -> p (a b)"))
                a4 = V(a_tm, a_tm.ap.rearrange("p a b -> p (a b)"))
                o.tt(dt4, PD[:, 0:8], tm[:, M_DTB4:M_DTB4 + 8], ALU.add)
                o.act(dt4, dt4, AF.Exp)
                o.act(dt4, dt4, AF.Ln, bias=1.0)
                o.stt(a4, dt4, -1.0, aneg4[:], ALU.mult, ALU.mult)
                PCS = self.bank()
                o.mm(PCS[:, 0:8], [(self.triU[:], a4)])
                o.mm(PCS[:, 8:16], [(self.ones_f[:], a4)])
                o.copy(acs4[:], PCS[:, 0:8])
                o.tt(dte4[:], PCS[:, 8:16], acs4[:], ALU.subtract)
                o.act(dte4[:], dte4[:], AF.Exp)
                o.act(eal4[:], PCS[:, 8:16], AF.Exp)
                PX = self.bank()
                PXb = V(PX, PX.ap[:, 0:256].bitcast(BF16))
                for s_ in range(4):
                    o.transpose(V(PX, PXb.ap[:, s_ * 128:(s_ + 1) * 128]), xsb[:, s_ * 128:(s_ + 1) * 128], self.ident_b[:])
                Xd8 = V(Xd4, Xd4.ap.rearrange("p a (h e) -> p (a h) e", h=2))
                Xdd8 = V(Xdd4, Xdd4.ap.rearrange("p a (h e) -> p (a h) e", h=2))
                o.tt(Xd8, V(PX, PXb.ap.rearrange("p (g e) -> p g e", e=64)),
                     V(dt_tm, dt4.ap.unsqueeze(2).broadcast_to([128, 8, 64])), ALU.mult)
                o.tt(Xdd8, Xd8, V(dte4, dte4.ap.unsqueeze(2).broadcast_to([128, 8, 64])), ALU.mult, eng="pool")
                PBt = self.bank()
                PBb = V(PBt, PBt.ap[:, 0:256].bitcast(BF16))
                for s_ in range(4):
                    o.transpose(V(PBt, PBb.ap[:, s_ * 128:(s_ + 1) * 128]), BT[:, s_ * 128:(s_ + 1) * 128], self.ident_b[:])
                o.copy(V(Btm4, Btm4.ap.rearrange("p a b -> p (a b)")), PBb)
                PSC = self.bank()
                for s_ in range(4):
                    cs = slice(s_ * 128, (s_ + 1) * 128)
                    o.mm(PSC[:, cs], [(BT[:, cs], CT[:, cs])])
                for h in range(2):
                    ah = V(a_tm, a_tm.ap[:, :, h:h + 1].broadcast_to([128, 4, 128]))
                    o.tt(lseg4[h][:], self.triS4[:], ah, ALU.mult, eng="pool")
                    o.tt(abc4[h][:], self.ones4[:], ah, ALU.mult, eng="pool")
                    PSEG = self.bank()
                    PAB = self.bank()
                    for s_ in range(4):
                        cs = slice(s_ * 128, (s_ + 1) * 128)
                        o.mm(PSEG[:, cs], [(lseg4[h][:, s_, :], self.triU[:]), (self.ident_f[:], self.maskb[:])])
                        o.mm(PAB[:, cs], [(abc4[h][:, s_, :], self.triU[:])])
                    o.act(LT4[h][:], PSEG[:], AF.Exp)
                    o.tt(WT4[h][:], PSC[:], LT4[h][:], ALU.mult)
                    o.act(Ecs4[h][:], PAB[:], AF.Exp)
                    o.tt(CsT4[h][:], CT[:], Ecs4[h][:], ALU.mult)
                for s_ in range(4):
                    cs = slice(s_ * 128, (s_ + 1) * 128)
                    for h in range(2):
                        hc = slice(h * 64, (h + 1) * 64)
                        o.mm(acc[hc, cs], [(Xd4[:, s_, hc], WT4[h][:, cs]), (STb[h][:], CsT4[h][:, cs])])
                        pst = self.bank()
                        o.mm(pst[:, 0:64], [(Btm4[:, s_, :], Xdd4[:, s_, hc])])
                        o.stt(ST32[h][:], ST32[h][:], eal4[:, s_ * 2 + h:s_ * 2 + h + 1], pst[:, 0:64], ALU.mult, ALU.add)
                        o.copy(STb[h][:], ST32[h][:], eng="pool")
                o.stt(T[2][:], xs32[:], cols[:, K_D:K_D + 1], acc[:], ALU.mult, ALU.add)
                o.tt(T[3][:], T[2][:], zs[:], ALU.mult)
                self.group_rms(T[3][:], 128, cols[:, K_SN:K_SN + 1], Yt[:, 2, :], None)
                o.dma(V(Yb_, Yr[:, hf, 2, tsl]), Yt[:, 2, :], "yo")

    def mix_b(self, cols_ap, scr, y_gather=None, prefetch=None):
        o = self.o
        A = self.alloc
        NT = SEQ // TT
        self.S.barrier()
        self.poff = self.mixb_base
        if prefetch is not None:
            self.prefetch_ffn(*prefetch)
        self.ring = [2, 3, 4, 5, 6, 7]
        oacc = [self.psum[0], self.psum[1]]
        cols = A([128, NCOLT], F32, "cols_b")
        o.dma(cols[:], V(self.S.dram(cols_ap, "colsd2"), cols_ap), "cols")
        KTh = [A([67, SEQ], BF16, "KTh%d" % h) for h in range(2)]
        for h in range(2):
            o.memset(KTh[h][64:67, :], 1.0)
            for q4 in range(4):
                qs = slice(q4 * (SEQ // 4), (q4 + 1) * (SEQ // 4))
                o.dma(KTh[h][0:64, qs], V(scr["K"], scr["K"].ap[h * 64:(h + 1) * 64, qs]), "kth%d" % h)
        Qh = [[A([67, TT], BF16, "Qh%d_%d" % (i, h)) for h in range(2)] for i in range(2)]
        FPr = scr["FP"].ap.rearrange("(g x) s -> x g s", x=32)
        PT = [A([128, TT], BF16, "PT%d" % i) for i in range(4)]
        Oa = A([65, TT], F32, "Oa")
        Osq = A([65, TT], BF16, "Osq")
        rs = A([64, TT], F32, "rs_b")
        Yb = [A([64, TT], BF16, "Yb%d" % h) for h in range(4)]
        Yrows = [(b_, v_.rearrange("(hf c g p) s -> p hf c g s", hf=2, c=4, g=2)) for b_, v_ in scr["Y"]]
        LAG = 2
        npt = 0
        nyb = 0

        def load_qf(qt):
            ts2 = slice(qt * TT, (qt + 1) * TT)
            for h in range(2):
                o.dma(Qh[qt % 2][h][0:64, :], V(scr["Q"], scr["Q"].ap[h * 64:(h + 1) * 64, ts2]), "qi%d_%d" % (qt % 2, h))
                o.dma(Qh[qt % 2][h][64:67, :], V(scr["FP"], FPr[h, :, ts2]), "qi%d_%d" % (qt % 2, h))
        load_qf(0)
        for qt in range(NT):
            hf, tl = qt // (NT // 2), qt % (NT // 2)
            tsl = slice((tl * TT) % CH, (tl * TT) % CH + TT)
            Ybuf, Yrow = Yrows[(tl * TT) // CH]
            if qt + 1 < NT:
                load_qf(qt + 1)
            Qc = Qh[qt % 2]
            nk = (qt + 1) * 4
            blocks = [(h, kt) for h in range(2) for kt in range(nk)]
            pend = []

            def stage2(item):
                nonlocal nyb
                h, kt, col0, pt = item
                oa = oacc[h % 2]
                o.mm1(oa[0:65, col0:TT], self.VA[:, kt, h, 0:65], pt[:, col0:TT], kt == 0, kt == nk - 1)
                if kt == nk - 1:
                    o.copy(Oa[:], oa[0:65, :], eng="act")
                    o.act(Osq[:], Oa[:], AF.Square)
                    pd = self.bank()
                    o.mm(pd[0:64, :], [(self.wden[:], Osq[:])])
                    o.act(rs[:], pd[0:64, :], AF.Sqrt)
                    o.recip(rs[:], rs[:])
                    yb = Yb[nyb % 4]
                    nyb += 1
                    o.stt(yb[:], Oa[0:64, :], cols[0:64, K_FON + h:K_FON + h + 1], rs[:], ALU.mult, ALU.mult)
                    o.dma(V(Ybuf, Yrow[:, hf, 1, h, tsl]), yb[:], "ybo%d" % (nyb % 4))

            for (h, kt) in blocks:
                R = slice(h * 64, (h + 1) * 64)
                c = kt - qt * 4
                col0 = max(c, 0) * 128
                sT = self.bank()
                o.mm(sT[:, col0:TT], [(KTh[h][:, kt * 128:(kt + 1) * 128], Qc[h][:, col0:TT])])
                if c >= 0:
                    o.tt(sT[:, col0:col0 + 128], sT[:, col0:col0 + 128], self.maskb[:], ALU.add)
                pt = PT[npt % 4]
                npt += 1
                o.act(pt[:, col0:TT], sT[:, col0:TT], AF.Exp, bias=self.EB[:, kt, h:h + 1])
                pend.append((h, kt, col0, pt))
                if len(pend) > LAG:
                    stage2(pend.pop(0))
            while pend:
                stage2(pend.pop(0))
            if y_gather is not None and hf == 1 and (tl * TT) % CH + TT == CH:
                k_ = (tl * TT) // CH
                self.all_gather(scr["Y"][k_][0], y_gather[k_][0])

    def outproj_phase(self, x_in, x_out, yfull, cols_ap, tm_ap, w_out_ap):
        o = self.o
        A = self.alloc
        self.phase_begin()
        self.ring = list(range(8))
        wo = A([128, NKC, D_MODEL], BF16, "wo")
        self.load_weight(wo, w_out_ap, NKC, D_MODEL, "wA")
        self.load_tables(cols_ap, tm_ap)
        cols = self.cols
        xts = [A([128, NKC, TT], F32, "xt_o%d" % i) for i in range(2)]
        Y0s = [A([128, NKC, TT], BF16, "Y0_%d" % i) for i in range(2)]
        Y1s = [A([128, NKC, TT], BF16, "Y1_%d" % i) for i in range(2)]
        Ys = A([128, NKC, TT], BF16, "Ys")
        xin_r = x_in.ap.rearrange("(c p) s -> p c s", p=128)
        xout_r = x_out.ap.rearrange("(c p) s -> p c s", p=128)
        yrs = [(b_, v_.rearrange("(r hf c p) s -> p hf r c s", r=2, hf=2, p=128)) for b_, v_ in yfull]

        def loads(t):
            i = t % 2
            o.dma(xts[i][:], V(x_in, xin_r[:, :, t * TT:(t + 1) * TT]), "xt%d" % i)
            yfb, yr = yrs[(t * TT) // CH]
            tsc = slice((t * TT) % CH, (t * TT) % CH + TT)
            for rr in range(2):
                o.dma(Y0s[i][:, rr * 4:(rr + 1) * 4, :], V(yfb, yr[:, 0, rr, :, tsc]), "y0%d" % i)
                o.dma(Y1s[i][:, rr * 4:(rr + 1) * 4, :], V(yfb, yr[:, 1, rr, :, tsc]), "y1%d" % i)
        loads(0)
        NTL = HALF // TT
        for t in range(NTL):
            ts_ = slice(t * TT, (t + 1) * TT)
            if t + 1 < NTL:
                loads(t + 1)
            xt, Y0, Y1 = xts[t % 2], Y0s[t % 2], Y1s[t % 2]
            o.ts(Ys[:], Y0[:], cols[:, K_SEL:K_SEL + 1], None, ALU.mult)
            o.stt(Ys[:], Y1[:], cols[:, K_SEL + 1:K_SEL + 2], Ys[:], ALU.mult, ALU.add)
            for oc in range(NKC):
                py = self.bank()
                o.mm(py[:], [(wo[:, k, oc * 128:(oc + 1) * 128], Ys[:, k, :]) for k in range(NKC)])
                o.tt(xt[:, oc, :], xt[:, oc, :], py[:], ALU.add)
            o.dma(V(x_out, xout_r[:, :, ts_]), xt[:], "xo%d" % (t % 2))


def build_program(depth, debug=None):
    nc = bass.Bass("TRN2", target_bir_lowering=False)
    stack = ExitStack()

    def din(name, shape, dt=F32):
        return nc.dram_tensor(name, list(shape), dt, kind="ExternalInput").ap()
    xT = din("xT", [D_MODEL, HALF])
    w = {}
    for l in range(depth):
        for f in (1, 2):
            w["g%d_%d" % (f, l)] = din("ffn%d_w_gate_%d" % (f, l), [D_MODEL, D_FF])
            w["u%d_%d" % (f, l)] = din("ffn%d_w_up_%d" % (f, l), [D_MODEL, D_FF])
            w["d%d_%d" % (f, l)] = din("ffn%d_w_down_%d" % (f, l), [D_FF, D_MODEL])
        w["win1_%d" % l] = din("w_in1_%d" % l, [D_MODEL, P1_N])
        w["win2_%d" % l] = din("w_in2_%d" % l, [D_MODEL, P2_N])
        w["wout_%d" % l] = din("w_out_%d" % l, [D_MODEL, D_MODEL])
        w["wup_%d" % l] = din("gla_wup_%d" % l, [16, 128])
        w["cols_%d" % l] = din("cols_%d" % l, [128, NCOLT])
        w["tm_%d" % l] = din("tm_%d" % l, [128, NTM])
    outT = nc.dram_tensor("outT", [D_MODEL, HALF], F32, kind="ExternalOutput").ap()
    with stack:
        B = Builder(nc, stack, HALF)
        S = B.S

        def scratch(name, shape, dt):
            return S.dram(nc.dram_tensor(name, list(shape), dt).ap(), name)
        xa = scratch("xa", [D_MODEL, HALF], F32)
        xb = scratch("xb", [D_MODEL, HALF], F32)
        NCH = HALF // CH

        def cc_pair(name):
            srcs, dsts = [], []
            for k in range(NCH):
                sb_ = scratch("%s_s%d" % (name, k), [128, 8 * CH], BF16)
                db_ = scratch("%s_d%d" % (name, k), [256, 8 * CH], BF16)
                srcs.append((sb_, sb_.ap.rearrange("p (a s) -> (p a) s", s=CH)))
                dsts.append((db_, db_.ap.rearrange("p (a s) -> (p a) s", s=CH)))
            return srcs, dsts
        hsrc, hfull = cc_pair("h")
        ysrc, yfull = cc_pair("y")
        scr = {"Y": ysrc,
               "Q": scratch("scrQ", [128, SEQ], BF16),
               "K": scratch("scrK", [128, SEQ], BF16),
               "FP": scratch("scrFP", [96, SEQ], BF16)}
        cur = S.dram(xT, "xT")
        xout = S.dram(outT, "outT")
        for l in range(depth):
            last = l == depth - 1
            ct = (w["cols_%d" % l], w["tm_%d" % l])
            B.ffn_phase(cur, xa, ct[0], ct[1], K_FFN1, w["g1_%d" % l], w["u1_%d" % l], w["d1_%d" % l], h_out=hsrc, h_gather=hfull)
            B.mix_a(1, hfull, ct[0], ct[1], w["win1_%d" % l], w["wup_%d" % l], scr)
            B.mix_a(2, hfull, ct[0], ct[1], w["win2_%d" % l], w["wup_%d" % l], scr)
            B.mix_b(ct[0], scr, y_gather=yfull, prefetch=(w["g2_%d" % l], w["u2_%d" % l]))
            B.outproj_phase(xa, xb, yfull, ct[0], ct[1], w["wout_%d" % l])
            dst = xout if last else xa
            B.ffn_phase(xb, dst, ct[0], ct[1], K_FFN2, w["g2_%d" % l], w["u2_%d" % l], w["d2_%d" % l])
            cur = dst
        S.barrier()
        S.emit()
        build_program.nsem = S.nsem
    return nc


def make_inputs(inp, depth, r):
    d = {}
    for l in range(depth):
        ffn = {1: (inp["ffn1_w_gate"], inp["ffn1_w_up"], inp["ffn1_w_down"]),
               2: (inp["ffn2_w_gate"], inp["ffn2_w_up"], inp["ffn2_w_down"])}
        for f in (1, 2):
            d["ffn%d_w_gate_%d" % (f, l)] = np.ascontiguousarray(ffn[f][0][l], dtype=np.float32)
            d["ffn%d_w_up_%d" % (f, l)] = np.ascontiguousarray(ffn[f][1][l], dtype=np.float32)
            d["ffn%d_w_down_%d" % (f, l)] = np.ascontiguousarray(ffn[f][2][l], dtype=np.float32)
        p1, p2 = host_win(inp, l, r)
        d["w_in1_%d" % l] = p1
        d["w_in2_%d" % l] = p2
        d["w_out_%d" % l] = host_wout(inp, l)
        d["gla_wup_%d" % l] = np.ascontiguousarray(np.asarray(inp["gla_w_gate_up"][l], np.float32)[:, r * 128:(r + 1) * 128])
        cols, tm = host_tables(inp, l, r)
        d["cols_%d" % l] = cols
        d["tm_%d" % l] = tm
    return d


_PROG_CACHE = {}


def kernel(**inputs):
    x = np.asarray(inputs["x"], np.float32)
    Bsz, L, D = x.shape
    depth = inputs["w_in"].shape[0]
    assert (Bsz, L, D) == (4, SEQ, D_MODEL)
    if depth not in _PROG_CACHE:
        _PROG_CACHE[depth] = build_program(depth)
    nc = _PROG_CACHE[depth]
    shared = [make_inputs(inputs, depth, r) for r in range(2)]
    in_maps = []
    for c in range(8):
        b, r = c // 2, c % 2
        m = dict(shared[r])
        m["xT"] = np.ascontiguousarray(x[b, r * HALF:(r + 1) * HALF].T)
        in_maps.append(m)
    res = run_bass_kernel_spmd(nc, in_maps, core_ids=list(range(8)))
    out = np.empty((Bsz, L, D), np.float32)
    for c in range(8):
        b, r = c // 2, c % 2
        out[b, r * HALF:(r + 1) * HALF] = res.results[c]["outT"].T
    return out
```
